# Optimizing a Trainium2 kernel written in Bass

```python
import jax, jax.numpy as jnp
from jax import lax
import numpy as np

D_MODEL = 4096
BATCH = 1
SEQ = 8192
DEPTH = 2

HEAD_DIM = 128
MIX_WIDTH = D_MODEL
N_MIXERS = 4
HEADS_PER_MIXER = MIX_WIDTH // (N_MIXERS * HEAD_DIM)
GROUP_WIDTH = HEADS_PER_MIXER * HEAD_DIM
Q_LORA_RANK = 896
KV_LORA_RANK = 512
MLA_NOPE_DIM = 128
MLA_ROPE_DIM = 64
ROPE_THETA = 500000.0
PARTIAL_ROPE_DIM = HEAD_DIM // 4
DILATED_PAIRS = ((128, 1), (512, 4), (2048, 16))
Q_BLOCK = 128
DIL_ALIGN = max(d for _, d in DILATED_PAIRS) * Q_BLOCK
D_FF = 4 * D_MODEL
EPS = 1e-6
NEG_INF = -1e30

IN_SIZES = (Q_LORA_RANK, KV_LORA_RANK, MLA_ROPE_DIM,
            3 * GROUP_WIDTH, 3 * GROUP_WIDTH, 3 * GROUP_WIDTH, HEADS_PER_MIXER)
IN_COLS = sum(IN_SIZES)
SPLIT_POINTS = tuple(int(v) for v in np.cumsum(IN_SIZES)[:-1])

kernel_name = "hymba_mla_dilated_stickbreak_fox_trunk"


def rmsnorm(x, g):
    x32 = x.astype(jnp.float32)
    y = x32 * lax.rsqrt(jnp.mean(x32 * x32, axis=-1, keepdims=True) + EPS)
    return (y * g.astype(jnp.float32)).astype(x.dtype)


def rope(t, pos):
    r = t.shape[-1]
    half = r // 2
    inv = ROPE_THETA ** (-jnp.arange(half, dtype=jnp.float32) * (2.0 / r))
    ang = pos.astype(jnp.float32)[:, None] * inv[None, :]
    cos = jnp.cos(ang)[:, None, :]
    sin = jnp.sin(ang)[:, None, :]
    t32 = t.astype(jnp.float32)
    t1, t2 = t32[..., :half], t32[..., half:]
    return jnp.concatenate([t1 * cos - t2 * sin, t1 * sin + t2 * cos], axis=-1).astype(t.dtype)


def partial_rope(t, pos):
    return jnp.concatenate([rope(t[..., :PARTIAL_ROPE_DIM], pos), t[..., PARTIAL_ROPE_DIM:]], axis=-1)


def to_blocks(t):
    b, s, h, d = t.shape
    return t.reshape(b, s // Q_BLOCK, Q_BLOCK, h, d).transpose(1, 0, 2, 3, 4)


def from_blocks(t):
    nb, b, q, h, d = t.shape
    return t.transpose(1, 0, 2, 3, 4).reshape(b, nb * q, h, d)


def mla_attention(q_nope, q_rope, k_nope, k_rope, v):
    s_len = k_nope.shape[1]
    kpos = jnp.arange(s_len)
    scale = (MLA_NOPE_DIM + MLA_ROPE_DIM) ** -0.5

    def blk(args):
        qn, qr, i = args
        s = (jnp.einsum('bqhd,bkhd->bhqk', qn, k_nope)
             + jnp.einsum('bqhd,bkd->bhqk', qr, k_rope)).astype(jnp.float32) * scale
        qpos = i * Q_BLOCK + jnp.arange(Q_BLOCK)
        s = jnp.where(kpos[None, :] <= qpos[:, None], s, NEG_INF)
        p = jax.nn.softmax(s, axis=-1)
        return jnp.einsum('bhqk,bkhd->bqhd', p, v)

    nb = s_len // Q_BLOCK
    out = lax.map(blk, (to_blocks(q_nope), to_blocks(q_rope), jnp.arange(nb)))
    return from_blocks(out)


def dilated_branch(q, k, v, window, dil):
    b, sp, h, d = q.shape
    sub_len = sp // dil
    nb = sub_len // Q_BLOCK
    band = window // dil
    scale = HEAD_DIM ** -0.5

    def to_sub(t):
        return t.reshape(b, sub_len, dil, h, d).transpose(0, 2, 1, 3, 4).reshape(b * dil, nb, Q_BLOCK, h, d)

    def with_prev(t):
        prev = jnp.pad(t[:, :-1], ((0, 0), (1, 0), (0, 0), (0, 0), (0, 0)))
        return jnp.concatenate([prev, t], axis=2)

    qs = to_sub(q)
    kb = with_prev(to_sub(k))
    vb = with_prev(to_sub(v))
    s = jnp.einsum('bnqhd,bnkhd->bnhqk', qs, kb).astype(jnp.float32) * scale
    i = jnp.arange(Q_BLOCK)[:, None]
    j = jnp.arange(2 * Q_BLOCK)[None, :]
    n = jnp.arange(nb)[:, None, None]
    dist = Q_BLOCK + i - j
    valid = (dist >= 0) & (dist <= band) & ((n > 0) | (j >= Q_BLOCK))
    s = jnp.where(valid[None, :, None], s, NEG_INF)
    m = jnp.max(s, axis=-1, keepdims=True)
    p = jnp.exp(s - m)
    l = jnp.sum(p, axis=-1, keepdims=True)
    o = jnp.einsum('bnhqk,bnkhd->bnhqd', p, vb) / l
    lse = (m + jnp.log(l))[..., 0]
    o = o.transpose(0, 1, 3, 2, 4).reshape(b, dil, sub_len, h, d).transpose(0, 2, 1, 3, 4).reshape(b, sp, h, d)
    lse = lse.transpose(0, 1, 3, 2).reshape(b, dil, sub_len, h).transpose(0, 2, 1, 3).reshape(b, sp, h)
    return o, lse


def dilated_attention(q, k, v):
    s_len = q.shape[1]
    sp = -(-s_len // DIL_ALIGN) * DIL_ALIGN
    padw = ((0, 0), (0, sp - s_len), (0, 0), (0, 0))
    qp, kp, vp = jnp.pad(q, padw), jnp.pad(k, padw), jnp.pad(v, padw)
    outs, lses = [], []
    for window, dil in DILATED_PAIRS:
        o, lse = dilated_branch(qp, kp, vp, window, dil)
        outs.append(o)
        lses.append(lse)
    w = jax.nn.softmax(jnp.stack(lses, axis=0), axis=0)
    o = jnp.sum(w[..., None] * jnp.stack(outs, axis=0), axis=0)
    return o[:, :s_len]


def stick_breaking_attention(q, k, v):
    s_len = k.shape[1]
    kpos = jnp.arange(s_len)
    scale = HEAD_DIM ** -0.5

    def blk(args):
        qb, i = args
        z = jnp.einsum('bqhd,bkhd->bhqk', qb, k).astype(jnp.float32) * scale
        qpos = i * Q_BLOCK + jnp.arange(Q_BLOCK)
        mask = kpos[None, :] < qpos[:, None]
        log_1m = jnp.where(mask, jax.nn.log_sigmoid(-z), 0.0)
        after = lax.cumsum(log_1m, axis=3, reverse=True) - log_1m
        a = jnp.where(mask, jnp.exp(jax.nn.log_sigmoid(z) + after), 0.0)
        return jnp.einsum('bhqk,bkhd->bqhd', a, v)

    nb = s_len // Q_BLOCK
    return from_blocks(lax.map(blk, (to_blocks(q), jnp.arange(nb))))


def forgetting_attention(q, k, v, logf_cum):
    b, s_len, h, _ = k.shape
    kpos = jnp.arange(s_len)
    scale = HEAD_DIM ** -0.5
    nb = s_len // Q_BLOCK
    ck = jnp.transpose(logf_cum, (0, 2, 1))[:, :, None, :]
    cq = logf_cum.reshape(b, nb, Q_BLOCK, h).transpose(1, 0, 3, 2)

    def blk(args):
        qb, cqb, i = args
        s = jnp.einsum('bqhd,bkhd->bhqk', qb, k).astype(jnp.float32) * scale + (cqb[..., None] - ck)
        qpos = i * Q_BLOCK + jnp.arange(Q_BLOCK)
        s = jnp.where(kpos[None, :] <= qpos[:, None], s, NEG_INF)
        p = jax.nn.softmax(s, axis=-1)
        return jnp.einsum('bhqk,bkhd->bqhd', p, v)

    return from_blocks(lax.map(blk, (to_blocks(q), cq, jnp.arange(nb))))


def setup_inputs(seed: int = 0) -> dict:
    key = jax.random.key(seed)
    ks = jax.random.split(key, 16)
    f32 = jnp.float32

    def nrm(k, shape, fan_in):
        return jax.random.normal(k, shape, f32) * (fan_in ** -0.5)

    def gain(k, shape):
        return 1.0 + 0.05 * jax.random.normal(k, shape, f32)

    return {
        "x": jax.random.normal(ks[0], (BATCH, SEQ, D_MODEL), f32),
        "g_attn": gain(ks[1], (DEPTH, D_MODEL)),
        "w_in": nrm(ks[2], (DEPTH, D_MODEL, IN_COLS), D_MODEL),
        "g_q": gain(ks[3], (DEPTH, Q_LORA_RANK)),
        "g_kv": gain(ks[4], (DEPTH, KV_LORA_RANK)),
        "w_uq": nrm(ks[5], (DEPTH, Q_LORA_RANK, HEADS_PER_MIXER * (MLA_NOPE_DIM + MLA_ROPE_DIM)), Q_LORA_RANK),
        "w_uk": nrm(ks[6], (DEPTH, KV_LORA_RANK, HEADS_PER_MIXER * MLA_NOPE_DIM), KV_LORA_RANK),
        "w_uv": nrm(ks[7], (DEPTH, KV_LORA_RANK, GROUP_WIDTH), KV_LORA_RANK),
        "b_f": jax.random.uniform(ks[8], (DEPTH, HEADS_PER_MIXER), f32, minval=1.0, maxval=5.0),
        "g_out": gain(ks[9], (DEPTH, MIX_WIDTH)),
        "w_o": nrm(ks[10], (DEPTH, MIX_WIDTH, D_MODEL), MIX_WIDTH),
        "g_mlp": gain(ks[11], (DEPTH, D_MODEL)),
        "w_up": nrm(ks[12], (DEPTH, D_MODEL, D_FF), D_MODEL),
        "w_down": nrm(ks[13], (DEPTH, D_FF, D_MODEL), D_FF),
        "g_final": gain(ks[14], (D_MODEL,)),
    }


def reference(x, g_attn, w_in, g_q, g_kv, w_uq, w_uk, w_uv, b_f, g_out, w_o, g_mlp, w_up, w_down, g_final):
    b, s_len, _ = x.shape
    pos = jnp.arange(s_len)

    def heads(t, d=HEAD_DIM):
        return t.reshape(b, s_len, -1, d)

    for l in range(DEPTH):
        h = rmsnorm(x, g_attn[l])
        proj = h @ w_in[l]
        p_cq, p_ckv, p_kr, p_b, p_c, p_d, p_f = jnp.split(proj, SPLIT_POINTS, axis=-1)

        qa = heads(rmsnorm(p_cq, g_q[l]) @ w_uq[l], MLA_NOPE_DIM + MLA_ROPE_DIM)
        q_nope = qa[..., :MLA_NOPE_DIM]
        q_rope = rope(qa[..., MLA_NOPE_DIM:], pos)
        c_kv = rmsnorm(p_ckv, g_kv[l])
        k_nope = heads(c_kv @ w_uk[l], MLA_NOPE_DIM)
        v_a = heads(c_kv @ w_uv[l])
        k_rope = rope(p_kr[:, :, None, :], pos)[:, :, 0, :]
        o_a = mla_attention(q_nope, q_rope, k_nope, k_rope, v_a)

        qb, kb, vb = jnp.split(p_b, 3, axis=-1)
        o_b = dilated_attention(partial_rope(heads(qb), pos), partial_rope(heads(kb), pos), heads(vb))

        qc, kc, vc = jnp.split(p_c, 3, axis=-1)
        o_c = stick_breaking_attention(heads(qc), heads(kc), heads(vc))

        qd, kd, vd = jnp.split(p_d, 3, axis=-1)
        logf = jax.nn.log_sigmoid(p_f.astype(jnp.float32) + b_f[l].astype(jnp.float32))
        o_d = forgetting_attention(heads(qd), heads(kd), heads(vd), jnp.cumsum(logf, axis=1))

        groups = []
        for gi, o in enumerate((o_a, o_b, o_c, o_d)):
            o = o.reshape(b, s_len, GROUP_WIDTH).astype(x.dtype)
            groups.append(rmsnorm(o, g_out[l, gi * GROUP_WIDTH:(gi + 1) * GROUP_WIDTH]))
        x = x + jnp.concatenate(groups, axis=-1) @ w_o[l]

        h = rmsnorm(x, g_mlp[l])
        x = x + jnp.square(jax.nn.relu(h @ w_up[l])) @ w_down[l]

    return rmsnorm(x, g_final)
```

```python
import numpy as np
from contextlib import ExitStack
import concourse.bass as bass
import concourse.mybir as mybir
from concourse.bass_utils import run_bass_kernel_spmd

F32 = mybir.dt.float32
BF16 = mybir.dt.bfloat16
AF = mybir.ActivationFunctionType
ALU = mybir.AluOpType
ENGS = ['sync', 'scalar', 'vector', 'gpsimd', 'tensor']
EPS = 1e-6


class DSem:
    def __init__(self, h):
        self.h = h
        self.count = 0


class Prog:
    def __init__(self, nc, es):
        self.nc = nc
        self.es = es
        self.ops = {e: [] for e in ENGS}
        self.psem = {}
        self.pcnt = {e: 0 for e in ENGS}
        self.waited = {}
        self.nsem = 0
        for e in ['scalar', 'vector', 'gpsimd', 'tensor']:
            self.psem[e] = es.enter_context(nc.semaphore(f"p_{e}"))

    def dsem(self, name):
        self.nsem += 1
        return DSem(self.es.enter_context(self.nc.semaphore(name)))

    def sb(self, name, shape, dt):
        return self.es.enter_context(self.nc.sbuf_tensor(name, shape, dt))

    def ps(self, name, shape, dt=F32):
        return self.es.enter_context(self.nc.psum_tensor(name, shape, dt))

    def op(self, eng, fn, signal=True):
        if signal:
            self.pcnt[eng] += 1
            s = self.psem[eng]
            self.ops[eng].append(lambda e, fn=fn, s=s: fn(e).then_inc(s, 1))
            return (s, self.pcnt[eng])
        self.ops[eng].append(fn)
        return None

    def wait(self, eng, *toks):
        for tok in toks:
            if tok is None:
                continue
            if isinstance(tok, list):
                self.wait(eng, *tok)
                continue
            s, v = tok
            if eng in self.psem and s is self.psem[eng]:
                pass
            key = (eng, id(s))
            if self.waited.get(key, 0) >= v:
                continue
            self.waited[key] = v
            self.ops[eng].append(lambda e, s=s, v=v: e.wait_ge(s, v))

    def dma(self, queue, out, in_, ds):
        ds.count += 16
        self.ops[queue].append(lambda e, out=out, in_=in_, h=ds.h: e.dma_start(out=out, in_=in_).then_inc(h, 16))
        return (ds.h, ds.count)

    def build(self):
        with self.nc.Block() as block:
            for name in ENGS:
                ops = self.ops[name]
                if not ops:
                    continue

                def body(e, ops=ops):
                    for f in ops:
                        f(e)
                getattr(block, name)(body)


class Banks:
    def __init__(self, P, banks):
        self.P = P
        self.banks = banks
        self.free = [None] * len(banks)
        self.n = 0

    def acquire(self):
        i = self.n % len(self.banks)
        self.n += 1
        self.P.wait('tensor', self.free[i])
        return i

    def release(self, i, *toks):
        self.free[i] = list(toks)


class TileRing:
    def __init__(self, tiles):
        self.tiles = tiles
        self.free = [None] * len(tiles)
        self.n = 0

    def next(self):
        i = self.n % len(self.tiles)
        self.n += 1
        return i


class SlabRing:
    def __init__(self, P, tiles, queue='gpsimd'):
        self.P = P
        self.tiles = tiles
        self.sems = [P.dsem(f"slab_sem{i}") for i in range(len(tiles))]
        self.free = [None] * len(tiles)
        self.n = 0
        self.queue = queue

    def load(self, W, r0, kc, c0, cw, kgroup=8):
        P = self.P
        slot = self.n % len(self.tiles)
        self.n += 1
        P.wait(self.queue, self.free[slot])
        t = self.tiles[slot]
        src = W[r0:r0 + kc * 128, c0:c0 + cw].rearrange("(k p) c -> p k c", p=128)
        tok = None
        for k0 in range(0, kc, kgroup):
            k1 = min(kc, k0 + kgroup)
            dst = t[:, k0 * cw:k1 * cw].rearrange("p (k c) -> p k c", c=cw)
            tok = P.dma(self.queue, dst, src[:, k0:k1, :], self.sems[slot])
        return slot, tok

    def release(self, slot, tok):
        self.free[slot] = tok


T = 1024
D = 4096
FF = 16384


class LoadRing:
    def __init__(self, P, tiles, name, queue='sync'):
        self.P = P
        self.tiles = [t if type(t).__name__ == 'AP' else t[:, :] for t in tiles]
        self.sems = [P.dsem(f"{name}_s{i}") for i in range(len(tiles))]
        self.free = [None] * len(tiles)
        self.n = 0
        self.queue = queue

    def load(self, src):
        P = self.P
        slot = self.n % len(self.tiles)
        self.n += 1
        P.wait(self.queue, self.free[slot])
        tok = P.dma(self.queue, self.tiles[slot], src, self.sems[slot])
        return slot, tok

    def release(self, slot, *toks):
        self.free[slot] = list(toks)


class StoreRing:
    def __init__(self, P, tiles, name, queue='sync'):
        self.P = P
        self.tiles = [t if type(t).__name__ == 'AP' else t[:, :] for t in tiles]
        self.sems = [P.dsem(f"{name}_s{i}") for i in range(len(tiles))]
        self.free = [None] * len(tiles)
        self.n = 0
        self.queue = queue

    def acquire(self, eng):
        slot = self.n % len(self.tiles)
        self.n += 1
        self.P.wait(eng, self.free[slot])
        return slot

    def store(self, slot, dst, after, extra=(), src=None):
        P = self.P
        P.wait(self.queue, after)
        tok = P.dma(self.queue, dst, self.tiles[slot] if src is None else src, self.sems[slot])
        self.free[slot] = [tok] + list(extra)
        return tok

    def final_toks(self):
        return [(s.h, s.count) for s in self.sems if s.count > 0]


def build_C(final):
    nc = bass.Bass("TRN2", target_bir_lowering=False)
    oT = nc.dram_tensor("oT", [D, T], F32, kind="ExternalInput").ap()
    xT = nc.dram_tensor("xT", [D, T], F32, kind="ExternalInput").ap()
    goutd = nc.dram_tensor("gout", [128, 32], F32, kind="ExternalInput").ap()
    gmlpd = nc.dram_tensor("gmlp", [128, 32], F32, kind="ExternalInput").ap()
    gfind = nc.dram_tensor("gfin", [128, 32], F32, kind="ExternalInput").ap()
    w_o = nc.dram_tensor("w_o", [D, D], F32, kind="ExternalInput").ap()
    w_up = nc.dram_tensor("w_up", [D, FF], F32, kind="ExternalInput").ap()
    w_down = nc.dram_tensor("w_down", [FF, D], F32, kind="ExternalInput").ap()
    onesd = nc.dram_tensor("ones", [128, 128], F32, kind="ExternalInput").ap()
    yT = nc.dram_tensor("yT", [D, T], F32, kind="ExternalOutput").ap()
    x1d = nc.dram_tensor("x1d", [D, T], F32).ap()
    partd = nc.dram_tensor("partd", [8, D, T], F32).ap()

    with ExitStack() as es:
        P = Prog(nc, es)
        actT = P.sb("actT", [128, 32 * T], BF16)
        ubuf = P.sb("ubuf", [128, 16 * T], BF16)
        slab_t = [P.sb(f"slab{i}", [128, 16384], BF16) for i in range(2)]
        tp = [P.sb(f"tp{i}", [128, T], F32) for i in range(6)]
        stg_t = [P.sb(f"stg{i}", [128, 512], F32) for i in range(4)]
        rl_t = [P.sb(f"rl{i}", [128, 512], F32) for i in range(2)]
        rstd = P.sb("rstd", [128, T], F32)
        ones = P.sb("ones_sb", [128, 128], F32)
        gout = P.sb("gout_sb", [128, 32], F32)
        gmlp = P.sb("gmlp_sb", [128, 32], F32)
        gfin = P.sb("gfin_sb", [128, 32], F32)
        bank_t = [P.ps(f"bank{i}", [128, 512]) for i in range(8)]
        st = bank_t[0:2]
        banks = Banks(P, bank_t[2:8])
        slabs = SlabRing(P, slab_t)
        csem = P.dsem("csem")
        P.dma('sync', ones[:, :], onesd, csem)
        P.dma('sync', gout[:, :], goutd, csem)
        P.dma('sync', gmlp[:, :], gmlpd, csem)
        ctok = P.dma('sync', gfin[:, :], gfind, csem)
        for e in ['scalar', 'vector', 'tensor']:
            P.wait(e, ctok)

        def act(k, half=None):
            if half is None:
                return actT[:, k * T:(k + 1) * T]
            return actT[:, k * T + half * 512:k * T + half * 512 + 512]

        ofp = [ubuf[:, 2 * j * T:2 * (j + 1) * T].bitcast(F32) for j in range(8)]
        oring = LoadRing(P, ofp, "oring")
        xring = LoadRing(P, tp[0:3], "xring")
        sqr = TileRing(tp[3:5])
        stg = StoreRing(P, stg_t, "stg")

        slab_tasks = []
        for s in range(8):
            slab_tasks.append(('o', w_o, 0, 32, 512 * s, 512))
        for fb in range(8):
            for s in range(4):
                slab_tasks.append(('u', w_up, 0, 32, fb * 2048 + 512 * s, 512))
            for s in range(4):
                slab_tasks.append(('d', w_down, fb * 2048, 16, 1024 * s, 1024))
        slab_loaded = {}

        def prefetch(i):
            if i < len(slab_tasks) and i not in slab_loaded:
                _, W, r0, kc, c0, cw = slab_tasks[i]
                slab_loaded[i] = slabs.load(W, r0, kc, c0, cw)

        prefetch(0)

        def rstd_from_stats(n_feat, after_tok, war_toks):
            P.wait('scalar', after_tok, war_toks)
            ta = None
            for half in range(2):
                ta = P.op('scalar', lambda e, half=half: e.activation(
                    out=rstd[:, half * 512:half * 512 + 512], in_=st[half][:, :], func=AF.Sqrt,
                    scale=1.0 / n_feat, bias=EPS))
            P.wait('vector', ta)
            tv = P.op('vector', lambda e: e.reciprocal(out=rstd[:, :], in_=rstd[:, :]))
            return ta, tv

        last_norm_tok = None
        st_read_tok = None
        for gi in range(4):
            ltoks = []
            for j in range(8):
                k = 8 * gi + j
                slot, tok = oring.load(oT[k * 128:(k + 1) * 128, :])
                ltoks.append(tok)
            pe_tok = None
            for j in range(8):
                P.wait('scalar', ltoks[j])
                si = sqr.next()
                P.wait('scalar', sqr.free[si])
                ta = P.op('scalar', lambda e, j=j, si=si: e.activation(out=sqr.tiles[si][:, :], in_=ofp[j], func=AF.Square))
                P.wait('tensor', ta)
                if j == 0:
                    P.wait('tensor', st_read_tok)
                for half in range(2):
                    pe_tok = P.op('tensor', lambda e, j=j, si=si, half=half: e.matmul(
                        st[half][:, :], ones[:, :], sqr.tiles[si][:, half * 512:half * 512 + 512],
                        start=(j == 0), stop=(j == 7)), signal=(half == 1))
                sqr.free[si] = pe_tok
            st_read_tok, tv = rstd_from_stats(1024.0, pe_tok, last_norm_tok)
            P.wait('vector', tv)
            for j in range(8):
                k = 8 * gi + j
                P.wait('vector', ltoks[j])
                last_norm_tok = P.op('vector', lambda e, j=j, k=k: e.scalar_tensor_tensor(
                    out=act(k), in0=ofp[j], scalar=gout[:, k:k + 1], in1=rstd[:, :], op0=ALU.mult, op1=ALU.mult))
                oring.release(j, last_norm_tok)

        P.wait('tensor', last_norm_tok)
        pending = None
        x1tok = {}
        xload = {}

        def xprefetch(oc):
            if oc < 32 and oc not in xload:
                xload[oc] = xring.load(xT[oc * 128:(oc + 1) * 128, :])

        xprefetch(0)
        xprefetch(1)
        ti = 0
        stat_tok = None
        for s in range(8):
            prefetch(ti + 1)
            slot, stok = slab_loaded[ti]
            P.wait('tensor', stok)
            sl = slab_t[slot]
            pe_tok = None
            for ocl in range(4):
                oc = 4 * s + ocl
                xprefetch(oc + 2)
                xslot, xtok = xload[oc]
                dtoks = []
                for half in range(2):
                    b = banks.acquire()
                    for k in range(32):
                        pe_tok = P.op('tensor', lambda e, b=b, k=k, ocl=ocl, half=half, sl=sl: e.matmul(
                            banks.banks[b][:, :], sl[:, k * 512 + ocl * 128:k * 512 + ocl * 128 + 128], act(k, half),
                            start=(k == 0), stop=(k == 31)), signal=(k == 31))
                    if pending is not None:
                        pending()
                        pending = None
                    sslot = stg.acquire('vector')
                    P.wait('vector', pe_tok, xtok)
                    tv = P.op('vector', lambda e, b=b, sslot=sslot, xslot=xslot, half=half: e.tensor_tensor(
                        out=stg_t[sslot][:, :], in0=banks.banks[b][:, :],
                        in1=xring.tiles[xslot][:, half * 512:half * 512 + 512], op=ALU.add))
                    banks.release(b, tv)
                    dtoks.append(tv)
                    si = sqr.next()
                    P.wait('scalar', tv, sqr.free[si])
                    ta = P.op('scalar', lambda e, si=si, sslot=sslot: e.activation(
                        out=sqr.tiles[si][:, 0:512], in_=stg_t[sslot][:, :], func=AF.Square))
                    x1tok[(oc, half)] = stg.store(sslot, x1d[oc * 128:(oc + 1) * 128, half * 512:half * 512 + 512], tv, extra=[ta])

                    def mk(si=si, half=half, oc=oc, ta=ta):
                        def f():
                            nonlocal stat_tok
                            P.wait('tensor', ta)
                            stat_tok = P.op('tensor', lambda e: e.matmul(
                                st[half][:, :], ones[:, :], sqr.tiles[si][:, 0:512], start=(oc == 0), stop=(oc == 31)))
                            sqr.free[si] = stat_tok
                        return f
                    if oc == 0:
                        P.wait('tensor', st_read_tok)
                    pending = mk()
                xring.release(xslot, *dtoks)
            slabs.release(slot, pe_tok)
            ti += 1
        pending()
        pending = None
        st_read_tok, tv = rstd_from_stats(4096.0, stat_tok, last_norm_tok)

        P.wait('vector', tv, stat_tok)
        xload2 = {}

        def x1prefetch(k):
            if k < 32 and k not in xload2:
                P.wait('sync', x1tok[(k, 0)], x1tok[(k, 1)])
                xload2[k] = xring.load(x1d[k * 128:(k + 1) * 128, :])
        x1prefetch(0)
        x1prefetch(1)
        for k in range(32):
            x1prefetch(k + 2)
            xslot, xtok = xload2[k]
            P.wait('vector', xtok)
            last_norm_tok = P.op('vector', lambda e, k=k, xslot=xslot: e.scalar_tensor_tensor(
                out=act(k), in0=xring.tiles[xslot][:, :], scalar=gmlp[:, k:k + 1], in1=rstd[:, :], op0=ALU.mult, op1=ALU.mult))
            xring.release(xslot, last_norm_tok)

        P.wait('tensor', last_norm_tok)
        rlr = TileRing(rl_t)
        down_last_pe = None
        nev = 0
        for fb in range(8):
            u_last = None
            for s in range(4):
                prefetch(ti + 1)
                slot, stok = slab_loaded[ti]
                P.wait('tensor', stok)
                sl = slab_t[slot]
                pe_tok = None
                for fl in range(4):
                    ffc = 4 * s + fl
                    for half in range(2):
                        b = banks.acquire()
                        for k in range(32):
                            pe_tok = P.op('tensor', lambda e, b=b, k=k, fl=fl, half=half, sl=sl: e.matmul(
                                banks.banks[b][:, :], sl[:, k * 512 + fl * 128:k * 512 + fl * 128 + 128], act(k, half),
                                start=(k == 0), stop=(k == 31)), signal=(k == 31))
                        ri = rlr.next()
                        P.wait('scalar', pe_tok, rlr.free[ri])
                        ta = P.op('scalar', lambda e, b=b, ri=ri: e.activation(out=rl_t[ri][:, :], in_=banks.banks[b][:, :], func=AF.Relu))
                        banks.release(b, ta)
                        P.wait('vector', ta, down_last_pe)
                        u_last = P.op('vector', lambda e, ri=ri, ffc=ffc, half=half: e.tensor_tensor(
                            out=ubuf[:, ffc * T + half * 512:ffc * T + half * 512 + 512], in0=rl_t[ri][:, :], in1=rl_t[ri][:, :], op=ALU.mult))
                        rlr.free[ri] = u_last
                slabs.release(slot, pe_tok)
                ti += 1
            P.wait('tensor', u_last)
            for s in range(4):
                prefetch(ti + 1)
                slot, stok = slab_loaded[ti]
                P.wait('tensor', stok)
                sl = slab_t[slot]
                pe_tok = None
                for ol in range(8):
                    oc = 8 * s + ol
                    for half in range(2):
                        b = banks.acquire()
                        for k in range(16):
                            pe_tok = P.op('tensor', lambda e, b=b, k=k, ol=ol, half=half, sl=sl: e.matmul(
                                banks.banks[b][:, :], sl[:, k * 1024 + ol * 128:k * 1024 + ol * 128 + 128],
                                ubuf[:, k * T + half * 512:k * T + half * 512 + 512],
                                start=(k == 0), stop=(k == 15)), signal=(k == 15))
                        eng = 'scalar' if nev % 2 == 0 else 'vector'
                        nev += 1
                        sslot = stg.acquire(eng)
                        P.wait(eng, pe_tok)
                        if eng == 'scalar':
                            te = P.op('scalar', lambda e, b=b, sslot=sslot: e.activation(out=stg_t[sslot][:, :], in_=banks.banks[b][:, :], func=AF.Copy))
                        else:
                            te = P.op('vector', lambda e, b=b, sslot=sslot: e.tensor_copy(out=stg_t[sslot][:, :], in_=banks.banks[b][:, :]))
                        banks.release(b, te)
                        stg.store(sslot, partd[fb, oc * 128:(oc + 1) * 128, half * 512:half * 512 + 512], te)
                down_last_pe = pe_tok
                slabs.release(slot, pe_tok)
                ti += 1

        P.wait('sync', stg.final_toks())
        P.wait('sync', down_last_pe)
        accr = LoadRing(P, tp[3:6], "accr")
        x2tok = {}
        fin_stat = None
        osem = P.dsem("osem")
        for k in range(32):
            aslot, atok = accr.load(x1d[k * 128:(k + 1) * 128, :])
            ptoks = []
            for fb in range(8):
                pslot, ptok = oring.load(partd[fb, k * 128:(k + 1) * 128, :])
                ptoks.append((pslot, ptok))
            P.wait('vector', atok)
            tv = None
            for fb in range(8):
                pslot, ptok = ptoks[fb]
                P.wait('vector', ptok, tv)
                tv = P.op('vector', lambda e, aslot=aslot, pslot=pslot: e.tensor_tensor(
                    out=accr.tiles[aslot][:, :], in0=accr.tiles[aslot][:, :], in1=ofp[pslot], op=ALU.add))
                oring.release(pslot, tv)
            if not final:
                P.wait('sync', tv)
                tok = P.dma('sync', yT[k * 128:(k + 1) * 128, :], accr.tiles[aslot][:, :], osem)
                accr.release(aslot, tok)
            else:
                P.wait('sync', tv)
                tok = P.dma('sync', x1d[k * 128:(k + 1) * 128, :], accr.tiles[aslot][:, :], osem)
                x2tok[k] = tok
                xs, _ = None, None
                si = xring.next() if False else None
                sslot = k % 3
                P.wait('scalar', tv, xring.free[sslot])
                ta = P.op('scalar', lambda e, aslot=aslot, sslot=sslot: e.activation(
                    out=xring.tiles[sslot][:, :], in_=accr.tiles[aslot][:, :], func=AF.Square))
                P.wait('tensor', ta)
                if k == 0:
                    P.wait('tensor', st_read_tok)
                for half in range(2):
                    fin_stat = P.op('tensor', lambda e, sslot=sslot, half=half, k=k: e.matmul(
                        st[half][:, :], ones[:, :], xring.tiles[sslot][:, half * 512:half * 512 + 512],
                        start=(k == 0), stop=(k == 31)), signal=(half == 1))
                xring.free[sslot] = [fin_stat]
                accr.release(aslot, tok, ta)
        if final:
            st_read_tok, tv = rstd_from_stats(4096.0, fin_stat, last_norm_tok)
            P.wait('vector', tv)
            for k in range(32):
                P.wait('sync', x2tok[k])
                aslot, atok = accr.load(x1d[k * 128:(k + 1) * 128, :])
                P.wait('vector', atok)
                tv2 = P.op('vector', lambda e, aslot=aslot, k=k: e.scalar_tensor_tensor(
                    out=accr.tiles[aslot][:, :], in0=accr.tiles[aslot][:, :], scalar=gfin[:, k:k + 1], in1=rstd[:, :],
                    op0=ALU.mult, op1=ALU.mult))
                P.wait('sync', tv2)
                tok = P.dma('sync', yT[k * 128:(k + 1) * 128, :], accr.tiles[aslot][:, :], osem)
                accr.release(aslot, tok)
        P.wait('sync', (osem.h, osem.count))
        P.build()
    return nc


INC = 10696
SC_A = 192.0 ** -0.5
SC_H = 128.0 ** -0.5


def build_A():
    nc = bass.Bass("TRN2", target_bir_lowering=False)
    din = lambda n, s, dt=F32: nc.dram_tensor(n, s, dt, kind="ExternalInput").ap()
    xT = din("xT", [D, T])
    w_in = din("w_in", [D, INC])
    gattd = din("gatt", [128, 32])
    gqd = din("gq", [128, 7])
    gkvd = din("gkv", [128, 4])
    w_uq = din("w_uq", [896, 1536])
    w_uk = din("w_uk", [512, 1024])
    w_uv = din("w_uv", [512, 1024])
    bfd = din("bf", [8, 1])
    cosAd = din("cosA", [64, T]); sinAd = din("sinA", [64, T])
    cosBd = din("cosB", [128, T]); sinBd = din("sinB", [128, T])
    RAd = din("RA", [64, 64]); RBd = din("RB", [128, 128])
    onesd = din("ones", [128, 128])
    qk = nc.dram_tensor("qk", [64, 128, T], BF16, kind="ExternalOutput").ap()
    r64 = nc.dram_tensor("r64", [9, 64, T], BF16, kind="ExternalOutput").ap()
    v = nc.dram_tensor("v", [T, 4096], BF16, kind="ExternalOutput").ap()
    lf = nc.dram_tensor("lf", [8, T], F32, kind="ExternalOutput").ap()
    latd = nc.dram_tensor("latd", [12, 128, T], F32).ap()

    with ExitStack() as es:
        P = Prog(nc, es)
        hT = P.sb("hT", [128, 32 * T], BF16)
        slab_t = [P.sb(f"slab{i}", [128, 16384], BF16) for i in range(2)]
        ckvn = P.sb("ckvn", [128, 4 * T], BF16)
        tp = [P.sb(f"tp{i}", [128, T], F32) for i in range(4)]
        rstd = P.sb("rstd", [128, T], F32)
        cosA = P.sb("cosA_sb", [64, T], F32); sinA = P.sb("sinA_sb", [64, T], F32)
        cosB = P.sb("cosB_sb", [128, T], F32); sinB = P.sb("sinB_sb", [128, T], F32)
        RA = P.sb("RA_sb", [64, 64], F32); RB = P.sb("RB_sb", [128, 128], F32)
        ones = P.sb("ones_sb", [128, 128], F32)
        gatt = P.sb("gatt_sb", [128, 32], F32); gq = P.sb("gq_sb", [128, 7], F32); gkv = P.sb("gkv_sb", [128, 4], F32)
        bfs = P.sb("bf_sb", [8, 1], F32); nbf = P.sb("nbf_sb", [8, 1], F32)
        sb16_t = [P.sb(f"sb16_{i}", [128, 512], BF16) for i in range(4)]
        sf32_t = [P.sb(f"sf32_{i}", [128, 512], F32) for i in range(6)]
        bank_t = [P.ps(f"bank{i}", [128, 512]) for i in range(8)]
        st = bank_t[0:2]
        banks = Banks(P, bank_t[2:8])
        slabs = SlabRing(P, slab_t)
        csem = P.dsem("csem")
        for dst, src in [(ones, onesd), (gatt, gattd), (gq, gqd), (gkv, gkvd), (bfs, bfd), (cosA, cosAd), (sinA, sinAd),
                         (cosB, cosBd), (sinB, sinBd), (RA, RAd), (RB, RBd)]:
            ctok = P.dma('sync', dst[:, :], src, csem)
        for e in ['scalar', 'vector', 'tensor']:
            P.wait(e, ctok)
        tnb = P.op('vector', lambda e: e.tensor_scalar(out=nbf[:, :], in0=bfs[:, :], scalar1=-1.0, scalar2=None, op0=ALU.mult))

        def hch(k, a=0, n=T):
            return hT[:, k * T + a:k * T + a + n]

        xring = LoadRing(P, tp[0:2], "xring")
        sqr = TileRing(tp[2:4])
        s16 = StoreRing(P, sb16_t, "s16")
        s32 = StoreRing(P, sf32_t[0:2], "s32")
        tfr = TileRing(sf32_t[2:4])
        abr = TileRing(sf32_t[4:6])

        tasks = [('lat', 0, 512, 0), ('lat', 512, 384, 4), ('lat', 896, 512, 7), ('kr', 1408, 64, 11)]
        for mi, mname in enumerate('BCD'):
            base = 1472 + 3072 * mi
            for part in range(2):
                tasks.append(('q' + mname, base + 512 * part, 512, 16 + 16 * mi + 4 * part))
            for part in range(2):
                tasks.append(('k' + mname, base + 1024 + 512 * part, 512, 24 + 16 * mi + 4 * part))
            for part in range(2):
                tasks.append(('v' + mname, base + 2048 + 512 * part, 512, 1024 * (mi + 1) + 512 * part))
        tasks.append(('pf', 10688, 8, 0))
        loaded = {}

        def prefetch(i):
            if i < len(tasks) and i not in loaded:
                _, c0, cw, _ = tasks[i]
                loaded[i] = slabs.load(w_in, 0, 32, c0, cw)
        prefetch(0)

        def rstd_from_stats(n_feat, after_tok, war_toks):
            P.wait('scalar', after_tok, war_toks)
            ta = None
            for half in range(2):
                ta = P.op('scalar', lambda e, half=half: e.activation(
                    out=rstd[:, half * 512:half * 512 + 512], in_=st[half][:, :], func=AF.Sqrt,
                    scale=1.0 / n_feat, bias=EPS))
            P.wait('vector', ta)
            tv = P.op('vector', lambda e: e.reciprocal(out=rstd[:, :], in_=rstd[:, :]))
            return ta, tv

        pe_tok = None
        for k in range(32):
            xs, xtok = xring.load(xT[k * 128:(k + 1) * 128, :])
            si = sqr.next()
            P.wait('scalar', xtok, sqr.free[si])
            ta = P.op('scalar', lambda e, xs=xs, si=si: e.activation(out=sqr.tiles[si][:, :], in_=xring.tiles[xs], func=AF.Square))
            xring.release(xs, ta)
            P.wait('tensor', ta)
            for half in range(2):
                pe_tok = P.op('tensor', lambda e, si=si, half=half, k=k: e.matmul(
                    st[half][:, :], ones[:, :], sqr.tiles[si][:, half * 512:half * 512 + 512],
                    start=(k == 0), stop=(k == 31)), signal=(half == 1))
            sqr.free[si] = pe_tok
        st_read_tok, tv = rstd_from_stats(4096.0, pe_tok, None)
        P.wait('vector', tv)
        last_norm = None
        for k in range(32):
            xs, xtok = xring.load(xT[k * 128:(k + 1) * 128, :])
            P.wait('vector', xtok)
            last_norm = P.op('vector', lambda e, xs=xs, k=k: e.scalar_tensor_tensor(
                out=hch(k), in0=xring.tiles[xs], scalar=gatt[:, k:k + 1], in1=rstd[:, :], op0=ALU.mult, op1=ALU.mult))
            xring.release(xs, last_norm)
        P.wait('tensor', last_norm)

        nev = [0]

        def evac_copy_store(b, npart, dst, pe_tok, scale=None, f32=False):
            ring = s32 if f32 else s16
            eng = 'scalar' if nev[0] % 2 == 0 else 'vector'
            nev[0] += 1
            slot = ring.acquire(eng)
            P.wait(eng, pe_tok)
            o = ring.tiles[slot][0:npart, :]
            i = banks.banks[b][0:npart, :]
            if eng == 'scalar':
                te = P.op('scalar', lambda e: e.activation(out=o, in_=i, func=AF.Copy, scale=(1.0 if scale is None else scale)))
            else:
                if scale is None:
                    te = P.op('vector', lambda e: e.tensor_copy(out=o, in_=i))
                else:
                    te = P.op('vector', lambda e: e.tensor_scalar(out=o, in0=i, scalar1=scale, scalar2=None, op0=ALU.mult))
            banks.release(b, te)
            ring.store(slot, dst, te, src=o)
            return te

        def rope_store(src_ap, src_tok, src_is_psum_bank, npart, Rm, cosT, sinT, half, scale, dst):
            ti_ = tfr.next()
            tft = tfr.tiles[ti_][0:npart, :]
            if src_is_psum_bank is not None:
                P.wait('scalar', src_tok, tfr.free[ti_])
                tcp = P.op('scalar', lambda e: e.activation(out=tft, in_=src_ap, func=AF.Copy, scale=scale))
                banks.release(src_is_psum_bank, tcp)
            else:
                P.wait('scalar', src_tok, tfr.free[ti_])
                tcp = P.op('scalar', lambda e: e.activation(out=tft, in_=src_ap, func=AF.Copy, scale=scale))
            b2 = banks.acquire()
            P.wait('tensor', tcp)
            trot = P.op('tensor', lambda e: e.matmul(banks.banks[b2][0:npart, :], Rm[:, :], tft, start=True, stop=True))
            ai = abr.next()
            bi = abr.next()
            at = abr.tiles[ai][0:npart, :]
            bt = abr.tiles[bi][0:npart, :]
            cs = cosT[0:npart, half * 512:half * 512 + 512]
            sn = sinT[0:npart, half * 512:half * 512 + 512]
            P.wait('vector', tcp, abr.free[ai])
            ta_ = P.op('vector', lambda e: e.tensor_tensor(out=at, in0=tft, in1=cs, op=ALU.mult))
            P.wait('vector', trot, abr.free[bi])
            tb_ = P.op('vector', lambda e: e.tensor_tensor(out=bt, in0=banks.banks[b2][0:npart, :], in1=sn, op=ALU.mult))
            banks.release(b2, tb_)
            slot = s16.acquire('vector')
            o = s16.tiles[slot][0:npart, :]
            P.wait('vector', ta_, tb_)
            to = P.op('vector', lambda e: e.tensor_tensor(out=o, in0=at, in1=bt, op=ALU.add))
            tfr.free[ti_] = [trot, ta_]
            abr.free[ai] = to
            abr.free[bi] = to
            s16.store(slot, dst, to, src=o)

        lat_tok = {}
        for ti, (kind, c0, cw, dsti) in enumerate(tasks):
            prefetch(ti + 1)
            slot, stok = loaded[ti]
            P.wait('tensor', stok)
            sl = slab_t[slot]
            pe_tok = None
            if kind[0] == 'v':
                for tt in range(8):
                    b = banks.acquire()
                    for k in range(32):
                        pe_tok = P.op('tensor', lambda e, b=b, k=k, tt=tt, sl=sl: e.matmul(
                            banks.banks[b][:, :], hch(k, tt * 128, 128), sl[:, k * 512:k * 512 + 512],
                            start=(k == 0), stop=(k == 31)), signal=(k == 31))
                    evac_copy_store(b, 128, v[tt * 128:(tt + 1) * 128, dsti:dsti + 512], pe_tok)
            elif kind == 'pf':
                for half in range(2):
                    b = banks.acquire()
                    for k in range(32):
                        pe_tok = P.op('tensor', lambda e, b=b, k=k, half=half, sl=sl: e.matmul(
                            banks.banks[b][0:8, :], sl[:, k * 8:k * 8 + 8], hch(k, half * 512, 512),
                            start=(k == 0), stop=(k == 31)), signal=(k == 31))
                    ti_ = tfr.next()
                    e1 = tfr.tiles[ti_][0:8, :]
                    P.wait('scalar', pe_tok, tfr.free[ti_], tnb)
                    t1 = P.op('scalar', lambda e, b=b, e1=e1: e.activation(out=e1, in_=banks.banks[b][0:8, :], func=AF.Exp, scale=-1.0, bias=nbf[:, 0:1]))
                    banks.release(b, t1)
                    P.wait('scalar', t1)
                    t2 = P.op('scalar', lambda e, e1=e1: e.activation(out=e1, in_=e1, func=AF.Ln, scale=1.0, bias=1.0))
                    slot2 = s32.acquire('vector')
                    o = s32.tiles[slot2][0:8, :]
                    P.wait('vector', t2)
                    t3 = P.op('vector', lambda e, o=o, e1=e1: e.tensor_scalar(out=o, in0=e1, scalar1=-1.0, scalar2=None, op0=ALU.mult))
                    tfr.free[ti_] = t3
                    s32.store(slot2, lf[:, half * 512:half * 512 + 512], t3, src=o)
            else:
                ntile = (cw + 127) // 128
                for tl in range(ntile):
                    m = min(128, cw - tl * 128)
                    for half in range(2):
                        b = banks.acquire()
                        for k in range(32):
                            pe_tok = P.op('tensor', lambda e, b=b, k=k, tl=tl, half=half, sl=sl, m=m, cw=cw: e.matmul(
                                banks.banks[b][0:m, :], sl[:, k * cw + tl * 128:k * cw + tl * 128 + m], hch(k, half * 512, 512),
                                start=(k == 0), stop=(k == 31)), signal=(k == 31))
                        hs = slice(half * 512, half * 512 + 512)
                        if kind in ('lat', 'kr'):
                            evac_copy_store(b, m, latd[dsti + tl, 0:m, hs], pe_tok, f32=True)
                        elif kind in ('qB', 'kB'):
                            rope_store(banks.banks[b][:, :], pe_tok, b, 128, RB, cosB, sinB, half,
                                       SC_H if kind == 'qB' else 1.0, qk[dsti + tl, :, hs])
                        else:
                            evac_copy_store(b, 128, qk[dsti + tl, :, hs], pe_tok, scale=(SC_H if kind[0] == 'q' else None))
            slabs.release(slot, pe_tok)
        last_inproj_pe = pe_tok

        P.wait('gpsimd', last_inproj_pe)
        wsem = P.dsem("wsem")
        wq = slab_t[0]
        wkv = slab_t[1]
        P.dma('gpsimd', wq[:, 0:7 * 1536].rearrange("p (k c) -> p k c", c=1536), w_uq.rearrange("(k p) c -> p k c", p=128), wsem)
        P.dma('gpsimd', wkv[:, 0:4096].rearrange("p (k c) -> p k c", c=1024), w_uk.rearrange("(k p) c -> p k c", p=128), wsem)
        wtok = P.dma('gpsimd', wkv[:, 4096:8192].rearrange("p (k c) -> p k c", c=1024), w_uv.rearrange("(k p) c -> p k c", p=128), wsem)
        P.wait('sync', s32.final_toks(), last_inproj_pe)
        lat = [hT[:, 2 * j * T:2 * (j + 1) * T].bitcast(F32) for j in range(12)]
        cqn = hT[:, 24 * T:31 * T]
        lsem = P.dsem("lsem")
        ltok = []
        for j in range(12):
            npart = 64 if j == 11 else 128
            ltok.append(P.dma('sync', lat[j][0:npart, :], latd[j, 0:npart, :], lsem))
        ltok_all = ltok[-1]

        def lat_norm(j0, n, gv, dst_fn, nfeat, war):
            pe = None
            for j in range(n):
                si = sqr.next()
                P.wait('scalar', ltok_all, sqr.free[si])
                ta = P.op('scalar', lambda e, j=j, si=si: e.activation(out=sqr.tiles[si][:, :], in_=lat[j0 + j], func=AF.Square))
                P.wait('tensor', ta)
                if j == 0:
                    P.wait('tensor', war[0])
                for half in range(2):
                    pe = P.op('tensor', lambda e, si=si, half=half, j=j: e.matmul(
                        st[half][:, :], ones[:, :], sqr.tiles[si][:, half * 512:half * 512 + 512],
                        start=(j == 0), stop=(j == n - 1)), signal=(half == 1))
                sqr.free[si] = pe
            sr, tv = rstd_from_stats(nfeat, pe, war[1])
            P.wait('vector', tv)
            tn = None
            for j in range(n):
                tn = P.op('vector', lambda e, j=j: e.scalar_tensor_tensor(
                    out=dst_fn(j), in0=lat[j0 + j], scalar=gv[:, j:j + 1], in1=rstd[:, :], op0=ALU.mult, op1=ALU.mult))
            return sr, tn

        sr, tn_q = lat_norm(0, 7, gq, lambda j: cqn[:, j * T:(j + 1) * T], 896.0, (st_read_tok, last_norm))
        sr, tn_kv = lat_norm(7, 4, gkv, lambda j: ckvn[:, j * T:(j + 1) * T], 512.0, (sr, tn_q))
        P.wait('tensor', wtok, tn_q, tn_kv)
        for h in range(8):
            for half in range(2):
                hs = slice(half * 512, half * 512 + 512)
                b = banks.acquire()
                for k in range(7):
                    pe_tok = P.op('tensor', lambda e, b=b, k=k, h=h, half=half: e.matmul(
                        banks.banks[b][:, :], wq[:, k * 1536 + 192 * h:k * 1536 + 192 * h + 128],
                        cqn[:, k * T + half * 512:k * T + half * 512 + 512], start=(k == 0), stop=(k == 6)), signal=(k == 6))
                evac_copy_store(b, 128, qk[h, :, hs], pe_tok, scale=SC_A)
                b = banks.acquire()
                for k in range(7):
                    pe_tok = P.op('tensor', lambda e, b=b, k=k, h=h, half=half: e.matmul(
                        banks.banks[b][0:64, :], wq[:, k * 1536 + 192 * h + 128:k * 1536 + 192 * h + 192],
                        cqn[:, k * T + half * 512:k * T + half * 512 + 512], start=(k == 0), stop=(k == 6)), signal=(k == 6))
                rope_store(banks.banks[b][0:64, :], pe_tok, b, 64, RA, cosA, sinA, half, SC_A, r64[h, :, hs])
        for h in range(8):
            for half in range(2):
                hs = slice(half * 512, half * 512 + 512)
                b = banks.acquire()
                for k in range(4):
                    pe_tok = P.op('tensor', lambda e, b=b, k=k, h=h, half=half: e.matmul(
                        banks.banks[b][:, :], wkv[:, k * 1024 + 128 * h:k * 1024 + 128 * h + 128],
                        ckvn[:, k * T + half * 512:k * T + half * 512 + 512], start=(k == 0), stop=(k == 3)), signal=(k == 3))
                evac_copy_store(b, 128, qk[8 + h, :, hs], pe_tok)
        for tt in range(8):
            for vs in range(2):
                b = banks.acquire()
                for k in range(4):
                    pe_tok = P.op('tensor', lambda e, b=b, k=k, tt=tt, vs=vs: e.matmul(
                        banks.banks[b][:, :], ckvn[:, k * T + tt * 128:k * T + tt * 128 + 128],
                        wkv[:, 4096 + k * 1024 + 512 * vs:4096 + k * 1024 + 512 * vs + 512], start=(k == 0), stop=(k == 3)), signal=(k == 3))
                evac_copy_store(b, 128, v[tt * 128:(tt + 1) * 128, 512 * vs:512 * vs + 512], pe_tok)
        for half in range(2):
            hs = slice(half * 512, half * 512 + 512)
            rope_store(lat[11][0:64, hs], ltok_all, None, 64, RA, cosA, sinA, half, 1.0, r64[8, :, hs])
        P.wait('sync', s16.final_toks(), s32.final_toks())
        P.build()
    return nc


S = 8192
NEG = -30000.0


def build_B(mixers="ABCD"):
    nc = bass.Bass("TRN2", target_bir_lowering=False)
    din = lambda n, s, dt=BF16: nc.dram_tensor(n, s, dt, kind="ExternalInput").ap()
    aq = din("aq", [128, S]); ak = din("ak", [128, S]); aqr = din("aqr", [64, S]); akr = din("akr", [64, S]); av = din("av", [128, 64 * 128])
    bq = din("bq", [128, S]); bk = din("bk", [128, S])
    bv1 = din("bv1", [128, 64 * 128]); bv4 = din("bv4", [128, 64 * 128]); bv16 = din("bv16", [128, 64 * 128])
    cq = din("cq", [128, S]); ck = din("ck", [128, S]); cv = din("cv", [128, 64 * 128])
    dq = din("dq", [128, S]); dk = din("dk", [128, S]); dv = din("dv", [128, 64 * 128])
    lfr = din("lfr", [64, 128], F32); lft = din("lft", [128, 64], F32)
    cbd = din("cb", [128, 7 * 128])
    sud = din("su", [64, 65]); trid = din("tri", [128, 128])
    oT = nc.dram_tensor("oT", [4, 128, S], F32, kind="ExternalOutput").ap()

    with ExitStack() as es:
        P = Prog(nc, es)
        bufQ = P.sb("bufQ", [128, S], BF16)
        bufK = P.sb("bufK", [128, S], BF16)
        bufV = P.sb("bufV", [128, 64 * 128], BF16)
        bufX1 = P.sb("bufX1", [128, S], BF16)
        bufX2 = P.sb("bufX2", [128, S], BF16)
        Oacc = P.sb("Oacc", [128, S], F32)
        Lacc = P.sb("Lacc", [128, S], F32)
        p_t = [P.sb(f"pt{i}", [128, 512], BF16) for i in range(4)]
        p2_t = [P.sb(f"p2t{i}", [128, 512], BF16) for i in range(3)]
        f_t = [P.sb(f"ft{i}", [128, 512], F32) for i in range(8)]
        cb = P.sb("cb_sb", [128, 7 * 128], BF16)
        su = P.sb("su_sb", [64, 65], BF16); tri = P.sb("tri_sb", [128, 128], BF16)
        lfr_s = P.sb("lfr_s", [64, 128], F32); lft_s = P.sb("lft_s", [128, 64], F32)
        nbt = P.sb("nbt", [128, 16 * 64], F32)
        ccol = P.sb("ccol", [128, 64], F32); offs = P.sb("offs", [128, 65], F32); p1s = P.sb("p1s", [128, 65], F32)
        spl = [P.sb(f"spl{i}", [128, 128], BF16) for i in range(9)]
        rsd = [P.sb(f"rsd{i}", [128, 128], F32) for i in range(2)]
        dm_t = [P.sb(f"dm{i}", [128, 512], BF16) for i in range(2)]
        bank = [P.ps(f"bank{i}", [128, 512]) for i in range(8)]
        ident = cb[:, 0:128]; onesb = cb[:, 128:256]; nmU = cb[:, 256:384]; nmL = cb[:, 384:512]
        nmS = cb[:, 512:640]; nTinc = cb[:, 640:768]; zerosb = cb[:, 768:896]
        csem = P.dsem("csem")
        P.dma('sync', cb[:, :], cbd, csem)
        P.dma('gpsimd', su[:, :], sud, csem)
        P.dma('gpsimd', tri[:, :], trid, csem)
        P.dma('sync', lfr_s[:, :], lfr, csem)
        ctok = P.dma('sync', lft_s[:, :], lft, csem)
        for e in ['scalar', 'vector', 'tensor']:
            P.wait(e, ctok)
        lsem = P.dsem("lsem")
        ost = StoreRing(P, f_t[0:2], "ost")

        def barrier():
            for e in ['scalar', 'vector', 'tensor', 'sync', 'gpsimd']:
                for o in ['scalar', 'vector', 'tensor', 'gpsimd']:
                    if o != e and P.pcnt[o] > 0:
                        P.wait(e, (P.psem[o], P.pcnt[o]))
            for e in ['scalar', 'vector', 'tensor']:
                P.wait(e, ost.final_toks())

        def load(pairs):
            tok = None
            for dst, src in pairs:
                tok = P.dma('sync', dst, src, lsem)
            for e in ['scalar', 'vector', 'tensor']:
                P.wait(e, tok)

        def cols(qc, kb):
            j = kb - 4 * qc
            c0 = 0 if j < 0 else 128 * j
            return j, c0

        def run_AD(mi, is_A, nb_fn=None, dm_fn=None):
            sb = Banks(P, bank[0:3]); ob = Banks(P, bank[3:5]); lb = Banks(P, bank[5:7])
            pr = TileRing(p_t)
            steps = [(qc, kb) for qc in range(16) for kb in range(4 * qc + 4)]
            LA = 2
            st_ = {}
            cur = {}

            def qk(i):
                qc, kb = steps[i]
                j, c0 = cols(qc, kb)
                b = sb.acquire()
                qs = slice(qc * 512 + c0, qc * 512 + 512)
                ks = slice(kb * 128, kb * 128 + 128)
                extra = []
                if is_A:
                    extra.append((bufX2[0:64, ks], bufX1[0:64, qs], slice(c0, 512)))
                else:
                    dmi = dm_fn(qc)
                    extra.append((onesb, dm_t[dmi][:, c0:512], slice(c0, 512)))
                if j >= 0:
                    extra.append((ident, nmU, slice(c0, c0 + 128)))
                P.op('tensor', lambda e: e.matmul(bank[b][:, c0:512], bufK[:, ks], bufQ[:, qs], start=True, stop=False), signal=False)
                t = None
                for n_, (l_, r_, cs_) in enumerate(extra):
                    last = n_ == len(extra) - 1
                    t = P.op('tensor', lambda e, l_=l_, r_=r_, cs_=cs_, last=last: e.matmul(bank[b][:, cs_], l_, r_, start=False, stop=last), signal=last)
                pi = pr.next()
                P.wait('scalar', t, pr.free[pi])
                if is_A:
                    te = P.op('scalar', lambda e: e.activation(out=p_t[pi][:, c0:512], in_=bank[b][:, c0:512], func=AF.Exp))
                else:
                    bias = nb_fn(qc, kb)
                    te = P.op('scalar', lambda e: e.activation(out=p_t[pi][:, c0:512], in_=bank[b][:, c0:512], func=AF.Exp, bias=bias, scale=1.0))
                sb.release(b, te)
                st_[i] = (pi, te)

            def pv(i):
                qc, kb = steps[i]
                j, c0 = cols(qc, kb)
                pi, te = st_.pop(i)
                first = kb == 0
                last = kb == 4 * qc + 3
                if first:
                    cur['o'] = ob.acquire(); cur['l'] = lb.acquire()
                o_, l_ = cur['o'], cur['l']
                P.wait('tensor', te)
                P.op('tensor', lambda e: e.matmul(bank[3 + o_][:, c0:512], bufV[:, kb * 128:kb * 128 + 128], p_t[pi][:, c0:512], start=first, stop=last), signal=False)
                t = P.op('tensor', lambda e: e.matmul(bank[5 + l_][:, c0:512], onesb, p_t[pi][:, c0:512], start=first, stop=last))
                pr.free[pi] = t
                if last:
                    P.wait('vector', t, cur.get('rlw'))
                    t1 = P.op('vector', lambda e: e.reciprocal(out=f_t[2][:, :], in_=bank[5 + l_][:, :]))
                    lb.release(l_, t1)
                    slot = ost.acquire('vector')
                    P.wait('vector', t1)
                    t2 = P.op('vector', lambda e: e.tensor_tensor(out=ost.tiles[slot], in0=bank[3 + o_][:, :], in1=f_t[2][:, :], op=ALU.mult))
                    ob.release(o_, t2)
                    cur['rlw'] = t2
                    ost.store(slot, oT[mi, :, qc * 512:qc * 512 + 512], t2)

            for i in range(len(steps) + LA):
                if i < len(steps):
                    qk(i)
                if i >= LA:
                    pv(i - LA)

        if 'A' in mixers:
            load([(bufQ[:, :], aq), (bufK[:, :], ak), (bufX1[0:64, :], aqr), (bufX2[0:64, :], akr), (bufV[:, :], av)])
            run_AD(0, True)
            barrier()

        if 'D' in mixers:
            load([(bufQ[:, :], dq), (bufK[:, :], dk), (bufV[:, :], dv)])
            def split3(src, npart, ncol, base):
                hi = spl[base][0:npart, 0:ncol]; mid = spl[base + 1][0:npart, 0:ncol]; lo = spl[base + 2][0:npart, 0:ncol]
                r1 = rsd[0][0:npart, 0:ncol]; r2 = rsd[1][0:npart, 0:ncol]
                t = P.op('vector', lambda e: e.tensor_copy(out=hi, in_=src))
                P.wait('vector', t)
                t = P.op('vector', lambda e: e.tensor_tensor(out=r1, in0=src, in1=hi, op=ALU.subtract))
                P.wait('vector', t)
                t = P.op('vector', lambda e: e.tensor_copy(out=mid, in_=r1))
                P.wait('vector', t)
                t = P.op('vector', lambda e: e.tensor_tensor(out=r2, in0=r1, in1=mid, op=ALU.subtract))
                P.wait('vector', t)
                t = P.op('vector', lambda e: e.tensor_copy(out=lo, in_=r2))
                return [hi, mid, lo], t
            lfr3, t_a = split3(lfr_s[:, :], 64, 128, 0)
            P.wait('vector', t_a)
            lft3, t_b = split3(lft_s[:, :], 128, 64, 3)
            P.wait('tensor', t_a, t_b)
            for n_, a_ in enumerate(lfr3):
                t = P.op('tensor', lambda e, a_=a_, n_=n_: e.matmul(bank[0][:, 0:65], a_, su[:, :], start=(n_ == 0), stop=(n_ == 2)), signal=(n_ == 2))
            P.wait('vector', t, t_b)
            t = P.op('vector', lambda e: e.tensor_copy(out=p1s[:, :], in_=bank[0][:, 0:65]))
            P.wait('vector', t)
            p13, t_c = split3(p1s[:, :], 128, 65, 6)
            P.wait('tensor', t_c)
            for n_, a_ in enumerate(p13):
                t = P.op('tensor', lambda e, a_=a_, n_=n_: e.matmul(bank[1][:, 0:65], onesb, a_, start=(n_ == 0), stop=(n_ == 2)), signal=(n_ == 2))
            for n_, a_ in enumerate(p13):
                P.op('tensor', lambda e, a_=a_, n_=n_: e.matmul(bank[2][:, 0:64], onesb, a_[:, 0:64], start=(n_ == 0), stop=False), signal=False)
            for n_, a_ in enumerate(lft3):
                t = P.op('tensor', lambda e, a_=a_, n_=n_: e.matmul(bank[2][:, 0:64], tri[:, :], a_, start=False, stop=(n_ == 2)), signal=(n_ == 2))
            P.wait('vector', t)
            P.op('vector', lambda e: e.tensor_copy(out=offs[:, :], in_=bank[1][:, 0:65]))
            t = P.op('vector', lambda e: e.tensor_copy(out=ccol[:, :], in_=bank[2][:, 0:64]))
            P.wait('vector', t)
            for qc in range(16):
                t = P.op('vector', lambda e, qc=qc: e.tensor_scalar(out=nbt[:, qc * 64:(qc + 1) * 64], in0=ccol[:, :], scalar1=-1.0,
                                                                    scalar2=offs[:, 4 * qc + 4:4 * qc + 5], op0=ALU.mult, op1=ALU.add))
            P.wait('scalar', t)
            P.wait('tensor', t)
            dmr = TileRing(dm_t)
            dm_state = {}

            def dm_fn(qc):
                if qc not in dm_state:
                    di = dmr.next()
                    P.wait('vector', dmr.free[di], t)
                    tt = None
                    for j in range(4):
                        idx = qc * 64 + 4 * qc + j
                        tt = P.op('vector', lambda e, j=j, idx=idx, di=di: e.tensor_scalar(
                            out=dm_t[di][:, 128 * j:128 * j + 128], in0=ident, scalar1=nbt[:, idx:idx + 1], scalar2=-1.0, op0=ALU.mult, op1=ALU.mult))
                    P.wait('tensor', tt)
                    dm_state[qc] = di
                    if qc >= 1:
                        pass
                return dm_state[qc]
            orig_dm_fn = dm_fn
            snap = {}

            def dm_fn2(qc):
                if qc not in dm_state and qc >= 1:
                    prev = dm_state[qc - 1]
                    dmr.free[prev] = (P.psem['tensor'], P.pcnt['tensor'])
                return orig_dm_fn(qc)

            run_AD(3, False, nb_fn=lambda qc, kb: nbt[:, qc * 64 + kb:qc * 64 + kb + 1], dm_fn=dm_fn2)
            barrier()

        if 'C' in mixers:
            load([(bufQ[:, :], cq), (bufK[:, :], ck), (bufV[:, :], cv)])
            zb = Banks(P, bank[0:4]); tb = Banks(P, bank[4:6]); ob = Banks(P, bank[6:8])
            et = TileRing(f_t[3:5]); at = TileRing(f_t[5:7]); Rb = f_t[7]
            spr = TileRing(p2_t); pr = TileRing(p_t)
            steps = [(qc, kb) for qc in range(16) for kb in range(4 * qc + 3, -1, -1)]
            sa = {}; sbb = {}; sc = {}
            cur = {}
            rb_tok = [None]

            def stA(i):
                qc, kb = steps[i]
                j, c0 = cols(qc, kb)
                b = zb.acquire()
                qs = slice(qc * 512 + c0, qc * 512 + 512)
                ks = slice(kb * 128, kb * 128 + 128)
                if j >= 0:
                    P.op('tensor', lambda e: e.matmul(bank[b][:, c0:512], bufK[:, ks], bufQ[:, qs], start=True, stop=False), signal=False)
                    t = P.op('tensor', lambda e: e.matmul(bank[b][:, c0:c0 + 128], ident, nmS, start=False, stop=True))
                else:
                    t = P.op('tensor', lambda e: e.matmul(bank[b][:, c0:512], bufK[:, ks], bufQ[:, qs], start=True, stop=True))
                sa[i] = (b, t)

            def stB(i):
                qc, kb = steps[i]
                j, c0 = cols(qc, kb)
                b, t = sa.pop(i)
                ei = et.next()
                P.wait('scalar', t, et.free[ei])
                t1 = P.op('scalar', lambda e: e.activation(out=et.tiles[ei][:, c0:512], in_=bank[b][:, c0:512], func=AF.Exp))
                si = spr.next()
                P.wait('scalar', t1, spr.free[si])
                t2 = P.op('scalar', lambda e: e.activation(out=p2_t[si][:, c0:512], in_=et.tiles[ei][:, c0:512], func=AF.Ln, bias=1.0, scale=1.0))
                et.free[ei] = t2
                P.wait('tensor', t2)
                t3 = P.op('tensor', lambda e: e.matmul(bank[b][:, c0:512], nTinc, p2_t[si][:, c0:512], start=False, stop=True))
                tbk = tb.acquire()
                t4 = P.op('tensor', lambda e: e.matmul(bank[4 + tbk][:, c0:512], onesb, p2_t[si][:, c0:512], start=True, stop=True))
                spr.free[si] = t4
                sbb[i] = (b, tbk, t3, t4)

            def stC(i):
                qc, kb = steps[i]
                j, c0 = cols(qc, kb)
                b, tbk, t3, t4 = sbb.pop(i)
                first = kb == 4 * qc + 3
                pi = pr.next()
                if first:
                    P.wait('scalar', t3, pr.free[pi])
                    t6 = P.op('scalar', lambda e: e.activation(out=p_t[pi][:, c0:512], in_=bank[b][:, c0:512], func=AF.Exp))
                    zb.release(b, t6)
                    P.wait('vector', t4, rb_tok[0])
                    tm = P.op('vector', lambda e: e.memset(Rb[:, :], 0.0))
                    P.wait('vector', tm)
                    t7 = P.op('vector', lambda e: e.tensor_copy(out=Rb[:, c0:512], in_=bank[4 + tbk][:, c0:512]))
                    rb_tok[0] = t7
                    tb.release(tbk, t7)
                else:
                    ai = at.next()
                    P.wait('vector', t3, at.free[ai], rb_tok[0])
                    t5 = P.op('vector', lambda e: e.tensor_tensor(out=at.tiles[ai][:, c0:512], in0=bank[b][:, c0:512], in1=Rb[:, c0:512], op=ALU.subtract))
                    zb.release(b, t5)
                    P.wait('vector', t4, t5)
                    t7 = P.op('vector', lambda e: e.tensor_tensor(out=Rb[:, c0:512], in0=Rb[:, c0:512], in1=bank[4 + tbk][:, c0:512], op=ALU.add))
                    rb_tok[0] = t7
                    tb.release(tbk, t7)
                    P.wait('scalar', t5, pr.free[pi])
                    t6 = P.op('scalar', lambda e: e.activation(out=p_t[pi][:, c0:512], in_=at.tiles[ai][:, c0:512], func=AF.Exp))
                    at.free[ai] = t6
                sc[i] = (pi, t6)

            def stD(i):
                qc, kb = steps[i]
                j, c0 = cols(qc, kb)
                pi, t6 = sc.pop(i)
                first = kb == 4 * qc + 3
                last = kb == 0
                if first:
                    cur['o'] = ob.acquire()
                    o_ = cur['o']
                    P.op('tensor', lambda e: e.matmul(bank[6 + o_][:, :], zerosb, bufQ[:, 0:512], start=True, stop=False), signal=False)
                o_ = cur['o']
                P.wait('tensor', t6)
                t = P.op('tensor', lambda e: e.matmul(bank[6 + o_][:, c0:512], bufV[:, kb * 128:kb * 128 + 128], p_t[pi][:, c0:512], start=False, stop=last))
                pr.free[pi] = t
                if last:
                    slot = ost.acquire('vector')
                    P.wait('vector', t)
                    t2 = P.op('vector', lambda e: e.tensor_copy(out=ost.tiles[slot], in_=bank[6 + o_][:, :]))
                    ob.release(o_, t2)
                    ost.store(slot, oT[2, :, qc * 512:qc * 512 + 512], t2)

            ns = len(steps)
            for i in range(ns + 3):
                if i < ns:
                    stA(i)
                if 0 <= i - 1 < ns:
                    stB(i - 1)
                if 0 <= i - 2 < ns:
                    stC(i - 2)
                if 0 <= i - 3 < ns:
                    stD(i - 3)
            barrier()

        if 'B' in mixers:
            load([(bufQ[:, :], bq), (bufK[:, :], bk), (bufV[:, :], bv1), (bufX1[:, :], bv4), (bufX2[:, :], bv16)])
            ss = Banks(P, bank[0:2]); spv = Banks(P, bank[2:4]); ob = Banks(P, bank[4:6]); lb = Banks(P, bank[6:8])
            prs = TileRing(p_t[0:2]); prp = TileRing(p_t[2:4])
            groups = []
            for dl, vb, nbk in [(1, bufV, 64), (4, bufX1, 16), (16, bufX2, 4)]:
                for r in range(dl):
                    for n0 in range(0, nbk, 4):
                        groups.append((dl, vb, nbk, r, n0))
            st_ = {}
            acc_tok = [None]

            def sub(buf, dl, r, n):
                s0 = r + dl * 128 * n
                return buf[:, s0:s0 + dl * 127 + 1:dl]

            def g1(i):
                dl, vb, nbk, r, n0 = groups[i]
                bs = ss.acquire(); bp = spv.acquire()
                t = None
                for j in range(4):
                    n = n0 + j
                    P.op('tensor', lambda e, j=j, n=n: e.matmul(bank[bs][:, 128 * j:128 * j + 128], sub(bufK, dl, r, n), sub(bufQ, dl, r, n), start=True, stop=False), signal=False)
                    t = P.op('tensor', lambda e, j=j: e.matmul(bank[bs][:, 128 * j:128 * j + 128], ident, nmU, start=False, stop=True), signal=(j == 3))
                tp_ = None
                for j in range(4):
                    n = n0 + j
                    if n == 0:
                        continue
                    P.op('tensor', lambda e, j=j, n=n: e.matmul(bank[2 + bp][:, 128 * j:128 * j + 128], sub(bufK, dl, r, n - 1), sub(bufQ, dl, r, n), start=True, stop=False), signal=False)
                    tp_ = P.op('tensor', lambda e, j=j: e.matmul(bank[2 + bp][:, 128 * j:128 * j + 128], ident, nmL, start=False, stop=True), signal=(j == 3))
                pc0 = 128 if n0 == 0 else 0
                ps_ = prs.next(); pp_ = prp.next()
                P.wait('scalar', t, prs.free[ps_])
                te1 = P.op('scalar', lambda e: e.activation(out=p_t[ps_][:, :], in_=bank[bs][:, :], func=AF.Exp))
                ss.release(bs, te1)
                P.wait('scalar', tp_, prp.free[pp_])
                te2 = P.op('scalar', lambda e: e.activation(out=p_t[2 + pp_][:, pc0:512], in_=bank[2 + bp][:, pc0:512], func=AF.Exp))
                spv.release(bp, te2)
                st_[i] = (ps_, pp_, te1, te2)

            def g2(i):
                dl, vb, nbk, r, n0 = groups[i]
                ps_, pp_, te1, te2 = st_.pop(i)
                o_ = ob.acquire(); l_ = lb.acquire()
                P.wait('tensor', te1, te2)
                t = None
                for j in range(4):
                    n = n0 + j
                    cs = slice(128 * j, 128 * j + 128)
                    vs = lambda nn: vb[:, (r * nbk + nn) * 128:(r * nbk + nn) * 128 + 128]
                    if n > 0:
                        P.op('tensor', lambda e, cs=cs, n=n, vs=vs: e.matmul(bank[4 + o_][:, cs], vs(n - 1), p_t[2 + pp_][:, cs], start=True, stop=False), signal=False)
                        P.op('tensor', lambda e, cs=cs, n=n, vs=vs: e.matmul(bank[4 + o_][:, cs], vs(n), p_t[ps_][:, cs], start=False, stop=True), signal=False)
                        P.op('tensor', lambda e, cs=cs: e.matmul(bank[6 + l_][:, cs], onesb, p_t[2 + pp_][:, cs], start=True, stop=False), signal=False)
                        t = P.op('tensor', lambda e, cs=cs: e.matmul(bank[6 + l_][:, cs], onesb, p_t[ps_][:, cs], start=False, stop=True))
                    else:
                        P.op('tensor', lambda e, cs=cs, n=n, vs=vs: e.matmul(bank[4 + o_][:, cs], vs(n), p_t[ps_][:, cs], start=True, stop=True), signal=False)
                        t = P.op('tensor', lambda e, cs=cs: e.matmul(bank[6 + l_][:, cs], onesb, p_t[ps_][:, cs], start=True, stop=True))
                prs.free[ps_] = t
                prp.free[pp_] = t
                s0 = r + dl * 128 * n0
                asl = slice(s0, s0 + dl * 511 + 1, dl)
                P.wait('vector', t, acc_tok[0])
                if dl == 1:
                    t1 = P.op('vector', lambda e: e.tensor_copy(out=Oacc[:, asl], in_=bank[4 + o_][:, :]))
                    t2 = P.op('vector', lambda e: e.tensor_copy(out=Lacc[:, asl], in_=bank[6 + l_][:, :]))
                else:
                    t1 = P.op('vector', lambda e: e.tensor_tensor(out=Oacc[:, asl], in0=Oacc[:, asl], in1=bank[4 + o_][:, :], op=ALU.add))
                    t2 = P.op('vector', lambda e: e.tensor_tensor(out=Lacc[:, asl], in0=Lacc[:, asl], in1=bank[6 + l_][:, :], op=ALU.add))
                ob.release(o_, t1); lb.release(l_, t2)
                acc_tok[0] = t2

            LA = 1
            for i in range(len(groups) + LA):
                if i < len(groups):
                    g1(i)
                if i >= LA:
                    g2(i - LA)
            P.wait('vector', acc_tok[0])
            for c in range(16):
                csl = slice(c * 512, c * 512 + 512)
                t1 = P.op('vector', lambda e, csl=csl: e.reciprocal(out=Lacc[:, csl], in_=Lacc[:, csl]))
                slot = ost.acquire('vector')
                P.wait('vector', t1)
                t2 = P.op('vector', lambda e, csl=csl, slot=slot: e.tensor_tensor(out=ost.tiles[slot], in0=Oacc[:, csl], in1=Lacc[:, csl], op=ALU.mult))
                ost.store(slot, oT[1, :, csl], t2)
            barrier()
        P.wait('sync', ost.final_toks())
        P.build()
    return nc


THETA = 500000.0
def rope_tables(pos):
    pos = pos.astype(np.float32)
    invA = (THETA ** (-np.arange(32, dtype=np.float32) * (2.0 / 64))).astype(np.float32)
    angA = pos[None, :] * invA[:, None]
    cosA = np.concatenate([np.cos(angA), np.cos(angA)], 0).astype(np.float32)
    sinA = np.concatenate([np.sin(angA), np.sin(angA)], 0).astype(np.float32)
    invB = (THETA ** (-np.arange(16, dtype=np.float32) * (2.0 / 32))).astype(np.float32)
    angB = pos[None, :] * invB[:, None]
    T = pos.shape[0]
    cosB = np.ones((128, T), np.float32); sinB = np.zeros((128, T), np.float32)
    cosB[0:16] = np.cos(angB); cosB[16:32] = np.cos(angB)
    sinB[0:16] = np.sin(angB); sinB[16:32] = np.sin(angB)
    return cosA, sinA, cosB, sinB
def rot_mats():
    RA = np.zeros((64, 64), np.float32)
    for i in range(32):
        RA[i + 32, i] = -1.0
        RA[i, i + 32] = 1.0
    RB = np.zeros((128, 128), np.float32)
    for i in range(16):
        RB[i + 16, i] = -1.0
        RB[i, i + 16] = 1.0
    return RA, RB
def lay(g, nk):
    return np.ascontiguousarray(g.reshape(nk, 128).T)


import ml_dtypes
_BF = ml_dtypes.bfloat16
_PROGS = {}
NCORES = 8


def _prog(name):
    if name not in _PROGS:
        if name == 'A':
            _PROGS[name] = build_A()
        elif name == 'B':
            _PROGS[name] = build_B()
        elif name == 'C0':
            _PROGS[name] = build_C(False)
        elif name == 'C1':
            _PROGS[name] = build_C(True)
    return _PROGS[name]


def _b_consts():
    p = np.arange(128)[:, None]
    f = np.arange(128)[None, :]
    ident = (p == f).astype(np.float32)
    ones = np.ones((128, 128), np.float32)
    nmU = np.where(p > f, NEG, 0.0)
    nmL = np.where(p < f, NEG, 0.0)
    nmS = np.where(p >= f, NEG, 0.0)
    nTinc = np.where(p >= f, -1.0, 0.0)
    zeros = np.zeros((128, 128))
    cb = np.concatenate([ident, ones, nmU, nmL, nmS, nTinc, zeros], 1).astype(np.float32)
    su = (np.arange(64)[:, None] < np.arange(65)[None, :]).astype(np.float32)
    tri = (p <= f).astype(np.float32)
    return cb.astype(_BF), su.astype(_BF), tri.astype(_BF)


def _vlay(v):
    return np.ascontiguousarray(v.reshape(64, 128, 128).transpose(1, 0, 2)).reshape(128, 64 * 128)


def _vlay_d(v, dl):
    nbk = S // dl // 128
    a = v.reshape(nbk, 128, dl, 128)
    return np.ascontiguousarray(a.transpose(1, 2, 0, 3)).reshape(128, 64 * 128)


def _run(nc, in_maps):
    res = run_bass_kernel_spmd(nc, in_maps, core_ids=list(range(NCORES)))
    return res.results


def kernel(x, g_attn, w_in, g_q, g_kv, w_uq, w_uk, w_uv, b_f, g_out, w_o, g_mlp, w_up, w_down, g_final):
    f32 = lambda a: np.ascontiguousarray(np.asarray(a, dtype=np.float32))
    x = f32(x)
    depth = w_in.shape[0]
    ones = np.ones((128, 128), np.float32)
    RA, RB = rot_mats()
    cb, su, tri = _b_consts()
    tabs = [rope_tables(c * T + np.arange(T)) for c in range(NCORES)]
    xT = [np.ascontiguousarray(x[0, c * T:(c + 1) * T, :].T) for c in range(NCORES)]
    for l in range(depth):
        w_in_l = f32(w_in[l]); w_uq_l = f32(w_uq[l]); w_uk_l = f32(w_uk[l]); w_uv_l = f32(w_uv[l])
        gatt = lay(f32(g_attn[l]), 32); gq = lay(f32(g_q[l]), 7); gkv = lay(f32(g_kv[l]), 4)
        bfl = f32(b_f[l]).reshape(8, 1)
        in_maps = []
        for c in range(NCORES):
            cosA, sinA, cosB, sinB = tabs[c]
            in_maps.append(dict(xT=xT[c], w_in=w_in_l, gatt=gatt, gq=gq, gkv=gkv, w_uq=w_uq_l, w_uk=w_uk_l, w_uv=w_uv_l,
                                bf=bfl, cosA=cosA, sinA=sinA, cosB=cosB, sinB=sinB, RA=RA, RB=RB, ones=ones))
        ra = _run(_prog('A'), in_maps)
        del in_maps
        qk = np.concatenate([r["qk"] for r in ra], axis=2)
        r64 = np.concatenate([r["r64"] for r in ra], axis=2)
        v = np.concatenate([r["v"] for r in ra], axis=0)
        lf = np.concatenate([r["lf"] for r in ra], axis=1)
        del ra
        in_maps = []
        cg = np.ascontiguousarray
        for h in range(NCORES):
            vs = lambda mi: cg(v[:, mi * 1024 + h * 128:mi * 1024 + (h + 1) * 128])
            vb = vs(1)
            lfh = cg(lf[h]).reshape(64, 128)
            in_maps.append(dict(
                aq=cg(qk[h]), ak=cg(qk[8 + h]), aqr=cg(r64[h]), akr=cg(r64[8]), av=_vlay(vs(0)),
                bq=cg(qk[16 + h]), bk=cg(qk[24 + h]), bv1=_vlay(vb), bv4=_vlay_d(vb, 4), bv16=_vlay_d(vb, 16),
                cq=cg(qk[32 + h]), ck=cg(qk[40 + h]), cv=_vlay(vs(2)),
                dq=cg(qk[48 + h]), dk=cg(qk[56 + h]), dv=_vlay(vs(3)),
                lfr=lfh, lft=cg(lfh.T), cb=cb, su=su, tri=tri))
        rb = _run(_prog('B'), in_maps)
        del in_maps, qk, r64, v, lf
        o_all = np.stack([r["oT"] for r in rb], axis=1)
        del rb
        o_all = o_all.reshape(4096, S)
        w_o_l = f32(w_o[l]); w_up_l = f32(w_up[l]); w_down_l = f32(w_down[l])
        gout = lay(f32(g_out[l]), 32); gmlp = lay(f32(g_mlp[l]), 32); gfin = lay(f32(g_final), 32)
        in_maps = []
        for c in range(NCORES):
            in_maps.append(dict(oT=cg(o_all[:, c * T:(c + 1) * T]), xT=xT[c], gout=gout, gmlp=gmlp, gfin=gfin,
                                w_o=w_o_l, w_up=w_up_l, w_down=w_down_l, ones=ones))
        rc = _run(_prog('C1' if l == depth - 1 else 'C0'), in_maps)
        del in_maps, o_all
        xT = [r["yT"] for r in rc]
        del rc
    out = np.concatenate([t.T for t in xT], axis=0).reshape(1, S, D).astype(np.float32)
    return out
```

```python
import numpy as np
from contextlib import ExitStack
import concourse.bass as bass
import concourse.mybir as mybir
from concourse.bass_utils import run_bass_kernel_spmd

F32 = mybir.dt.float32
BF16 = mybir.dt.bfloat16
AF = mybir.ActivationFunctionType
ALU = mybir.AluOpType
ENGS = ['sync', 'scalar', 'vector', 'gpsimd', 'tensor']
EPS = 1e-6


class DSem:
    def __init__(self, h):
        self.h = h
        self.count = 0


class Prog:
    def __init__(self, nc, es):
        self.nc = nc
        self.es = es
        self.ops = {e: [] for e in ENGS}
        self.psem = {}
        self.pcnt = {e: 0 for e in ENGS}
        self.waited = {}
        self.nsem = 0
        for e in ['scalar', 'vector', 'gpsimd', 'tensor']:
            self.psem[e] = es.enter_context(nc.semaphore(f"p_{e}"))
        self.bank_t = [es.enter_context(nc.psum_tensor(f"bank{i}", [128, 512], F32)) for i in range(8)]
        self.sem_pool = []
        self.phase_sems = []
        self.phase_es = None
        self.phase_id = 0
        self.n_instr = 0

    def dsem(self, name):
        if self.sem_pool:
            d = self.sem_pool.pop()
        else:
            d = DSem(self.es.enter_context(self.nc.semaphore(f"ds{self.nsem}")))
            self.nsem += 1
        self.phase_sems.append(d)
        return d

    def sb(self, name, shape, dt):
        return self.phase_es.enter_context(self.nc.sbuf_tensor(f"{name}_p{self.phase_id}", shape, dt))

    def pid_of(self, e, name):
        if not hasattr(self, '_pid'):
            self._pid = {}
        if name not in self._pid:
            self._pid[name] = e.partition_id()
        return self._pid[name]

    def begin_phase(self):
        self.phase_es = ExitStack()
        self.phase_id += 1
        self.snap_p = dict(self.pcnt)
        for d in self.sem_pool:
            d.byq = {}

    def barrier_all(self):
        for e in ENGS:
            for o in ['scalar', 'vector', 'gpsimd', 'tensor']:
                if o != e and self.pcnt[o] > 0:
                    self.wait(e, (self.psem[o], self.pcnt[o]))
            for d in self.phase_sems:
                if d.count > 0:
                    self.wait(e, (d.h, d.count))

    def end_phase(self, cond_core=None):
        self.barrier_all()
        self.build(cond_core)
        for k in self.ops:
            self.n_instr += len(self.ops[k])
            self.ops[k] = []
        self.phase_es.close()
        self.phase_es = None
        self.sem_pool.extend(self.phase_sems)
        self.phase_sems = []

    def ps(self, name, shape, dt=F32):
        return self.es.enter_context(self.nc.psum_tensor(name, shape, dt))

    def op(self, eng, fn, signal=True):
        if signal:
            self.pcnt[eng] += 1
            s = self.psem[eng]
            self.ops[eng].append(lambda e, fn=fn, s=s: fn(e).then_inc(s, 1))
            return (s, self.pcnt[eng])
        self.ops[eng].append(fn)
        return None

    def wait(self, eng, *toks):
        for tok in toks:
            if tok is None:
                continue
            if isinstance(tok, list):
                self.wait(eng, *tok)
                continue
            s, v = tok
            if eng in self.psem and s is self.psem[eng]:
                pass
            key = (eng, id(s))
            if self.waited.get(key, 0) >= v:
                continue
            self.waited[key] = v
            self.ops[eng].append(lambda e, s=s, v=v: e.wait_ge(s, v))

    def dma(self, queue, out, in_, ds, **kw):
        ds.count += 16
        ds.byq = getattr(ds, 'byq', {})
        ds.byq[queue] = ds.byq.get(queue, 0) + 16
        self.ops[queue].append(lambda e, out=out, in_=in_, h=ds.h, kw=kw: e.dma_start(
            out=(out(e) if callable(out) else out), in_=(in_(e) if callable(in_) else in_), **kw).then_inc(h, 16))
        return (ds.h, ds.count)

    def build(self, cond_core=None):
        with self.nc.Block() as block:
            for name in ENGS:
                ops = self.ops[name]
                if not ops:
                    continue
                comp = []
                if cond_core is not None:
                    if name in self.psem and self.pcnt[name] > self.snap_p[name]:
                        comp.append((self.psem[name], self.pcnt[name] - self.snap_p[name]))
                    for d in self.phase_sems:
                        n = getattr(d, 'byq', {}).get(name, 0)
                        if n:
                            comp.append((d.h, n))

                def body(e, ops=ops, comp=comp):
                    if cond_core is None:
                        for f in ops:
                            f(e)
                    else:
                        pid = self.pid_of(e, name)
                        with e.If(pid == cond_core):
                            for f in ops:
                                f(e)
                        with e.Else():
                            for s, n in comp:
                                e.sem_inc(s, n)
                getattr(block, name)(body)


class Banks:
    def __init__(self, P, banks):
        self.P = P
        self.banks = banks
        self.free = [None] * len(banks)
        self.n = 0

    def acquire(self):
        i = self.n % len(self.banks)
        self.n += 1
        self.P.wait('tensor', self.free[i])
        return i

    def release(self, i, *toks):
        self.free[i] = list(toks)


class TileRing:
    def __init__(self, tiles):
        self.tiles = tiles
        self.free = [None] * len(tiles)
        self.n = 0

    def next(self):
        i = self.n % len(self.tiles)
        self.n += 1
        return i


class SlabRing:
    def __init__(self, P, tiles, queue='gpsimd'):
        self.P = P
        self.tiles = tiles
        self.sems = [P.dsem(f"slab_sem{i}") for i in range(len(tiles))]
        self.free = [None] * len(tiles)
        self.n = 0
        self.queue = queue

    def load(self, W, r0, kc, c0, cw, kgroup=8):
        P = self.P
        slot = self.n % len(self.tiles)
        self.n += 1
        P.wait(self.queue, self.free[slot])
        t = self.tiles[slot]
        src = W[r0:r0 + kc * 128, c0:c0 + cw].rearrange("(k p) c -> p k c", p=128)
        tok = None
        for k0 in range(0, kc, kgroup):
            k1 = min(kc, k0 + kgroup)
            dst = t[:, k0 * cw:k1 * cw].rearrange("p (k c) -> p k c", c=cw)
            tok = P.dma(self.queue, dst, src[:, k0:k1, :], self.sems[slot])
        return slot, tok

    def release(self, slot, tok):
        self.free[slot] = tok


T = 1024
D = 4096
FF = 16384


class LoadRing:
    def __init__(self, P, tiles, name, queue='sync'):
        self.P = P
        self.tiles = [t if type(t).__name__ == 'AP' else t[:, :] for t in tiles]
        self.sems = [P.dsem(f"{name}_s{i}") for i in range(len(tiles))]
        self.free = [None] * len(tiles)
        self.n = 0
        self.queue = queue

    def load(self, src):
        P = self.P
        slot = self.n % len(self.tiles)
        self.n += 1
        P.wait(self.queue, self.free[slot])
        tok = P.dma(self.queue, self.tiles[slot], src, self.sems[slot])
        return slot, tok

    def release(self, slot, *toks):
        self.free[slot] = list(toks)


class StoreRing:
    def __init__(self, P, tiles, name, queue='sync'):
        self.P = P
        self.tiles = [t if type(t).__name__ == 'AP' else t[:, :] for t in tiles]
        self.sems = [P.dsem(f"{name}_s{i}") for i in range(len(tiles))]
        self.free = [None] * len(tiles)
        self.n = 0
        self.queue = queue

    def acquire(self, eng):
        slot = self.n % len(self.tiles)
        self.n += 1
        self.P.wait(eng, self.free[slot])
        return slot

    def store(self, slot, dst, after, extra=(), src=None):
        P = self.P
        P.wait(self.queue, after)
        tok = P.dma(self.queue, dst, self.tiles[slot] if src is None else src, self.sems[slot])
        self.free[slot] = [tok] + list(extra)
        return tok

    def final_toks(self):
        return [(s.h, s.count) for s in self.sems if s.count > 0]


def xrows(xT, k):
    if callable(xT):
        return xT(k)
    return xT[k * 128:(k + 1) * 128, :]

def emit_C(P, final, oT_tile, xT, goutd, gmlpd, gfind, w_o, w_up, w_down, onesd, yT, x1d, accd):
    P.begin_phase()
    acc_t = accd if final else yT
    acc_tok = {}
    if True:
        actT = P.sb("actT", [128, 32 * T], BF16)
        ubuf = P.sb("ubuf", [128, 16 * T], BF16)
        slab_t = [P.sb(f"slab{i}", [128, 16384], BF16) for i in range(2)]
        tp = [P.sb(f"tp{i}", [128, T], F32) for i in range(6)]
        stg_t = [P.sb(f"stg{i}", [128, 512], F32) for i in range(4)]
        rl_t = [P.sb(f"rl{i}", [128, 512], F32) for i in range(2)]
        rstd = P.sb("rstd", [128, T], F32)
        ones = P.sb("ones_sb", [128, 128], F32)
        gout = P.sb("gout_sb", [128, 32], F32)
        gmlp = P.sb("gmlp_sb", [128, 32], F32)
        gfin = P.sb("gfin_sb", [128, 32], F32)
        bank_t = P.bank_t
        st = bank_t[0:2]
        banks = Banks(P, bank_t[2:8])
        slabs = SlabRing(P, slab_t)
        csem = P.dsem("csem")
        P.dma('sync', ones[:, :], onesd, csem)
        P.dma('sync', gout[:, :], goutd, csem)
        P.dma('sync', gmlp[:, :], gmlpd, csem)
        ctok = P.dma('sync', gfin[:, :], gfind, csem)
        for e in ['scalar', 'vector', 'tensor']:
            P.wait(e, ctok)

        def act(k, half=None):
            if half is None:
                return actT[:, k * T:(k + 1) * T]
            return actT[:, k * T + half * 512:k * T + half * 512 + 512]

        ofp = [ubuf[:, 2 * j * T:2 * (j + 1) * T].bitcast(F32) for j in range(8)]
        oring = LoadRing(P, ofp, "oring")
        xring = LoadRing(P, tp[0:3], "xring")
        sqr = TileRing(tp[3:5])
        stg = StoreRing(P, stg_t, "stg")

        slab_tasks = []
        for s in range(8):
            slab_tasks.append(('o', w_o, 0, 32, 512 * s, 512))
        for fb in range(8):
            for s in range(4):
                slab_tasks.append(('u', w_up, 0, 32, fb * 2048 + 512 * s, 512))
            for s in range(4):
                slab_tasks.append(('d', w_down, fb * 2048, 16, 1024 * s, 1024))
        slab_loaded = {}

        def prefetch(i):
            if i < len(slab_tasks) and i not in slab_loaded:
                _, W, r0, kc, c0, cw = slab_tasks[i]
                slab_loaded[i] = slabs.load(W, r0, kc, c0, cw)

        prefetch(0)

        def rstd_from_stats(n_feat, after_tok, war_toks):
            P.wait('scalar', after_tok, war_toks)
            ta = None
            for half in range(2):
                ta = P.op('scalar', lambda e, half=half: e.activation(
                    out=rstd[:, half * 512:half * 512 + 512], in_=st[half][:, :], func=AF.Sqrt,
                    scale=1.0 / n_feat, bias=EPS))
            P.wait('vector', ta)
            tv = P.op('vector', lambda e: e.reciprocal(out=rstd[:, :], in_=rstd[:, :]))
            return ta, tv

        last_norm_tok = None
        st_read_tok = None
        for gi in range(4):
            ltoks = []
            for j in range(8):
                k = 8 * gi + j
                slot, tok = oring.load(oT_tile(k))
                ltoks.append(tok)
            pe_tok = None
            for j in range(8):
                P.wait('scalar', ltoks[j])
                si = sqr.next()
                P.wait('scalar', sqr.free[si])
                ta = P.op('scalar', lambda e, j=j, si=si: e.activation(out=sqr.tiles[si][:, :], in_=ofp[j], func=AF.Square))
                P.wait('tensor', ta)
                if j == 0:
                    P.wait('tensor', st_read_tok)
                for half in range(2):
                    pe_tok = P.op('tensor', lambda e, j=j, si=si, half=half: e.matmul(
                        st[half][:, :], ones[:, :], sqr.tiles[si][:, half * 512:half * 512 + 512],
                        start=(j == 0), stop=(j == 7)), signal=(half == 1))
                sqr.free[si] = pe_tok
            st_read_tok, tv = rstd_from_stats(1024.0, pe_tok, last_norm_tok)
            P.wait('vector', tv)
            for j in range(8):
                k = 8 * gi + j
                P.wait('vector', ltoks[j])
                last_norm_tok = P.op('vector', lambda e, j=j, k=k: e.scalar_tensor_tensor(
                    out=act(k), in0=ofp[j], scalar=gout[:, k:k + 1], in1=rstd[:, :], op0=ALU.mult, op1=ALU.mult))
                oring.release(j, last_norm_tok)

        P.wait('tensor', last_norm_tok)
        pending = None
        x1tok = {}
        xload = {}

        def xprefetch(oc):
            if oc < 32 and oc not in xload:
                xload[oc] = xring.load(xrows(xT, oc))

        xprefetch(0)
        xprefetch(1)
        ti = 0
        stat_tok = None
        for s in range(8):
            prefetch(ti + 1)
            slot, stok = slab_loaded[ti]
            P.wait('tensor', stok)
            sl = slab_t[slot]
            pe_tok = None
            for ocl in range(4):
                oc = 4 * s + ocl
                xprefetch(oc + 2)
                xslot, xtok = xload[oc]
                dtoks = []
                for half in range(2):
                    b = banks.acquire()
                    for k in range(32):
                        pe_tok = P.op('tensor', lambda e, b=b, k=k, ocl=ocl, half=half, sl=sl: e.matmul(
                            banks.banks[b][:, :], sl[:, k * 512 + ocl * 128:k * 512 + ocl * 128 + 128], act(k, half),
                            start=(k == 0), stop=(k == 31)), signal=(k == 31))
                    if pending is not None:
                        pending()
                        pending = None
                    sslot = stg.acquire('vector')
                    P.wait('vector', pe_tok, xtok)
                    tv = P.op('vector', lambda e, b=b, sslot=sslot, xslot=xslot, half=half: e.tensor_tensor(
                        out=stg_t[sslot][:, :], in0=banks.banks[b][:, :],
                        in1=xring.tiles[xslot][:, half * 512:half * 512 + 512], op=ALU.add))
                    banks.release(b, tv)
                    dtoks.append(tv)
                    si = sqr.next()
                    P.wait('scalar', tv, sqr.free[si])
                    ta = P.op('scalar', lambda e, si=si, sslot=sslot: e.activation(
                        out=sqr.tiles[si][:, 0:512], in_=stg_t[sslot][:, :], func=AF.Square))
                    x1tok[(oc, half)] = stg.store(sslot, x1d[oc * 128:(oc + 1) * 128, half * 512:half * 512 + 512], tv, extra=[ta])
                    acc_tok[(oc, half)] = P.dma('sync', acc_t[oc * 128:(oc + 1) * 128, half * 512:half * 512 + 512], stg.tiles[sslot], stg.sems[sslot])
                    stg.free[sslot].append(acc_tok[(oc, half)])

                    def mk(si=si, half=half, oc=oc, ta=ta):
                        def f():
                            nonlocal stat_tok
                            P.wait('tensor', ta)
                            stat_tok = P.op('tensor', lambda e: e.matmul(
                                st[half][:, :], ones[:, :], sqr.tiles[si][:, 0:512], start=(oc == 0), stop=(oc == 31)))
                            sqr.free[si] = stat_tok
                        return f
                    if oc == 0:
                        P.wait('tensor', st_read_tok)
                    pending = mk()
                xring.release(xslot, *dtoks)
            slabs.release(slot, pe_tok)
            ti += 1
        pending()
        pending = None
        st_read_tok, tv = rstd_from_stats(4096.0, stat_tok, last_norm_tok)

        P.wait('vector', tv, stat_tok)
        xload2 = {}

        def x1prefetch(k):
            if k < 32 and k not in xload2:
                P.wait('sync', x1tok[(k, 0)], x1tok[(k, 1)])
                xload2[k] = xring.load(x1d[k * 128:(k + 1) * 128, :])
        x1prefetch(0)
        x1prefetch(1)
        for k in range(32):
            x1prefetch(k + 2)
            xslot, xtok = xload2[k]
            P.wait('vector', xtok)
            last_norm_tok = P.op('vector', lambda e, k=k, xslot=xslot: e.scalar_tensor_tensor(
                out=act(k), in0=xring.tiles[xslot][:, :], scalar=gmlp[:, k:k + 1], in1=rstd[:, :], op0=ALU.mult, op1=ALU.mult))
            xring.release(xslot, last_norm_tok)

        P.wait('tensor', last_norm_tok)
        rlr = TileRing(rl_t)
        down_last_pe = None
        nev = 0
        for fb in range(8):
            u_last = None
            for s in range(4):
                prefetch(ti + 1)
                slot, stok = slab_loaded[ti]
                P.wait('tensor', stok)
                sl = slab_t[slot]
                pe_tok = None
                for fl in range(4):
                    ffc = 4 * s + fl
                    for half in range(2):
                        b = banks.acquire()
                        for k in range(32):
                            pe_tok = P.op('tensor', lambda e, b=b, k=k, fl=fl, half=half, sl=sl: e.matmul(
                                banks.banks[b][:, :], sl[:, k * 512 + fl * 128:k * 512 + fl * 128 + 128], act(k, half),
                                start=(k == 0), stop=(k == 31)), signal=(k == 31))
                        ri = rlr.next()
                        P.wait('scalar', pe_tok, rlr.free[ri])
                        ta = P.op('scalar', lambda e, b=b, ri=ri: e.activation(out=rl_t[ri][:, :], in_=banks.banks[b][:, :], func=AF.Relu))
                        banks.release(b, ta)
                        P.wait('vector', ta, down_last_pe)
                        u_last = P.op('vector', lambda e, ri=ri, ffc=ffc, half=half: e.tensor_tensor(
                            out=ubuf[:, ffc * T + half * 512:ffc * T + half * 512 + 512], in0=rl_t[ri][:, :], in1=rl_t[ri][:, :], op=ALU.mult))
                        rlr.free[ri] = u_last
                slabs.release(slot, pe_tok)
                ti += 1
            P.wait('tensor', u_last)
            for s in range(4):
                prefetch(ti + 1)
                slot, stok = slab_loaded[ti]
                P.wait('tensor', stok)
                sl = slab_t[slot]
                pe_tok = None
                for ol in range(8):
                    oc = 8 * s + ol
                    for half in range(2):
                        b = banks.acquire()
                        for k in range(16):
                            pe_tok = P.op('tensor', lambda e, b=b, k=k, ol=ol, half=half, sl=sl: e.matmul(
                                banks.banks[b][:, :], sl[:, k * 1024 + ol * 128:k * 1024 + ol * 128 + 128],
                                ubuf[:, k * T + half * 512:k * T + half * 512 + 512],
                                start=(k == 0), stop=(k == 15)), signal=(k == 15))
                        eng = 'scalar' if nev % 2 == 0 else 'vector'
                        nev += 1
                        sslot = stg.acquire(eng)
                        P.wait(eng, pe_tok)
                        if eng == 'scalar':
                            te = P.op('scalar', lambda e, b=b, sslot=sslot: e.activation(out=stg_t[sslot][:, :], in_=banks.banks[b][:, :], func=AF.Copy))
                        else:
                            te = P.op('vector', lambda e, b=b, sslot=sslot: e.tensor_copy(out=stg_t[sslot][:, :], in_=banks.banks[b][:, :]))
                        banks.release(b, te)
                        P.wait('gpsimd', te, acc_tok[(oc, half)])
                        acc_tok[(oc, half)] = P.dma('gpsimd', acc_t[oc * 128:(oc + 1) * 128, half * 512:half * 512 + 512],
                                                    stg.tiles[sslot], stg.sems[sslot], accum_op=ALU.add)
                        stg.free[sslot] = [acc_tok[(oc, half)]]
                down_last_pe = pe_tok
                slabs.release(slot, pe_tok)
                ti += 1

        P.wait('sync', stg.final_toks())
        osem = P.dsem("osem")
        if final:
            accr = LoadRing(P, tp[3:6], "accr")
            fin_stat = None
            for k in range(32):
                aslot, atok = accr.load(acc_t[k * 128:(k + 1) * 128, :])
                sslot = k % 3
                P.wait('scalar', atok, xring.free[sslot])
                ta = P.op('scalar', lambda e, aslot=aslot, sslot=sslot: e.activation(
                    out=xring.tiles[sslot][:, :], in_=accr.tiles[aslot][:, :], func=AF.Square))
                accr.release(aslot, ta)
                P.wait('tensor', ta)
                if k == 0:
                    P.wait('tensor', st_read_tok)
                for half in range(2):
                    fin_stat = P.op('tensor', lambda e, sslot=sslot, half=half, k=k: e.matmul(
                        st[half][:, :], ones[:, :], xring.tiles[sslot][:, half * 512:half * 512 + 512],
                        start=(k == 0), stop=(k == 31)), signal=(half == 1))
                xring.free[sslot] = [fin_stat]
            st_read_tok, tv = rstd_from_stats(4096.0, fin_stat, last_norm_tok)
            P.wait('vector', tv)
            for k in range(32):
                aslot, atok = accr.load(acc_t[k * 128:(k + 1) * 128, :])
                P.wait('vector', atok)
                tv2 = P.op('vector', lambda e, aslot=aslot, k=k: e.scalar_tensor_tensor(
                    out=accr.tiles[aslot][:, :], in0=accr.tiles[aslot][:, :], scalar=gfin[:, k:k + 1], in1=rstd[:, :],
                    op0=ALU.mult, op1=ALU.mult))
                P.wait('sync', tv2)
                tok = P.dma('sync', yT[k * 128:(k + 1) * 128, :], accr.tiles[aslot][:, :], osem)
                accr.release(aslot, tok)
        P.wait('sync', (osem.h, osem.count))
    P.end_phase()


INC = 10696
SC_A = 192.0 ** -0.5
SC_H = 128.0 ** -0.5

def emit_A(P, xT, w_in, gattd, gqd, gkvd, w_uq, w_uk, w_uv, bfd, cosAd, sinAd, cosBd, sinBd, RAd, RBd, onesd, qk, r64, v, lf, latd):
    P.begin_phase()
    if True:
        hT = P.sb("hT", [128, 32 * T], BF16)
        slab_t = [P.sb(f"slab{i}", [128, 16384], BF16) for i in range(2)]
        ckvn = P.sb("ckvn", [128, 4 * T], BF16)
        tp = [P.sb(f"tp{i}", [128, T], F32) for i in range(4)]
        rstd = P.sb("rstd", [128, T], F32)
        cosA = P.sb("cosA_sb", [64, T], F32); sinA = P.sb("sinA_sb", [64, T], F32)
        cosB = P.sb("cosB_sb", [128, T], F32); sinB = P.sb("sinB_sb", [128, T], F32)
        RA = P.sb("RA_sb", [64, 64], F32); RB = P.sb("RB_sb", [128, 128], F32)
        ones = P.sb("ones_sb", [128, 128], F32)
        gatt = P.sb("gatt_sb", [128, 32], F32); gq = P.sb("gq_sb", [128, 7], F32); gkv = P.sb("gkv_sb", [128, 4], F32)
        bfs = P.sb("bf_sb", [8, 1], F32); nbf = P.sb("nbf_sb", [8, 1], F32)
        sb16_t = [P.sb(f"sb16_{i}", [128, 512], BF16) for i in range(4)]
        sf32_t = [P.sb(f"sf32_{i}", [128, 512], F32) for i in range(6)]
        bank_t = P.bank_t
        st = bank_t[0:2]
        banks = Banks(P, bank_t[2:8])
        slabs = SlabRing(P, slab_t)
        csem = P.dsem("csem")
        for dst, src in [(ones, onesd), (gatt, gattd), (gq, gqd), (gkv, gkvd), (bfs, bfd), (cosA, cosAd), (sinA, sinAd),
                         (cosB, cosBd), (sinB, sinBd), (RA, RAd), (RB, RBd)]:
            ctok = P.dma('sync', dst[:, :], src, csem)
        for e in ['scalar', 'vector', 'tensor']:
            P.wait(e, ctok)
        tnb = P.op('vector', lambda e: e.tensor_scalar(out=nbf[:, :], in0=bfs[:, :], scalar1=-1.0, scalar2=None, op0=ALU.mult))

        def hch(k, a=0, n=T):
            return hT[:, k * T + a:k * T + a + n]

        xring = LoadRing(P, tp[0:2], "xring")
        sqr = TileRing(tp[2:4])
        s16 = StoreRing(P, sb16_t, "s16")
        s32 = StoreRing(P, sf32_t[0:2], "s32")
        tfr = TileRing(sf32_t[2:4])
        abr = TileRing(sf32_t[4:6])

        tasks = [('lat', 0, 512, 0), ('lat', 512, 384, 4), ('lat', 896, 512, 7), ('kr', 1408, 64, 11)]
        for mi, mname in enumerate('BCD'):
            base = 1472 + 3072 * mi
            for part in range(2):
                tasks.append(('q' + mname, base + 512 * part, 512, 16 + 16 * mi + 4 * part))
            for part in range(2):
                tasks.append(('k' + mname, base + 1024 + 512 * part, 512, 24 + 16 * mi + 4 * part))
            for part in range(2):
                tasks.append(('v' + mname, base + 2048 + 512 * part, 512, 1024 * (mi + 1) + 512 * part))
        tasks.append(('pf', 10688, 8, 0))
        loaded = {}

        def prefetch(i):
            if i < len(tasks) and i not in loaded:
                _, c0, cw, _ = tasks[i]
                loaded[i] = slabs.load(w_in, 0, 32, c0, cw)
        prefetch(0)

        def rstd_from_stats(n_feat, after_tok, war_toks):
            P.wait('scalar', after_tok, war_toks)
            ta = None
            for half in range(2):
                ta = P.op('scalar', lambda e, half=half: e.activation(
                    out=rstd[:, half * 512:half * 512 + 512], in_=st[half][:, :], func=AF.Sqrt,
                    scale=1.0 / n_feat, bias=EPS))
            P.wait('vector', ta)
            tv = P.op('vector', lambda e: e.reciprocal(out=rstd[:, :], in_=rstd[:, :]))
            return ta, tv

        pe_tok = None
        for k in range(32):
            xs, xtok = xring.load(xT[k * 128:(k + 1) * 128, :])
            si = sqr.next()
            P.wait('scalar', xtok, sqr.free[si])
            ta = P.op('scalar', lambda e, xs=xs, si=si: e.activation(out=sqr.tiles[si][:, :], in_=xring.tiles[xs], func=AF.Square))
            xring.release(xs, ta)
            P.wait('tensor', ta)
            for half in range(2):
                pe_tok = P.op('tensor', lambda e, si=si, half=half, k=k: e.matmul(
                    st[half][:, :], ones[:, :], sqr.tiles[si][:, half * 512:half * 512 + 512],
                    start=(k == 0), stop=(k == 31)), signal=(half == 1))
            sqr.free[si] = pe_tok
        st_read_tok, tv = rstd_from_stats(4096.0, pe_tok, None)
        P.wait('vector', tv)
        last_norm = None
        for k in range(32):
            xs, xtok = xring.load(xT[k * 128:(k + 1) * 128, :])
            P.wait('vector', xtok)
            last_norm = P.op('vector', lambda e, xs=xs, k=k: e.scalar_tensor_tensor(
                out=hch(k), in0=xring.tiles[xs], scalar=gatt[:, k:k + 1], in1=rstd[:, :], op0=ALU.mult, op1=ALU.mult))
            xring.release(xs, last_norm)
        P.wait('tensor', last_norm)

        nev = [0]

        def evac_copy_store(b, npart, dst, pe_tok, scale=None, f32=False):
            ring = s32 if f32 else s16
            eng = 'scalar' if nev[0] % 2 == 0 else 'vector'
            nev[0] += 1
            slot = ring.acquire(eng)
            P.wait(eng, pe_tok)
            o = ring.tiles[slot][0:npart, :]
            i = banks.banks[b][0:npart, :]
            if eng == 'scalar':
                te = P.op('scalar', lambda e: e.activation(out=o, in_=i, func=AF.Copy, scale=(1.0 if scale is None else scale)))
            else:
                if scale is None:
                    te = P.op('vector', lambda e: e.tensor_copy(out=o, in_=i))
                else:
                    te = P.op('vector', lambda e: e.tensor_scalar(out=o, in0=i, scalar1=scale, scalar2=None, op0=ALU.mult))
            banks.release(b, te)
            ring.store(slot, dst, te, src=o)
            return te

        def rope_store(src_ap, src_tok, src_is_psum_bank, npart, Rm, cosT, sinT, half, scale, dst):
            ti_ = tfr.next()
            tft = tfr.tiles[ti_][0:npart, :]
            if src_is_psum_bank is not None:
                P.wait('scalar', src_tok, tfr.free[ti_])
                tcp = P.op('scalar', lambda e: e.activation(out=tft, in_=src_ap, func=AF.Copy, scale=scale))
                banks.release(src_is_psum_bank, tcp)
            else:
                P.wait('scalar', src_tok, tfr.free[ti_])
                tcp = P.op('scalar', lambda e: e.activation(out=tft, in_=src_ap, func=AF.Copy, scale=scale))
            b2 = banks.acquire()
            P.wait('tensor', tcp)
            trot = P.op('tensor', lambda e: e.matmul(banks.banks[b2][0:npart, :], Rm[:, :], tft, start=True, stop=True))
            ai = abr.next()
            bi = abr.next()
            at = abr.tiles[ai][0:npart, :]
            bt = abr.tiles[bi][0:npart, :]
            cs = cosT[0:npart, half * 512:half * 512 + 512]
            sn = sinT[0:npart, half * 512:half * 512 + 512]
            P.wait('vector', tcp, abr.free[ai])
            ta_ = P.op('vector', lambda e: e.tensor_tensor(out=at, in0=tft, in1=cs, op=ALU.mult))
            P.wait('vector', trot, abr.free[bi])
            tb_ = P.op('vector', lambda e: e.tensor_tensor(out=bt, in0=banks.banks[b2][0:npart, :], in1=sn, op=ALU.mult))
            banks.release(b2, tb_)
            slot = s16.acquire('vector')
            o = s16.tiles[slot][0:npart, :]
            P.wait('vector', ta_, tb_)
            to = P.op('vector', lambda e: e.tensor_tensor(out=o, in0=at, in1=bt, op=ALU.add))
            tfr.free[ti_] = [trot, ta_]
            abr.free[ai] = to
            abr.free[bi] = to
            s16.store(slot, dst, to, src=o)

        lat_tok = {}
        for ti, (kind, c0, cw, dsti) in enumerate(tasks):
            prefetch(ti + 1)
            slot, stok = loaded[ti]
            P.wait('tensor', stok)
            sl = slab_t[slot]
            pe_tok = None
            if kind[0] == 'v':
                for tt in range(8):
                    b = banks.acquire()
                    for k in range(32):
                        pe_tok = P.op('tensor', lambda e, b=b, k=k, tt=tt, sl=sl: e.matmul(
                            banks.banks[b][:, :], hch(k, tt * 128, 128), sl[:, k * 512:k * 512 + 512],
                            start=(k == 0), stop=(k == 31)), signal=(k == 31))
                    evac_copy_store(b, 128, v[tt * 128:(tt + 1) * 128, dsti:dsti + 512], pe_tok)
            elif kind == 'pf':
                for half in range(2):
                    b = banks.acquire()
                    for k in range(32):
                        pe_tok = P.op('tensor', lambda e, b=b, k=k, half=half, sl=sl: e.matmul(
                            banks.banks[b][0:8, :], sl[:, k * 8:k * 8 + 8], hch(k, half * 512, 512),
                            start=(k == 0), stop=(k == 31)), signal=(k == 31))
                    ti_ = tfr.next()
                    e1 = tfr.tiles[ti_][0:8, :]
                    P.wait('scalar', pe_tok, tfr.free[ti_], tnb)
                    t1 = P.op('scalar', lambda e, b=b, e1=e1: e.activation(out=e1, in_=banks.banks[b][0:8, :], func=AF.Exp, scale=-1.0, bias=nbf[:, 0:1]))
                    banks.release(b, t1)
                    P.wait('scalar', t1)
                    t2 = P.op('scalar', lambda e, e1=e1: e.activation(out=e1, in_=e1, func=AF.Ln, scale=1.0, bias=1.0))
                    slot2 = s32.acquire('vector')
                    o = s32.tiles[slot2][0:8, :]
                    P.wait('vector', t2)
                    t3 = P.op('vector', lambda e, o=o, e1=e1: e.tensor_scalar(out=o, in0=e1, scalar1=-1.0, scalar2=None, op0=ALU.mult))
                    tfr.free[ti_] = t3
                    s32.store(slot2, lf[:, half * 512:half * 512 + 512], t3, src=o)
            else:
                ntile = (cw + 127) // 128
                for tl in range(ntile):
                    m = min(128, cw - tl * 128)
                    for half in range(2):
                        b = banks.acquire()
                        for k in range(32):
                            pe_tok = P.op('tensor', lambda e, b=b, k=k, tl=tl, half=half, sl=sl, m=m, cw=cw: e.matmul(
                                banks.banks[b][0:m, :], sl[:, k * cw + tl * 128:k * cw + tl * 128 + m], hch(k, half * 512, 512),
                                start=(k == 0), stop=(k == 31)), signal=(k == 31))
                        hs = slice(half * 512, half * 512 + 512)
                        if kind in ('lat', 'kr'):
                            evac_copy_store(b, m, latd[dsti + tl, 0:m, hs], pe_tok, f32=True)
                        elif kind in ('qB', 'kB'):
                            rope_store(banks.banks[b][:, :], pe_tok, b, 128, RB, cosB, sinB, half,
                                       SC_H if kind == 'qB' else 1.0, qk[dsti + tl, :, hs])
                        else:
                            evac_copy_store(b, 128, qk[dsti + tl, :, hs], pe_tok, scale=(SC_H if kind[0] == 'q' else None))
            slabs.release(slot, pe_tok)
        last_inproj_pe = pe_tok

        P.wait('gpsimd', last_inproj_pe)
        wsem = P.dsem("wsem")
        wq = slab_t[0]
        wkv = slab_t[1]
        P.dma('gpsimd', wq[:, 0:7 * 1536].rearrange("p (k c) -> p k c", c=1536), w_uq.rearrange("(k p) c -> p k c", p=128), wsem)
        P.dma('gpsimd', wkv[:, 0:4096].rearrange("p (k c) -> p k c", c=1024), w_uk.rearrange("(k p) c -> p k c", p=128), wsem)
        wtok = P.dma('gpsimd', wkv[:, 4096:8192].rearrange("p (k c) -> p k c", c=1024), w_uv.rearrange("(k p) c -> p k c", p=128), wsem)
        P.wait('sync', s32.final_toks(), last_inproj_pe)
        lat = [hT[:, 2 * j * T:2 * (j + 1) * T].bitcast(F32) for j in range(12)]
        cqn = hT[:, 24 * T:31 * T]
        lsem = P.dsem("lsem")
        ltok = []
        for j in range(12):
            npart = 64 if j == 11 else 128
            ltok.append(P.dma('sync', lat[j][0:npart, :], latd[j, 0:npart, :], lsem))
        ltok_all = ltok[-1]

        def lat_norm(j0, n, gv, dst_fn, nfeat, war):
            pe = None
            for j in range(n):
                si = sqr.next()
                P.wait('scalar', ltok_all, sqr.free[si])
                ta = P.op('scalar', lambda e, j=j, si=si: e.activation(out=sqr.tiles[si][:, :], in_=lat[j0 + j], func=AF.Square))
                P.wait('tensor', ta)
                if j == 0:
                    P.wait('tensor', war[0])
                for half in range(2):
                    pe = P.op('tensor', lambda e, si=si, half=half, j=j: e.matmul(
                        st[half][:, :], ones[:, :], sqr.tiles[si][:, half * 512:half * 512 + 512],
                        start=(j == 0), stop=(j == n - 1)), signal=(half == 1))
                sqr.free[si] = pe
            sr, tv = rstd_from_stats(nfeat, pe, war[1])
            P.wait('vector', tv)
            tn = None
            for j in range(n):
                tn = P.op('vector', lambda e, j=j: e.scalar_tensor_tensor(
                    out=dst_fn(j), in0=lat[j0 + j], scalar=gv[:, j:j + 1], in1=rstd[:, :], op0=ALU.mult, op1=ALU.mult))
            return sr, tn

        sr, tn_q = lat_norm(0, 7, gq, lambda j: cqn[:, j * T:(j + 1) * T], 896.0, (st_read_tok, last_norm))
        sr, tn_kv = lat_norm(7, 4, gkv, lambda j: ckvn[:, j * T:(j + 1) * T], 512.0, (sr, tn_q))
        P.wait('tensor', wtok, tn_q, tn_kv)
        for h in range(8):
            for half in range(2):
                hs = slice(half * 512, half * 512 + 512)
                b = banks.acquire()
                for k in range(7):
                    pe_tok = P.op('tensor', lambda e, b=b, k=k, h=h, half=half: e.matmul(
                        banks.banks[b][:, :], wq[:, k * 1536 + 192 * h:k * 1536 + 192 * h + 128],
                        cqn[:, k * T + half * 512:k * T + half * 512 + 512], start=(k == 0), stop=(k == 6)), signal=(k == 6))
                evac_copy_store(b, 128, qk[h, :, hs], pe_tok, scale=SC_A)
                b = banks.acquire()
                for k in range(7):
                    pe_tok = P.op('tensor', lambda e, b=b, k=k, h=h, half=half: e.matmul(
                        banks.banks[b][0:64, :], wq[:, k * 1536 + 192 * h + 128:k * 1536 + 192 * h + 192],
                        cqn[:, k * T + half * 512:k * T + half * 512 + 512], start=(k == 0), stop=(k == 6)), signal=(k == 6))
                rope_store(banks.banks[b][0:64, :], pe_tok, b, 64, RA, cosA, sinA, half, SC_A, r64[h, :, hs])
        for h in range(8):
            for half in range(2):
                hs = slice(half * 512, half * 512 + 512)
                b = banks.acquire()
                for k in range(4):
                    pe_tok = P.op('tensor', lambda e, b=b, k=k, h=h, half=half: e.matmul(
                        banks.banks[b][:, :], wkv[:, k * 1024 + 128 * h:k * 1024 + 128 * h + 128],
                        ckvn[:, k * T + half * 512:k * T + half * 512 + 512], start=(k == 0), stop=(k == 3)), signal=(k == 3))
                evac_copy_store(b, 128, qk[8 + h, :, hs], pe_tok)
        for tt in range(8):
            for vs in range(2):
                b = banks.acquire()
                for k in range(4):
                    pe_tok = P.op('tensor', lambda e, b=b, k=k, tt=tt, vs=vs: e.matmul(
                        banks.banks[b][:, :], ckvn[:, k * T + tt * 128:k * T + tt * 128 + 128],
                        wkv[:, 4096 + k * 1024 + 512 * vs:4096 + k * 1024 + 512 * vs + 512], start=(k == 0), stop=(k == 3)), signal=(k == 3))
                evac_copy_store(b, 128, v[tt * 128:(tt + 1) * 128, 512 * vs:512 * vs + 512], pe_tok)
        for half in range(2):
            hs = slice(half * 512, half * 512 + 512)
            rope_store(lat[11][0:64, hs], ltok_all, None, 64, RA, cosA, sinA, half, 1.0, r64[8, :, hs])
        P.wait('sync', s16.final_toks(), s32.final_toks())
    P.end_phase()


S = 8192
NEG = -30000.0

def emit_B(P, h, qkd, r64d, vd, lfd, cbd, sud, trid, od, mixers="ABCD", qchunks=None, cond_core=None):
    P.begin_phase()
    if True:
        bufQ = P.sb("bufQ", [128, S], BF16)
        bufK = P.sb("bufK", [128, S], BF16)
        bufV = P.sb("bufV", [128, 64 * 128], BF16)
        bufX1 = P.sb("bufX1", [128, S], BF16)
        bufX2 = P.sb("bufX2", [128, S], BF16)
        Oacc = P.sb("Oacc", [128, S], F32)
        Lacc = P.sb("Lacc", [128, S], F32)
        p_t = [P.sb(f"pt{i}", [128, 512], BF16) for i in range(4)]
        p2_t = [P.sb(f"p2t{i}", [128, 512], BF16) for i in range(3)]
        f_t = [P.sb(f"ft{i}", [128, 512], F32) for i in range(8)]
        cb = P.sb("cb_sb", [128, 7 * 128], BF16)
        su = P.sb("su_sb", [64, 65], BF16); tri = P.sb("tri_sb", [128, 128], BF16)
        lfr_s = P.sb("lfr_s", [64, 128], F32); lft_s = P.sb("lft_s", [128, 64], F32)
        nbt = P.sb("nbt", [128, 16 * 64], F32)
        ccol = P.sb("ccol", [128, 64], F32); offs = P.sb("offs", [128, 65], F32); p1s = P.sb("p1s", [128, 65], F32)
        spl = [P.sb(f"spl{i}", [128, 128], BF16) for i in range(9)]
        rsd = [P.sb(f"rsd{i}", [128, 128], F32) for i in range(2)]
        dm_t = [P.sb(f"dm{i}", [128, 512], BF16) for i in range(2)]
        bank = P.bank_t
        ident = cb[:, 0:128]; onesb = cb[:, 128:256]; nmU = cb[:, 256:384]; nmL = cb[:, 384:512]
        nmS = cb[:, 512:640]; nTinc = cb[:, 640:768]; zerosb = cb[:, 768:896]
        csem = P.dsem("csem")
        P.dma('sync', cb[:, :], cbd, csem)
        P.dma('gpsimd', su[:, :], sud, csem)
        P.dma('gpsimd', tri[:, :], trid, csem)
        for c_ in range(8):
            P.dma('sync', lfr_s[8 * c_:8 * c_ + 8, :], lfd[c_, h, :].rearrange("(b i) -> b i", i=128), csem)
            ctok = P.dma('sync', lft_s[:, 8 * c_:8 * c_ + 8], lfd[c_, h, :].rearrange("(b i) -> i b", i=128), csem, allow_slow_non_contiguous=True)
        for e in ['scalar', 'vector', 'tensor']:
            P.wait(e, ctok)
        lsem = P.dsem("lsem")
        ost = StoreRing(P, f_t[0:2], "ost")

        def barrier():
            for e in ['scalar', 'vector', 'tensor', 'sync', 'gpsimd']:
                for o in ['scalar', 'vector', 'tensor', 'gpsimd']:
                    if o != e and P.pcnt[o] > 0:
                        P.wait(e, (P.psem[o], P.pcnt[o]))
            for e in ['scalar', 'vector', 'tensor']:
                P.wait(e, ost.final_toks())

        def load(pairs):
            tok = None
            for dst, src in pairs:
                tok = P.dma('sync', dst, src, lsem)
            for e in ['scalar', 'vector', 'tensor']:
                P.wait(e, tok)

        def vpairs(buf, mi):
            c0 = mi * 1024 + h * 128
            prs = []
            for c in range(8):
                prs.append((buf[:, c * 1024:(c + 1) * 1024].rearrange("p (b d) -> p b d", d=128),
                            vd[c, :, c0:c0 + 128].rearrange("(b p) d -> p b d", p=128)))
            return prs

        def vdil(buf, dl):
            c0 = 1024 + h * 128
            nbk = S // dl // 128
            src = vd.rearrange("c t f -> (c t) f")[:, c0:c0 + 128].rearrange("(n i r) d -> r i n d", r=dl, i=128)
            prs = []
            for r in range(dl):
                prs.append((buf[:, r * nbk * 128:(r + 1) * nbk * 128].rearrange("p (n d) -> p n d", d=128), src[r]))
            return prs

        def cols(qc, kb):
            j = kb - 4 * qc
            c0 = 0 if j < 0 else 128 * j
            return j, c0

        def run_AD(mi, is_A, nb_fn=None, dm_fn=None):
            sb = Banks(P, bank[0:3]); ob = Banks(P, bank[3:5]); lb = Banks(P, bank[5:7])
            pr = TileRing(p_t)
            steps = [(qc, kb) for qc in (qchunks if qchunks is not None else range(16)) for kb in range(4 * qc + 4)]
            LA = 2
            st_ = {}
            cur = {}

            def qk(i):
                qc, kb = steps[i]
                j, c0 = cols(qc, kb)
                b = sb.acquire()
                qs = slice(qc * 512 + c0, qc * 512 + 512)
                ks = slice(kb * 128, kb * 128 + 128)
                extra = []
                if is_A:
                    extra.append((bufX2[0:64, ks], bufX1[0:64, qs], slice(c0, 512)))
                else:
                    dmi = dm_fn(qc)
                    extra.append((onesb, dm_t[dmi][:, c0:512], slice(c0, 512)))
                if j >= 0:
                    extra.append((ident, nmU, slice(c0, c0 + 128)))
                P.op('tensor', lambda e: e.matmul(bank[b][:, c0:512], bufK[:, ks], bufQ[:, qs], start=True, stop=False), signal=False)
                t = None
                for n_, (l_, r_, cs_) in enumerate(extra):
                    last = n_ == len(extra) - 1
                    t = P.op('tensor', lambda e, l_=l_, r_=r_, cs_=cs_, last=last: e.matmul(bank[b][:, cs_], l_, r_, start=False, stop=last), signal=last)
                pi = pr.next()
                P.wait('scalar', t, pr.free[pi])
                if is_A:
                    te = P.op('scalar', lambda e: e.activation(out=p_t[pi][:, c0:512], in_=bank[b][:, c0:512], func=AF.Exp))
                else:
                    bias = nb_fn(qc, kb)
                    te = P.op('scalar', lambda e: e.activation(out=p_t[pi][:, c0:512], in_=bank[b][:, c0:512], func=AF.Exp, bias=bias, scale=1.0))
                sb.release(b, te)
                st_[i] = (pi, te)

            def pv(i):
                qc, kb = steps[i]
                j, c0 = cols(qc, kb)
                pi, te = st_.pop(i)
                first = kb == 0
                last = kb == 4 * qc + 3
                if first:
                    cur['o'] = ob.acquire(); cur['l'] = lb.acquire()
                o_, l_ = cur['o'], cur['l']
                P.wait('tensor', te)
                P.op('tensor', lambda e: e.matmul(bank[3 + o_][:, c0:512], bufV[:, kb * 128:kb * 128 + 128], p_t[pi][:, c0:512], start=first, stop=last), signal=False)
                t = P.op('tensor', lambda e: e.matmul(bank[5 + l_][:, c0:512], onesb, p_t[pi][:, c0:512], start=first, stop=last))
                pr.free[pi] = t
                if last:
                    P.wait('vector', t, cur.get('rlw'))
                    t1 = P.op('vector', lambda e: e.reciprocal(out=f_t[2][:, :], in_=bank[5 + l_][:, :]))
                    lb.release(l_, t1)
                    slot = ost.acquire('vector')
                    P.wait('vector', t1)
                    t2 = P.op('vector', lambda e: e.tensor_tensor(out=ost.tiles[slot], in0=bank[3 + o_][:, :], in1=f_t[2][:, :], op=ALU.mult))
                    ob.release(o_, t2)
                    cur['rlw'] = t2
                    ost.store(slot, od[mi, h, :, qc * 512:qc * 512 + 512], t2)

            for i in range(len(steps) + LA):
                if i < len(steps):
                    qk(i)
                if i >= LA:
                    pv(i - LA)

        if 'A' in mixers:
            load([(bufQ[:, :].rearrange("p (c t) -> p c t", c=8), qkd[:, h].rearrange("c p t -> p c t")), (bufK[:, :].rearrange("p (c t) -> p c t", c=8), qkd[:, 8 + h].rearrange("c p t -> p c t")),
                  (bufX1[0:64, :].rearrange("p (c t) -> p c t", c=8), r64d[:, h].rearrange("c p t -> p c t")), (bufX2[0:64, :].rearrange("p (c t) -> p c t", c=8), r64d[:, 8].rearrange("c p t -> p c t"))]
                 + vpairs(bufV, 0))
            run_AD(0, True)
            barrier()

        if 'D' in mixers:
            load([(bufQ[:, :].rearrange("p (c t) -> p c t", c=8), qkd[:, 48 + h].rearrange("c p t -> p c t")), (bufK[:, :].rearrange("p (c t) -> p c t", c=8), qkd[:, 56 + h].rearrange("c p t -> p c t"))] + vpairs(bufV, 3))
            def split3(src, npart, ncol, base):
                hi = spl[base][0:npart, 0:ncol]; mid = spl[base + 1][0:npart, 0:ncol]; lo = spl[base + 2][0:npart, 0:ncol]
                r1 = rsd[0][0:npart, 0:ncol]; r2 = rsd[1][0:npart, 0:ncol]
                t = P.op('vector', lambda e: e.tensor_copy(out=hi, in_=src))
                P.wait('vector', t)
                t = P.op('vector', lambda e: e.tensor_tensor(out=r1, in0=src, in1=hi, op=ALU.subtract))
                P.wait('vector', t)
                t = P.op('vector', lambda e: e.tensor_copy(out=mid, in_=r1))
                P.wait('vector', t)
                t = P.op('vector', lambda e: e.tensor_tensor(out=r2, in0=r1, in1=mid, op=ALU.subtract))
                P.wait('vector', t)
                t = P.op('vector', lambda e: e.tensor_copy(out=lo, in_=r2))
                return [hi, mid, lo], t
            lfr3, t_a = split3(lfr_s[:, :], 64, 128, 0)
            P.wait('vector', t_a)
            lft3, t_b = split3(lft_s[:, :], 128, 64, 3)
            P.wait('tensor', t_a, t_b)
            for n_, a_ in enumerate(lfr3):
                t = P.op('tensor', lambda e, a_=a_, n_=n_: e.matmul(bank[0][:, 0:65], a_, su[:, :], start=(n_ == 0), stop=(n_ == 2)), signal=(n_ == 2))
            P.wait('vector', t, t_b)
            t = P.op('vector', lambda e: e.tensor_copy(out=p1s[:, :], in_=bank[0][:, 0:65]))
            P.wait('vector', t)
            p13, t_c = split3(p1s[:, :], 128, 65, 6)
            P.wait('tensor', t_c)
            for n_, a_ in enumerate(p13):
                t = P.op('tensor', lambda e, a_=a_, n_=n_: e.matmul(bank[1][:, 0:65], onesb, a_, start=(n_ == 0), stop=(n_ == 2)), signal=(n_ == 2))
            for n_, a_ in enumerate(p13):
                P.op('tensor', lambda e, a_=a_, n_=n_: e.matmul(bank[2][:, 0:64], onesb, a_[:, 0:64], start=(n_ == 0), stop=False), signal=False)
            for n_, a_ in enumerate(lft3):
                t = P.op('tensor', lambda e, a_=a_, n_=n_: e.matmul(bank[2][:, 0:64], tri[:, :], a_, start=False, stop=(n_ == 2)), signal=(n_ == 2))
            P.wait('vector', t)
            P.op('vector', lambda e: e.tensor_copy(out=offs[:, :], in_=bank[1][:, 0:65]))
            t = P.op('vector', lambda e: e.tensor_copy(out=ccol[:, :], in_=bank[2][:, 0:64]))
            P.wait('vector', t)
            for qc in range(16):
                t = P.op('vector', lambda e, qc=qc: e.tensor_scalar(out=nbt[:, qc * 64:(qc + 1) * 64], in0=ccol[:, :], scalar1=-1.0,
                                                                    scalar2=offs[:, 4 * qc + 4:4 * qc + 5], op0=ALU.mult, op1=ALU.add))
            P.wait('scalar', t)
            P.wait('tensor', t)
            dmr = TileRing(dm_t)
            dm_state = {}

            def dm_fn(qc):
                if qc not in dm_state:
                    di = dmr.next()
                    P.wait('vector', dmr.free[di], t)
                    tt = None
                    for j in range(4):
                        idx = qc * 64 + 4 * qc + j
                        tt = P.op('vector', lambda e, j=j, idx=idx, di=di: e.tensor_scalar(
                            out=dm_t[di][:, 128 * j:128 * j + 128], in0=ident, scalar1=nbt[:, idx:idx + 1], scalar2=-1.0, op0=ALU.mult, op1=ALU.mult))
                    P.wait('tensor', tt)
                    dm_state[qc] = di
                    if qc >= 1:
                        pass
                return dm_state[qc]
            orig_dm_fn = dm_fn
            snap = {}

            def dm_fn2(qc):
                if qc not in dm_state and dm_state:
                    prev = dm_state[max(dm_state)]
                    dmr.free[prev] = (P.psem['tensor'], P.pcnt['tensor'])
                return orig_dm_fn(qc)

            run_AD(3, False, nb_fn=lambda qc, kb: nbt[:, qc * 64 + kb:qc * 64 + kb + 1], dm_fn=dm_fn2)
            barrier()

        if 'C' in mixers:
            load([(bufQ[:, :].rearrange("p (c t) -> p c t", c=8), qkd[:, 32 + h].rearrange("c p t -> p c t")), (bufK[:, :].rearrange("p (c t) -> p c t", c=8), qkd[:, 40 + h].rearrange("c p t -> p c t"))] + vpairs(bufV, 2))
            zb = Banks(P, bank[0:4]); tb = Banks(P, bank[4:6]); ob = Banks(P, bank[6:8])
            et = TileRing(f_t[3:5]); at = TileRing(f_t[5:7]); Rb = f_t[7]
            spr = TileRing(p2_t); pr = TileRing(p_t)
            steps = [(qc, kb) for qc in (qchunks if qchunks is not None else range(16)) for kb in range(4 * qc + 3, -1, -1)]
            sa = {}; sbb = {}; sc = {}
            cur = {}
            rb_tok = [None]

            def stA(i):
                qc, kb = steps[i]
                j, c0 = cols(qc, kb)
                b = zb.acquire()
                qs = slice(qc * 512 + c0, qc * 512 + 512)
                ks = slice(kb * 128, kb * 128 + 128)
                if j >= 0:
                    P.op('tensor', lambda e: e.matmul(bank[b][:, c0:512], bufK[:, ks], bufQ[:, qs], start=True, stop=False), signal=False)
                    t = P.op('tensor', lambda e: e.matmul(bank[b][:, c0:c0 + 128], ident, nmS, start=False, stop=True))
                else:
                    t = P.op('tensor', lambda e: e.matmul(bank[b][:, c0:512], bufK[:, ks], bufQ[:, qs], start=True, stop=True))
                sa[i] = (b, t)

            def stB(i):
                qc, kb = steps[i]
                j, c0 = cols(qc, kb)
                b, t = sa.pop(i)
                ei = et.next()
                P.wait('scalar', t, et.free[ei])
                t1 = P.op('scalar', lambda e: e.activation(out=et.tiles[ei][:, c0:512], in_=bank[b][:, c0:512], func=AF.Exp))
                si = spr.next()
                P.wait('scalar', t1, spr.free[si])
                t2 = P.op('scalar', lambda e: e.activation(out=p2_t[si][:, c0:512], in_=et.tiles[ei][:, c0:512], func=AF.Ln, bias=1.0, scale=1.0))
                et.free[ei] = t2
                P.wait('tensor', t2)
                t3 = P.op('tensor', lambda e: e.matmul(bank[b][:, c0:512], nTinc, p2_t[si][:, c0:512], start=False, stop=True))
                tbk = tb.acquire()
                t4 = P.op('tensor', lambda e: e.matmul(bank[4 + tbk][:, c0:512], onesb, p2_t[si][:, c0:512], start=True, stop=True))
                spr.free[si] = t4
                sbb[i] = (b, tbk, t3, t4)

            def stC(i):
                qc, kb = steps[i]
                j, c0 = cols(qc, kb)
                b, tbk, t3, t4 = sbb.pop(i)
                first = kb == 4 * qc + 3
                pi = pr.next()
                if first:
                    P.wait('scalar', t3, pr.free[pi])
                    t6 = P.op('scalar', lambda e: e.activation(out=p_t[pi][:, c0:512], in_=bank[b][:, c0:512], func=AF.Exp))
                    zb.release(b, t6)
                    P.wait('vector', t4, rb_tok[0])
                    tm = P.op('vector', lambda e: e.memset(Rb[:, :], 0.0))
                    P.wait('vector', tm)
                    t7 = P.op('vector', lambda e: e.tensor_copy(out=Rb[:, c0:512], in_=bank[4 + tbk][:, c0:512]))
                    rb_tok[0] = t7
                    tb.release(tbk, t7)
                else:
                    ai = at.next()
                    P.wait('vector', t3, at.free[ai], rb_tok[0])
                    t5 = P.op('vector', lambda e: e.tensor_tensor(out=at.tiles[ai][:, c0:512], in0=bank[b][:, c0:512], in1=Rb[:, c0:512], op=ALU.subtract))
                    zb.release(b, t5)
                    P.wait('vector', t4, t5)
                    t7 = P.op('vector', lambda e: e.tensor_tensor(out=Rb[:, c0:512], in0=Rb[:, c0:512], in1=bank[4 + tbk][:, c0:512], op=ALU.add))
                    rb_tok[0] = t7
                    tb.release(tbk, t7)
                    P.wait('scalar', t5, pr.free[pi])
                    t6 = P.op('scalar', lambda e: e.activation(out=p_t[pi][:, c0:512], in_=at.tiles[ai][:, c0:512], func=AF.Exp))
                    at.free[ai] = t6
                sc[i] = (pi, t6)

            def stD(i):
                qc, kb = steps[i]
                j, c0 = cols(qc, kb)
                pi, t6 = sc.pop(i)
                first = kb == 4 * qc + 3
                last = kb == 0
                if first:
                    cur['o'] = ob.acquire()
                    o_ = cur['o']
                    P.op('tensor', lambda e: e.matmul(bank[6 + o_][:, :], zerosb, bufQ[:, 0:512], start=True, stop=False), signal=False)
                o_ = cur['o']
                P.wait('tensor', t6)
                t = P.op('tensor', lambda e: e.matmul(bank[6 + o_][:, c0:512], bufV[:, kb * 128:kb * 128 + 128], p_t[pi][:, c0:512], start=False, stop=last))
                pr.free[pi] = t
                if last:
                    slot = ost.acquire('vector')
                    P.wait('vector', t)
                    t2 = P.op('vector', lambda e: e.tensor_copy(out=ost.tiles[slot], in_=bank[6 + o_][:, :]))
                    ob.release(o_, t2)
                    ost.store(slot, od[2, h, :, qc * 512:qc * 512 + 512], t2)

            ns = len(steps)
            for i in range(ns + 3):
                if i < ns:
                    stA(i)
                if 0 <= i - 1 < ns:
                    stB(i - 1)
                if 0 <= i - 2 < ns:
                    stC(i - 2)
                if 0 <= i - 3 < ns:
                    stD(i - 3)
            barrier()

        if 'B' in mixers:
            load([(bufQ[:, :].rearrange("p (c t) -> p c t", c=8), qkd[:, 16 + h].rearrange("c p t -> p c t")), (bufK[:, :].rearrange("p (c t) -> p c t", c=8), qkd[:, 24 + h].rearrange("c p t -> p c t"))] + vpairs(bufV, 1) + vdil(bufX1, 4) + vdil(bufX2, 16))
            ss = Banks(P, bank[0:2]); spv = Banks(P, bank[2:4]); ob = Banks(P, bank[4:6]); lb = Banks(P, bank[6:8])
            prs = TileRing(p_t[0:2]); prp = TileRing(p_t[2:4])
            groups = []
            for dl, vb, nbk in [(1, bufV, 64), (4, bufX1, 16), (16, bufX2, 4)]:
                for r in range(dl):
                    for n0 in range(0, nbk, 4):
                        groups.append((dl, vb, nbk, r, n0))
            st_ = {}
            acc_tok = [None]

            def sub(buf, dl, r, n):
                s0 = r + dl * 128 * n
                return buf[:, s0:s0 + dl * 127 + 1:dl]

            def g1(i):
                dl, vb, nbk, r, n0 = groups[i]
                bs = ss.acquire(); bp = spv.acquire()
                t = None
                for j in range(4):
                    n = n0 + j
                    P.op('tensor', lambda e, j=j, n=n: e.matmul(bank[bs][:, 128 * j:128 * j + 128], sub(bufK, dl, r, n), sub(bufQ, dl, r, n), start=True, stop=False), signal=False)
                    t = P.op('tensor', lambda e, j=j: e.matmul(bank[bs][:, 128 * j:128 * j + 128], ident, nmU, start=False, stop=True), signal=(j == 3))
                tp_ = None
                for j in range(4):
                    n = n0 + j
                    if n == 0:
                        continue
                    P.op('tensor', lambda e, j=j, n=n: e.matmul(bank[2 + bp][:, 128 * j:128 * j + 128], sub(bufK, dl, r, n - 1), sub(bufQ, dl, r, n), start=True, stop=False), signal=False)
                    tp_ = P.op('tensor', lambda e, j=j: e.matmul(bank[2 + bp][:, 128 * j:128 * j + 128], ident, nmL, start=False, stop=True), signal=(j == 3))
                pc0 = 128 if n0 == 0 else 0
                ps_ = prs.next(); pp_ = prp.next()
                P.wait('scalar', t, prs.free[ps_])
                te1 = P.op('scalar', lambda e: e.activation(out=p_t[ps_][:, :], in_=bank[bs][:, :], func=AF.Exp))
                ss.release(bs, te1)
                P.wait('scalar', tp_, prp.free[pp_])
                te2 = P.op('scalar', lambda e: e.activation(out=p_t[2 + pp_][:, pc0:512], in_=bank[2 + bp][:, pc0:512], func=AF.Exp))
                spv.release(bp, te2)
                st_[i] = (ps_, pp_, te1, te2)

            def g2(i):
                dl, vb, nbk, r, n0 = groups[i]
                ps_, pp_, te1, te2 = st_.pop(i)
                o_ = ob.acquire(); l_ = lb.acquire()
                P.wait('tensor', te1, te2)
                t = None
                for j in range(4):
                    n = n0 + j
                    cs = slice(128 * j, 128 * j + 128)
                    vs = lambda nn: vb[:, (r * nbk + nn) * 128:(r * nbk + nn) * 128 + 128]
                    if n > 0:
                        P.op('tensor', lambda e, cs=cs, n=n, vs=vs: e.matmul(bank[4 + o_][:, cs], vs(n - 1), p_t[2 + pp_][:, cs], start=True, stop=False), signal=False)
                        P.op('tensor', lambda e, cs=cs, n=n, vs=vs: e.matmul(bank[4 + o_][:, cs], vs(n), p_t[ps_][:, cs], start=False, stop=True), signal=False)
                        P.op('tensor', lambda e, cs=cs: e.matmul(bank[6 + l_][:, cs], onesb, p_t[2 + pp_][:, cs], start=True, stop=False), signal=False)
                        t = P.op('tensor', lambda e, cs=cs: e.matmul(bank[6 + l_][:, cs], onesb, p_t[ps_][:, cs], start=False, stop=True))
                    else:
                        P.op('tensor', lambda e, cs=cs, n=n, vs=vs: e.matmul(bank[4 + o_][:, cs], vs(n), p_t[ps_][:, cs], start=True, stop=True), signal=False)
                        t = P.op('tensor', lambda e, cs=cs: e.matmul(bank[6 + l_][:, cs], onesb, p_t[ps_][:, cs], start=True, stop=True))
                prs.free[ps_] = t
                prp.free[pp_] = t
                s0 = r + dl * 128 * n0
                asl = slice(s0, s0 + dl * 511 + 1, dl)
                P.wait('vector', t, acc_tok[0])
                if dl == 1:
                    t1 = P.op('vector', lambda e: e.tensor_copy(out=Oacc[:, asl], in_=bank[4 + o_][:, :]))
                    t2 = P.op('vector', lambda e: e.tensor_copy(out=Lacc[:, asl], in_=bank[6 + l_][:, :]))
                else:
                    t1 = P.op('vector', lambda e: e.tensor_tensor(out=Oacc[:, asl], in0=Oacc[:, asl], in1=bank[4 + o_][:, :], op=ALU.add))
                    t2 = P.op('vector', lambda e: e.tensor_tensor(out=Lacc[:, asl], in0=Lacc[:, asl], in1=bank[6 + l_][:, :], op=ALU.add))
                ob.release(o_, t1); lb.release(l_, t2)
                acc_tok[0] = t2

            LA = 1
            for i in range(len(groups) + LA):
                if i < len(groups):
                    g1(i)
                if i >= LA:
                    g2(i - LA)
            P.wait('vector', acc_tok[0])
            for c in range(16):
                csl = slice(c * 512, c * 512 + 512)
                t1 = P.op('vector', lambda e, csl=csl: e.reciprocal(out=Lacc[:, csl], in_=Lacc[:, csl]))
                slot = ost.acquire('vector')
                P.wait('vector', t1)
                t2 = P.op('vector', lambda e, csl=csl, slot=slot: e.tensor_tensor(out=ost.tiles[slot], in0=Oacc[:, csl], in1=Lacc[:, csl], op=ALU.mult))
                ost.store(slot, od[1, h, :, csl], t2)
            barrier()
        P.wait('sync', ost.final_toks())
    P.end_phase(cond_core)


THETA = 500000.0
def rope_tables(pos):
    pos = pos.astype(np.float32)
    invA = (THETA ** (-np.arange(32, dtype=np.float32) * (2.0 / 64))).astype(np.float32)
    angA = pos[None, :] * invA[:, None]
    cosA = np.concatenate([np.cos(angA), np.cos(angA)], 0).astype(np.float32)
    sinA = np.concatenate([np.sin(angA), np.sin(angA)], 0).astype(np.float32)
    invB = (THETA ** (-np.arange(16, dtype=np.float32) * (2.0 / 32))).astype(np.float32)
    angB = pos[None, :] * invB[:, None]
    T = pos.shape[0]
    cosB = np.ones((128, T), np.float32); sinB = np.zeros((128, T), np.float32)
    cosB[0:16] = np.cos(angB); cosB[16:32] = np.cos(angB)
    sinB[0:16] = np.sin(angB); sinB[16:32] = np.sin(angB)
    return cosA, sinA, cosB, sinB
def rot_mats():
    RA = np.zeros((64, 64), np.float32)
    for i in range(32):
        RA[i + 32, i] = -1.0
        RA[i, i + 32] = 1.0
    RB = np.zeros((128, 128), np.float32)
    for i in range(16):
        RB[i + 16, i] = -1.0
        RB[i, i + 16] = 1.0
    return RA, RB
def lay(g, nk):
    return np.ascontiguousarray(g.reshape(nk, 128).T)


NSH = 8
DEPTH = 2


def build_fused():
    nc = bass.Bass("TRN2", target_bir_lowering=False)
    din = lambda n, s, dt=F32: nc.dram_tensor(n, s, dt, kind="ExternalInput").ap()
    xin = din("xin", [NSH, D, T])
    w_in = din("w_in", [DEPTH, D, INC]); gatt = din("gatt", [DEPTH, 128, 32]); gq = din("gq", [DEPTH, 128, 7]); gkv = din("gkv", [DEPTH, 128, 4])
    w_uq = din("w_uq", [DEPTH, 896, 1536]); w_uk = din("w_uk", [DEPTH, 512, 1024]); w_uv = din("w_uv", [DEPTH, 512, 1024])
    bfd = din("bf", [DEPTH, 8, 1])
    gout = din("gout", [DEPTH, 128, 32]); w_o = din("w_o", [DEPTH, D, D]); gmlp = din("gmlp", [DEPTH, 128, 32])
    w_up = din("w_up", [DEPTH, D, FF]); w_down = din("w_down", [DEPTH, FF, D]); gfin = din("gfin", [128, 32])
    cosA = din("cosA", [NSH, 64, T]); sinA = din("sinA", [NSH, 64, T]); cosB = din("cosB", [NSH, 128, T]); sinB = din("sinB", [NSH, 128, T])
    RA = din("RA", [64, 64]); RB = din("RB", [128, 128]); ones = din("ones", [128, 128])
    cbd = din("cb", [128, 7 * 128], BF16); sud = din("su", [64, 65], BF16); trid = din("tri", [128, 128], BF16)
    yT = nc.dram_tensor("yT", [D, T], F32, kind="ExternalOutput").ap()
    X1 = nc.dram_tensor("X1", [NSH, D, T], F32).ap()
    qkd = nc.dram_tensor("qkd", [NSH, 64, 128, T], BF16).ap()
    r64d = nc.dram_tensor("r64d", [NSH, 9, 64, T], BF16).ap()
    vd = nc.dram_tensor("vd", [NSH, T, 4096], BF16).ap()
    lfd = nc.dram_tensor("lfd", [NSH, 8, T], F32).ap()
    od = nc.dram_tensor("od", [4, 8, 128, S], F32).ap()
    latd = nc.dram_tensor("latd", [12, 128, T], F32).ap()
    x1d = nc.dram_tensor("x1d", [D, T], F32).ap()
    own_o = nc.dram_tensor("own_o", [4, 8, 128, T], F32).ap()
    own_x = nc.dram_tensor("own_x", [D, T], F32).ap()
    accd = nc.dram_tensor("accd", [D, T], F32).ap()
    with ExitStack() as es:
        P = Prog(nc, es)
        for l in range(DEPTH):
            Xi = xin if l == 0 else X1
            Xo = X1
            for c in range(NSH):
                emit_A(P, Xi[c], w_in[l], gatt[l], gq[l], gkv[l], w_uq[l], w_uk[l], w_uv[l], bfd[l],
                       cosA[c], sinA[c], cosB[c], sinB[c], RA, RB, ones, qkd[c], r64d[c], vd[c], lfd[c], latd)
            for h in range(8):
                if l < DEPTH - 1:
                    emit_B(P, h, qkd, r64d, vd, lfd, cbd, sud, trid, od)
                else:
                    emit_B(P, h, qkd, r64d, vd, lfd, cbd, sud, trid, od, mixers="B")
                    for c in range(NSH):
                        emit_B(P, h, qkd, r64d, vd, lfd, cbd, sud, trid, od, mixers="ADC", qchunks=[2 * c, 2 * c + 1], cond_core=c)
            if l < DEPTH - 1:
                for c in range(NSH):
                    emit_C(P, False, (lambda k, c=c: od[k // 8, k % 8, :, c * T:(c + 1) * T]), Xi[c],
                           gout[l], gmlp[l], gfin, w_o[l], w_up[l], w_down[l], ones, Xo[c], x1d, accd)
            else:
                P.begin_phase()
                gsem = P.dsem("gsem")
                gsem.count += 16 * 8

                def gather(e):
                    pid = P.pid_of(e, 'sync')
                    for m_ in range(4):
                        e.dma_start(out=own_o[m_].rearrange("h p t -> (h p) t"),
                                    in_=od[m_, :, :, bass.ts(pid, T)].rearrange("h p t -> (h p) t")).then_inc(gsem.h, 16)
                    xsrc = Xi[bass.ts(pid, 1)].rearrange("o d t -> d (o t)")
                    for r_ in range(4):
                        e.dma_start(out=own_x[r_ * 1024:(r_ + 1) * 1024, :], in_=xsrc[r_ * 1024:(r_ + 1) * 1024, :]).then_inc(gsem.h, 16)
                P.ops['sync'].append(gather)
                P.wait('sync', (gsem.h, gsem.count))
                P.end_phase()
                emit_C(P, True, (lambda k: own_o[k // 8, k % 8, :, :]), own_x, gout[l], gmlp[l], gfin, w_o[l], w_up[l], w_down[l], ones, yT, x1d, accd)
        n_instr = P.n_instr
    return nc, n_instr


import ml_dtypes
_BF = ml_dtypes.bfloat16
_FUSED = {}


def _b_consts():
    p = np.arange(128)[:, None]
    f = np.arange(128)[None, :]
    ident = (p == f).astype(np.float32)
    ones = np.ones((128, 128), np.float32)
    nmU = np.where(p > f, NEG, 0.0)
    nmL = np.where(p < f, NEG, 0.0)
    nmS = np.where(p >= f, NEG, 0.0)
    nTinc = np.where(p >= f, -1.0, 0.0)
    zeros = np.zeros((128, 128))
    cb = np.concatenate([ident, ones, nmU, nmL, nmS, nTinc, zeros], 1).astype(np.float32)
    su = (np.arange(64)[:, None] < np.arange(65)[None, :]).astype(np.float32)
    tri = (p <= f).astype(np.float32)
    return cb.astype(_BF), su.astype(_BF), tri.astype(_BF)


def kernel(x, g_attn, w_in, g_q, g_kv, w_uq, w_uk, w_uv, b_f, g_out, w_o, g_mlp, w_up, w_down, g_final):
    f32 = lambda a: np.ascontiguousarray(np.asarray(a, dtype=np.float32))
    if 'nc' not in _FUSED:
        _FUSED['nc'], _ = build_fused()
    nc = _FUSED['nc']
    x = f32(x)
    RA, RB = rot_mats()
    cb, su, tri = _b_consts()
    tabs = [rope_tables(c * T + np.arange(T)) for c in range(NSH)]
    layn = lambda g, nk: np.stack([lay(f32(g[l]), nk) for l in range(DEPTH)])
    ins = dict(
        xin=np.ascontiguousarray(x[0].reshape(NSH, T, D).transpose(0, 2, 1)),
        w_in=f32(w_in), gatt=layn(g_attn, 32), gq=layn(g_q, 7), gkv=layn(g_kv, 4),
        w_uq=f32(w_uq), w_uk=f32(w_uk), w_uv=f32(w_uv), bf=f32(b_f).reshape(DEPTH, 8, 1),
        gout=layn(g_out, 32), w_o=f32(w_o), gmlp=layn(g_mlp, 32), w_up=f32(w_up), w_down=f32(w_down), gfin=lay(f32(g_final), 32),
        cosA=np.stack([t[0] for t in tabs]), sinA=np.stack([t[1] for t in tabs]),
        cosB=np.stack([t[2] for t in tabs]), sinB=np.stack([t[3] for t in tabs]),
        RA=RA, RB=RB, ones=np.ones((128, 128), np.float32), cb=cb, su=su, tri=tri)
    res = run_bass_kernel_spmd(nc, [ins for _ in range(NSH)], core_ids=list(range(NSH)))
    out = np.concatenate([res.results[c]["yT"].T for c in range(NSH)], axis=0)
    return np.ascontiguousarray(out).reshape(1, S, D).astype(np.float32)
```

```python
import numpy as np
from contextlib import ExitStack
import concourse.bass as bass
import concourse.mybir as mybir
from concourse.bass_utils import run_bass_kernel_spmd

F32 = mybir.dt.float32
BF16 = mybir.dt.bfloat16
AF = mybir.ActivationFunctionType
ALU = mybir.AluOpType
ENGS = ['sync', 'scalar', 'vector', 'gpsimd', 'tensor']
EPS = 1e-6


class DSem:
    def __init__(self, h):
        self.h = h
        self.count = 0


class Prog:
    def __init__(self, nc, es):
        self.nc = nc
        self.es = es
        self.ops = {e: [] for e in ENGS}
        self.psem = {}
        self.pcnt = {e: 0 for e in ENGS}
        self.waited = {}
        self.nsem = 0
        for e in ['scalar', 'vector', 'gpsimd', 'tensor']:
            self.psem[e] = es.enter_context(nc.semaphore(f"p_{e}"))
        self.bank_t = [es.enter_context(nc.psum_tensor(f"bank{i}", [128, 512], F32)) for i in range(8)]
        self.sem_pool = []
        self.phase_sems = []
        self.phase_es = None
        self.phase_id = 0
        self.n_instr = 0

    def dsem(self, name):
        if self.sem_pool:
            d = self.sem_pool.pop()
        else:
            d = DSem(self.es.enter_context(self.nc.semaphore(f"ds{self.nsem}")))
            self.nsem += 1
        self.phase_sems.append(d)
        return d

    def sb(self, name, shape, dt):
        return self.phase_es.enter_context(self.nc.sbuf_tensor(f"{name}_p{self.phase_id}", shape, dt))

    def pid_of(self, e, name):
        if not hasattr(self, '_pid'):
            self._pid = {}
        if name not in self._pid:
            self._pid[name] = e.partition_id()
        return self._pid[name]

    def begin_phase(self):
        self.phase_es = ExitStack()
        self.phase_id += 1
        self.snap_p = dict(self.pcnt)
        for d in self.sem_pool:
            d.byq = {}

    def barrier_all(self):
        for e in ENGS:
            for o in ['scalar', 'vector', 'gpsimd', 'tensor']:
                if o != e and self.pcnt[o] > 0:
                    self.wait(e, (self.psem[o], self.pcnt[o]))
            for d in self.phase_sems:
                if d.count > 0:
                    self.wait(e, (d.h, d.count))

    def end_phase(self, cond_core=None):
        self.barrier_all()
        self.build(cond_core)
        for k in self.ops:
            self.n_instr += len(self.ops[k])
            self.ops[k] = []
        self.phase_es.close()
        self.phase_es = None
        self.sem_pool.extend(self.phase_sems)
        self.phase_sems = []

    def ps(self, name, shape, dt=F32):
        return self.es.enter_context(self.nc.psum_tensor(name, shape, dt))

    def op(self, eng, fn, signal=True):
        if signal:
            self.pcnt[eng] += 1
            s = self.psem[eng]
            self.ops[eng].append(lambda e, fn=fn, s=s: fn(e).then_inc(s, 1))
            return (s, self.pcnt[eng])
        self.ops[eng].append(fn)
        return None

    def wait(self, eng, *toks):
        for tok in toks:
            if tok is None:
                continue
            if isinstance(tok, list):
                self.wait(eng, *tok)
                continue
            s, v = tok
            if eng in self.psem and s is self.psem[eng]:
                pass
            key = (eng, id(s))
            if self.waited.get(key, 0) >= v:
                continue
            self.waited[key] = v
            self.ops[eng].append(lambda e, s=s, v=v: e.wait_ge(s, v))

    def dma(self, queue, out, in_, ds, **kw):
        ds.count += 16
        ds.byq = getattr(ds, 'byq', {})
        ds.byq[queue] = ds.byq.get(queue, 0) + 16
        self.ops[queue].append(lambda e, out=out, in_=in_, h=ds.h, kw=kw: e.dma_start(
            out=(out(e) if callable(out) else out), in_=(in_(e) if callable(in_) else in_), **kw).then_inc(h, 16))
        return (ds.h, ds.count)

    def build(self, cond_core=None):
        with self.nc.Block() as block:
            for name in ENGS:
                ops = self.ops[name]
                if not ops:
                    continue
                comp = []
                if cond_core is not None:
                    if name in self.psem and self.pcnt[name] > self.snap_p[name]:
                        comp.append((self.psem[name], self.pcnt[name] - self.snap_p[name]))
                    for d in self.phase_sems:
                        n = getattr(d, 'byq', {}).get(name, 0)
                        if n:
                            comp.append((d.h, n))

                def body(e, ops=ops, comp=comp):
                    if cond_core is None:
                        for f in ops:
                            f(e)
                    else:
                        pid = self.pid_of(e, name)
                        with e.If(pid == cond_core):
                            for f in ops:
                                f(e)
                        with e.Else():
                            for s, n in comp:
                                e.sem_inc(s, n)
                getattr(block, name)(body)


class Banks:
    def __init__(self, P, banks):
        self.P = P
        self.banks = banks
        self.free = [None] * len(banks)
        self.n = 0

    def acquire(self):
        i = self.n % len(self.banks)
        self.n += 1
        self.P.wait('tensor', self.free[i])
        return i

    def release(self, i, *toks):
        self.free[i] = list(toks)


class TileRing:
    def __init__(self, tiles):
        self.tiles = tiles
        self.free = [None] * len(tiles)
        self.n = 0

    def next(self):
        i = self.n % len(self.tiles)
        self.n += 1
        return i


class SlabRing:
    def __init__(self, P, tiles, queue='gpsimd'):
        self.P = P
        self.tiles = tiles
        self.sems = [P.dsem(f"slab_sem{i}") for i in range(len(tiles))]
        self.free = [None] * len(tiles)
        self.n = 0
        self.queue = queue

    def load(self, W, r0, kc, c0, cw, kgroup=8):
        P = self.P
        slot = self.n % len(self.tiles)
        self.n += 1
        P.wait(self.queue, self.free[slot])
        t = self.tiles[slot]
        src = W[r0:r0 + kc * 128, c0:c0 + cw].rearrange("(k p) c -> p k c", p=128)
        tok = None
        for k0 in range(0, kc, kgroup):
            k1 = min(kc, k0 + kgroup)
            dst = t[:, k0 * cw:k1 * cw].rearrange("p (k c) -> p k c", c=cw)
            tok = P.dma(self.queue, dst, src[:, k0:k1, :], self.sems[slot])
        return slot, tok

    def release(self, slot, tok):
        self.free[slot] = tok


T = 1024
D = 4096
FF = 16384


class LoadRing:
    def __init__(self, P, tiles, name, queue='sync'):
        self.P = P
        self.tiles = [t if type(t).__name__ == 'AP' else t[:, :] for t in tiles]
        self.sems = [P.dsem(f"{name}_s{i}") for i in range(len(tiles))]
        self.free = [None] * len(tiles)
        self.n = 0
        self.queue = queue

    def load(self, src):
        P = self.P
        slot = self.n % len(self.tiles)
        self.n += 1
        P.wait(self.queue, self.free[slot])
        tok = P.dma(self.queue, self.tiles[slot], src, self.sems[slot])
        return slot, tok

    def release(self, slot, *toks):
        self.free[slot] = list(toks)


class StoreRing:
    def __init__(self, P, tiles, name, queue='sync'):
        self.P = P
        self.tiles = [t if type(t).__name__ == 'AP' else t[:, :] for t in tiles]
        self.sems = [P.dsem(f"{name}_s{i}") for i in range(len(tiles))]
        self.free = [None] * len(tiles)
        self.n = 0
        self.queue = queue

    def acquire(self, eng):
        slot = self.n % len(self.tiles)
        self.n += 1
        self.P.wait(eng, self.free[slot])
        return slot

    def store(self, slot, dst, after, extra=(), src=None):
        P = self.P
        P.wait(self.queue, after)
        tok = P.dma(self.queue, dst, self.tiles[slot] if src is None else src, self.sems[slot])
        self.free[slot] = [tok] + list(extra)
        return tok

    def final_toks(self):
        return [(s.h, s.count) for s in self.sems if s.count > 0]


def xrows(xT, k):
    if callable(xT):
        return xT(k)
    return xT[k * 128:(k + 1) * 128, :]

def emit_C(P, final, oT_tile, xT, goutd, gmlpd, gfind, w_o, w_up, w_down, onesd, yT, x1d, accd):
    P.begin_phase()
    acc_t = accd if final else yT
    acc_tok = {}
    if True:
        actT = P.sb("actT", [128, 32 * T], BF16)
        ubuf = P.sb("ubuf", [128, 16 * T], BF16)
        slab_t = [P.sb(f"slab{i}", [128, 16384], BF16) for i in range(2)]
        tp = [P.sb(f"tp{i}", [128, T], F32) for i in range(6)]
        stg_t = [P.sb(f"stg{i}", [128, 512], F32) for i in range(4)]
        rl_t = [P.sb(f"rl{i}", [128, 512], F32) for i in range(2)]
        rstd = P.sb("rstd", [128, T], F32)
        ones = P.sb("ones_sb", [128, 128], F32)
        gout = P.sb("gout_sb", [128, 32], F32)
        gmlp = P.sb("gmlp_sb", [128, 32], F32)
        gfin = P.sb("gfin_sb", [128, 32], F32)
        bank_t = P.bank_t
        st = bank_t[0:2]
        banks = Banks(P, bank_t[2:8])
        slabs = SlabRing(P, slab_t)
        csem = P.dsem("csem")
        P.dma('sync', ones[:, :], onesd, csem)
        P.dma('sync', gout[:, :], goutd, csem)
        P.dma('sync', gmlp[:, :], gmlpd, csem)
        ctok = P.dma('sync', gfin[:, :], gfind, csem)
        for e in ['scalar', 'vector', 'tensor']:
            P.wait(e, ctok)

        def act(k, half=None):
            if half is None:
                return actT[:, k * T:(k + 1) * T]
            return actT[:, k * T + half * 512:k * T + half * 512 + 512]

        ofp = [ubuf[:, 2 * j * T:2 * (j + 1) * T].bitcast(F32) for j in range(8)]
        oring = LoadRing(P, ofp, "oring")
        xring = LoadRing(P, tp[0:3], "xring")
        sqr = TileRing(tp[3:5])
        stg = StoreRing(P, stg_t, "stg")

        slab_tasks = []
        for s in range(8):
            slab_tasks.append(('o', w_o, 0, 32, 512 * s, 512))
        for fb in range(8):
            for s in range(4):
                slab_tasks.append(('u', w_up, 0, 32, fb * 2048 + 512 * s, 512))
            for s in range(4):
                slab_tasks.append(('d', w_down, fb * 2048, 16, 1024 * s, 1024))
        slab_loaded = {}

        def prefetch(i):
            if i < len(slab_tasks) and i not in slab_loaded:
                _, W, r0, kc, c0, cw = slab_tasks[i]
                slab_loaded[i] = slabs.load(W, r0, kc, c0, cw)

        prefetch(0)

        def rstd_from_stats(n_feat, after_tok, war_toks):
            P.wait('scalar', after_tok, war_toks)
            ta = None
            for half in range(2):
                ta = P.op('scalar', lambda e, half=half: e.activation(
                    out=rstd[:, half * 512:half * 512 + 512], in_=st[half][:, :], func=AF.Sqrt,
                    scale=1.0 / n_feat, bias=EPS))
            P.wait('vector', ta)
            tv = P.op('vector', lambda e: e.reciprocal(out=rstd[:, :], in_=rstd[:, :]))
            return ta, tv

        last_norm_tok = None
        st_read_tok = None
        for gi in range(4):
            ltoks = []
            for j in range(8):
                k = 8 * gi + j
                slot, tok = oring.load(oT_tile(k))
                ltoks.append(tok)
            pe_tok = None
            for j in range(8):
                P.wait('scalar', ltoks[j])
                si = sqr.next()
                P.wait('scalar', sqr.free[si])
                ta = P.op('scalar', lambda e, j=j, si=si: e.activation(out=sqr.tiles[si][:, :], in_=ofp[j], func=AF.Square))
                P.wait('tensor', ta)
                if j == 0:
                    P.wait('tensor', st_read_tok)
                for half in range(2):
                    pe_tok = P.op('tensor', lambda e, j=j, si=si, half=half: e.matmul(
                        st[half][:, :], ones[:, :], sqr.tiles[si][:, half * 512:half * 512 + 512],
                        start=(j == 0), stop=(j == 7)), signal=(half == 1))
                sqr.free[si] = pe_tok
            st_read_tok, tv = rstd_from_stats(1024.0, pe_tok, last_norm_tok)
            P.wait('vector', tv)
            for j in range(8):
                k = 8 * gi + j
                P.wait('vector', ltoks[j])
                last_norm_tok = P.op('vector', lambda e, j=j, k=k: e.scalar_tensor_tensor(
                    out=act(k), in0=ofp[j], scalar=gout[:, k:k + 1], in1=rstd[:, :], op0=ALU.mult, op1=ALU.mult))
                oring.release(j, last_norm_tok)

        P.wait('tensor', last_norm_tok)
        pending = None
        x1tok = {}
        xload = {}

        def xprefetch(oc):
            if oc < 32 and oc not in xload:
                xload[oc] = xring.load(xrows(xT, oc))

        xprefetch(0)
        xprefetch(1)
        ti = 0
        stat_tok = None
        for s in range(8):
            prefetch(ti + 1)
            slot, stok = slab_loaded[ti]
            P.wait('tensor', stok)
            sl = slab_t[slot]
            pe_tok = None
            for ocl in range(4):
                oc = 4 * s + ocl
                xprefetch(oc + 2)
                xslot, xtok = xload[oc]
                dtoks = []
                for half in range(2):
                    b = banks.acquire()
                    for k in range(32):
                        pe_tok = P.op('tensor', lambda e, b=b, k=k, ocl=ocl, half=half, sl=sl: e.matmul(
                            banks.banks[b][:, :], sl[:, k * 512 + ocl * 128:k * 512 + ocl * 128 + 128], act(k, half),
                            start=(k == 0), stop=(k == 31)), signal=(k == 31))
                    if pending is not None:
                        pending()
                        pending = None
                    sslot = stg.acquire('vector')
                    P.wait('vector', pe_tok, xtok)
                    tv = P.op('vector', lambda e, b=b, sslot=sslot, xslot=xslot, half=half: e.tensor_tensor(
                        out=stg_t[sslot][:, :], in0=banks.banks[b][:, :],
                        in1=xring.tiles[xslot][:, half * 512:half * 512 + 512], op=ALU.add))
                    banks.release(b, tv)
                    dtoks.append(tv)
                    si = sqr.next()
                    P.wait('scalar', tv, sqr.free[si])
                    ta = P.op('scalar', lambda e, si=si, sslot=sslot: e.activation(
                        out=sqr.tiles[si][:, 0:512], in_=stg_t[sslot][:, :], func=AF.Square))
                    x1tok[(oc, half)] = stg.store(sslot, x1d[oc * 128:(oc + 1) * 128, half * 512:half * 512 + 512], tv, extra=[ta])
                    acc_tok[(oc, half)] = P.dma('sync', acc_t[oc * 128:(oc + 1) * 128, half * 512:half * 512 + 512], stg.tiles[sslot], stg.sems[sslot])
                    stg.free[sslot].append(acc_tok[(oc, half)])

                    def mk(si=si, half=half, oc=oc, ta=ta):
                        def f():
                            nonlocal stat_tok
                            P.wait('tensor', ta)
                            stat_tok = P.op('tensor', lambda e: e.matmul(
                                st[half][:, :], ones[:, :], sqr.tiles[si][:, 0:512], start=(oc == 0), stop=(oc == 31)))
                            sqr.free[si] = stat_tok
                        return f
                    if oc == 0:
                        P.wait('tensor', st_read_tok)
                    pending = mk()
                xring.release(xslot, *dtoks)
            slabs.release(slot, pe_tok)
            ti += 1
        pending()
        pending = None
        st_read_tok, tv = rstd_from_stats(4096.0, stat_tok, last_norm_tok)

        P.wait('vector', tv, stat_tok)
        xload2 = {}

        def x1prefetch(k):
            if k < 32 and k not in xload2:
                P.wait('sync', x1tok[(k, 0)], x1tok[(k, 1)])
                xload2[k] = xring.load(x1d[k * 128:(k + 1) * 128, :])
        x1prefetch(0)
        x1prefetch(1)
        for k in range(32):
            x1prefetch(k + 2)
            xslot, xtok = xload2[k]
            P.wait('vector', xtok)
            last_norm_tok = P.op('vector', lambda e, k=k, xslot=xslot: e.scalar_tensor_tensor(
                out=act(k), in0=xring.tiles[xslot][:, :], scalar=gmlp[:, k:k + 1], in1=rstd[:, :], op0=ALU.mult, op1=ALU.mult))
            xring.release(xslot, last_norm_tok)

        P.wait('tensor', last_norm_tok)
        rlr = TileRing(rl_t)
        down_last_pe = None
        nev = 0
        for fb in range(8):
            u_last = None
            for s in range(4):
                prefetch(ti + 1)
                slot, stok = slab_loaded[ti]
                P.wait('tensor', stok)
                sl = slab_t[slot]
                pe_tok = None
                for fl in range(4):
                    ffc = 4 * s + fl
                    for half in range(2):
                        b = banks.acquire()
                        for k in range(32):
                            pe_tok = P.op('tensor', lambda e, b=b, k=k, fl=fl, half=half, sl=sl: e.matmul(
                                banks.banks[b][:, :], sl[:, k * 512 + fl * 128:k * 512 + fl * 128 + 128], act(k, half),
                                start=(k == 0), stop=(k == 31)), signal=(k == 31))
                        ri = rlr.next()
                        P.wait('scalar', pe_tok, rlr.free[ri])
                        ta = P.op('scalar', lambda e, b=b, ri=ri: e.activation(out=rl_t[ri][:, :], in_=banks.banks[b][:, :], func=AF.Relu))
                        banks.release(b, ta)
                        P.wait('vector', ta, down_last_pe)
                        u_last = P.op('vector', lambda e, ri=ri, ffc=ffc, half=half: e.tensor_tensor(
                            out=ubuf[:, ffc * T + half * 512:ffc * T + half * 512 + 512], in0=rl_t[ri][:, :], in1=rl_t[ri][:, :], op=ALU.mult))
                        rlr.free[ri] = u_last
                slabs.release(slot, pe_tok)
                ti += 1
            P.wait('tensor', u_last)
            for s in range(4):
                prefetch(ti + 1)
                slot, stok = slab_loaded[ti]
                P.wait('tensor', stok)
                sl = slab_t[slot]
                pe_tok = None
                for ol in range(8):
                    oc = 8 * s + ol
                    for half in range(2):
                        b = banks.acquire()
                        for k in range(16):
                            pe_tok = P.op('tensor', lambda e, b=b, k=k, ol=ol, half=half, sl=sl: e.matmul(
                                banks.banks[b][:, :], sl[:, k * 1024 + ol * 128:k * 1024 + ol * 128 + 128],
                                ubuf[:, k * T + half * 512:k * T + half * 512 + 512],
                                start=(k == 0), stop=(k == 15)), signal=(k == 15))
                        eng = 'scalar' if nev % 2 == 0 else 'vector'
                        nev += 1
                        sslot = stg.acquire(eng)
                        P.wait(eng, pe_tok)
                        if eng == 'scalar':
                            te = P.op('scalar', lambda e, b=b, sslot=sslot: e.activation(out=stg_t[sslot][:, :], in_=banks.banks[b][:, :], func=AF.Copy))
                        else:
                            te = P.op('vector', lambda e, b=b, sslot=sslot: e.tensor_copy(out=stg_t[sslot][:, :], in_=banks.banks[b][:, :]))
                        banks.release(b, te)
                        P.wait('gpsimd', te, acc_tok[(oc, half)])
                        acc_tok[(oc, half)] = P.dma('gpsimd', acc_t[oc * 128:(oc + 1) * 128, half * 512:half * 512 + 512],
                                                    stg.tiles[sslot], stg.sems[sslot], accum_op=ALU.add)
                        stg.free[sslot] = [acc_tok[(oc, half)]]
                down_last_pe = pe_tok
                slabs.release(slot, pe_tok)
                ti += 1

        P.wait('sync', stg.final_toks())
        osem = P.dsem("osem")
        if final:
            accr = LoadRing(P, tp[3:6], "accr")
            fin_stat = None
            for k in range(32):
                aslot, atok = accr.load(acc_t[k * 128:(k + 1) * 128, :])
                sslot = k % 3
                P.wait('scalar', atok, xring.free[sslot])
                ta = P.op('scalar', lambda e, aslot=aslot, sslot=sslot: e.activation(
                    out=xring.tiles[sslot][:, :], in_=accr.tiles[aslot][:, :], func=AF.Square))
                accr.release(aslot, ta)
                P.wait('tensor', ta)
                if k == 0:
                    P.wait('tensor', st_read_tok)
                for half in range(2):
                    fin_stat = P.op('tensor', lambda e, sslot=sslot, half=half, k=k: e.matmul(
                        st[half][:, :], ones[:, :], xring.tiles[sslot][:, half * 512:half * 512 + 512],
                        start=(k == 0), stop=(k == 31)), signal=(half == 1))
                xring.free[sslot] = [fin_stat]
            st_read_tok, tv = rstd_from_stats(4096.0, fin_stat, last_norm_tok)
            P.wait('vector', tv)
            for k in range(32):
                aslot, atok = accr.load(acc_t[k * 128:(k + 1) * 128, :])
                P.wait('vector', atok)
                tv2 = P.op('vector', lambda e, aslot=aslot, k=k: e.scalar_tensor_tensor(
                    out=accr.tiles[aslot][:, :], in0=accr.tiles[aslot][:, :], scalar=gfin[:, k:k + 1], in1=rstd[:, :],
                    op0=ALU.mult, op1=ALU.mult))
                P.wait('sync', tv2)
                tok = P.dma('sync', yT[k * 128:(k + 1) * 128, :], accr.tiles[aslot][:, :], osem)
                accr.release(aslot, tok)
        P.wait('sync', (osem.h, osem.count))
    P.end_phase()


INC = 10696
SC_A = 192.0 ** -0.5
SC_H = 128.0 ** -0.5

def emit_A(P, xT, w_in, gattd, gqd, gkvd, w_uq, w_uk, w_uv, bfd, cosAd, sinAd, cosBd, sinBd, RAd, RBd, onesd, qk, r64, v, lf, latd, which='all', cond_core=None):
    P.begin_phase()
    if True:
        hT = P.sb("hT", [128, 32 * T], BF16)
        slab_t = [P.sb(f"slab{i}", [128, 16384], BF16) for i in range(2)]
        ckvn = P.sb("ckvn", [128, 4 * T], BF16)
        tp = [P.sb(f"tp{i}", [128, T], F32) for i in range(4)]
        rstd = P.sb("rstd", [128, T], F32)
        cosA = P.sb("cosA_sb", [64, T], F32); sinA = P.sb("sinA_sb", [64, T], F32)
        cosB = P.sb("cosB_sb", [128, T], F32); sinB = P.sb("sinB_sb", [128, T], F32)
        RA = P.sb("RA_sb", [64, 64], F32); RB = P.sb("RB_sb", [128, 128], F32)
        ones = P.sb("ones_sb", [128, 128], F32)
        gatt = P.sb("gatt_sb", [128, 32], F32); gq = P.sb("gq_sb", [128, 7], F32); gkv = P.sb("gkv_sb", [128, 4], F32)
        bfs = P.sb("bf_sb", [8, 1], F32); nbf = P.sb("nbf_sb", [8, 1], F32)
        sb16_t = [P.sb(f"sb16_{i}", [128, 512], BF16) for i in range(4)]
        sf32_t = [P.sb(f"sf32_{i}", [128, 512], F32) for i in range(6)]
        bank_t = P.bank_t
        st = bank_t[0:2]
        banks = Banks(P, bank_t[2:8])
        slabs = SlabRing(P, slab_t)
        csem = P.dsem("csem")
        for dst, src in [(ones, onesd), (gatt, gattd), (gq, gqd), (gkv, gkvd), (bfs, bfd), (cosA, cosAd), (sinA, sinAd),
                         (cosB, cosBd), (sinB, sinBd), (RA, RAd), (RB, RBd)]:
            ctok = P.dma('sync', dst[:, :], src, csem)
        for e in ['scalar', 'vector', 'tensor']:
            P.wait(e, ctok)
        tnb = P.op('vector', lambda e: e.tensor_scalar(out=nbf[:, :], in0=bfs[:, :], scalar1=-1.0, scalar2=None, op0=ALU.mult))

        def hch(k, a=0, n=T):
            return hT[:, k * T + a:k * T + a + n]

        xring = LoadRing(P, tp[0:2], "xring")
        sqr = TileRing(tp[2:4])
        s16 = StoreRing(P, sb16_t, "s16")
        s32 = StoreRing(P, sf32_t[0:2], "s32")
        tfr = TileRing(sf32_t[2:4])
        abr = TileRing(sf32_t[4:6])

        tasks = [('lat', 0, 512, 0), ('lat', 512, 384, 4), ('lat', 896, 512, 7), ('kr', 1408, 64, 11)]
        for mi, mname in enumerate('BCD'):
            base = 1472 + 3072 * mi
            for part in range(2):
                tasks.append(('q' + mname, base + 512 * part, 512, 16 + 16 * mi + 4 * part))
            for part in range(2):
                tasks.append(('k' + mname, base + 1024 + 512 * part, 512, 24 + 16 * mi + 4 * part))
            for part in range(2):
                tasks.append(('v' + mname, base + 2048 + 512 * part, 512, 1024 * (mi + 1) + 512 * part))
        tasks.append(('pf', 10688, 8, 0))
        if which == 'kv':
            tasks = [t_ for t_ in tasks if not (t_[0] in ('qC', 'qD') or (t_[0] == 'lat' and t_[1] < 896))]
        elif which == 'q':
            tasks = [t_ for t_ in tasks if (t_[0] in ('qC', 'qD') or (t_[0] == 'lat' and t_[1] < 896))]
        do_q = which in ('all', 'q')
        do_kv = which in ('all', 'kv')
        loaded = {}

        def prefetch(i):
            if i < len(tasks) and i not in loaded:
                _, c0, cw, _ = tasks[i]
                loaded[i] = slabs.load(w_in, 0, 32, c0, cw)
        prefetch(0)

        def rstd_from_stats(n_feat, after_tok, war_toks):
            P.wait('scalar', after_tok, war_toks)
            ta = None
            for half in range(2):
                ta = P.op('scalar', lambda e, half=half: e.activation(
                    out=rstd[:, half * 512:half * 512 + 512], in_=st[half][:, :], func=AF.Sqrt,
                    scale=1.0 / n_feat, bias=EPS))
            P.wait('vector', ta)
            tv = P.op('vector', lambda e: e.reciprocal(out=rstd[:, :], in_=rstd[:, :]))
            return ta, tv

        pe_tok = None
        for k in range(32):
            xs, xtok = xring.load(xT[k * 128:(k + 1) * 128, :])
            si = sqr.next()
            P.wait('scalar', xtok, sqr.free[si])
            ta = P.op('scalar', lambda e, xs=xs, si=si: e.activation(out=sqr.tiles[si][:, :], in_=xring.tiles[xs], func=AF.Square))
            xring.release(xs, ta)
            P.wait('tensor', ta)
            for half in range(2):
                pe_tok = P.op('tensor', lambda e, si=si, half=half, k=k: e.matmul(
                    st[half][:, :], ones[:, :], sqr.tiles[si][:, half * 512:half * 512 + 512],
                    start=(k == 0), stop=(k == 31)), signal=(half == 1))
            sqr.free[si] = pe_tok
        st_read_tok, tv = rstd_from_stats(4096.0, pe_tok, None)
        P.wait('vector', tv)
        last_norm = None
        for k in range(32):
            xs, xtok = xring.load(xT[k * 128:(k + 1) * 128, :])
            P.wait('vector', xtok)
            last_norm = P.op('vector', lambda e, xs=xs, k=k: e.scalar_tensor_tensor(
                out=hch(k), in0=xring.tiles[xs], scalar=gatt[:, k:k + 1], in1=rstd[:, :], op0=ALU.mult, op1=ALU.mult))
            xring.release(xs, last_norm)
        P.wait('tensor', last_norm)

        nev = [0]

        def evac_copy_store(b, npart, dst, pe_tok, scale=None, f32=False):
            ring = s32 if f32 else s16
            eng = 'scalar' if nev[0] % 2 == 0 else 'vector'
            nev[0] += 1
            slot = ring.acquire(eng)
            P.wait(eng, pe_tok)
            o = ring.tiles[slot][0:npart, :]
            i = banks.banks[b][0:npart, :]
            if eng == 'scalar':
                te = P.op('scalar', lambda e: e.activation(out=o, in_=i, func=AF.Copy, scale=(1.0 if scale is None else scale)))
            else:
                if scale is None:
                    te = P.op('vector', lambda e: e.tensor_copy(out=o, in_=i))
                else:
                    te = P.op('vector', lambda e: e.tensor_scalar(out=o, in0=i, scalar1=scale, scalar2=None, op0=ALU.mult))
            banks.release(b, te)
            ring.store(slot, dst, te, src=o)
            return te

        def rope_store(src_ap, src_tok, src_is_psum_bank, npart, Rm, cosT, sinT, half, scale, dst):
            ti_ = tfr.next()
            tft = tfr.tiles[ti_][0:npart, :]
            if src_is_psum_bank is not None:
                P.wait('scalar', src_tok, tfr.free[ti_])
                tcp = P.op('scalar', lambda e: e.activation(out=tft, in_=src_ap, func=AF.Copy, scale=scale))
                banks.release(src_is_psum_bank, tcp)
            else:
                P.wait('scalar', src_tok, tfr.free[ti_])
                tcp = P.op('scalar', lambda e: e.activation(out=tft, in_=src_ap, func=AF.Copy, scale=scale))
            b2 = banks.acquire()
            P.wait('tensor', tcp)
            trot = P.op('tensor', lambda e: e.matmul(banks.banks[b2][0:npart, :], Rm[:, :], tft, start=True, stop=True))
            ai = abr.next()
            bi = abr.next()
            at = abr.tiles[ai][0:npart, :]
            bt = abr.tiles[bi][0:npart, :]
            cs = cosT[0:npart, half * 512:half * 512 + 512]
            sn = sinT[0:npart, half * 512:half * 512 + 512]
            P.wait('vector', tcp, abr.free[ai])
            ta_ = P.op('vector', lambda e: e.tensor_tensor(out=at, in0=tft, in1=cs, op=ALU.mult))
            P.wait('vector', trot, abr.free[bi])
            tb_ = P.op('vector', lambda e: e.tensor_tensor(out=bt, in0=banks.banks[b2][0:npart, :], in1=sn, op=ALU.mult))
            banks.release(b2, tb_)
            slot = s16.acquire('vector')
            o = s16.tiles[slot][0:npart, :]
            P.wait('vector', ta_, tb_)
            to = P.op('vector', lambda e: e.tensor_tensor(out=o, in0=at, in1=bt, op=ALU.add))
            tfr.free[ti_] = [trot, ta_]
            abr.free[ai] = to
            abr.free[bi] = to
            s16.store(slot, dst, to, src=o)

        lat_tok = {}
        for ti, (kind, c0, cw, dsti) in enumerate(tasks):
            prefetch(ti + 1)
            slot, stok = loaded[ti]
            P.wait('tensor', stok)
            sl = slab_t[slot]
            pe_tok = None
            if kind[0] == 'v':
                for tt in range(8):
                    b = banks.acquire()
                    for k in range(32):
                        pe_tok = P.op('tensor', lambda e, b=b, k=k, tt=tt, sl=sl: e.matmul(
                            banks.banks[b][:, :], hch(k, tt * 128, 128), sl[:, k * 512:k * 512 + 512],
                            start=(k == 0), stop=(k == 31)), signal=(k == 31))
                    evac_copy_store(b, 128, v[tt * 128:(tt + 1) * 128, dsti:dsti + 512], pe_tok)
            elif kind == 'pf':
                for half in range(2):
                    b = banks.acquire()
                    for k in range(32):
                        pe_tok = P.op('tensor', lambda e, b=b, k=k, half=half, sl=sl: e.matmul(
                            banks.banks[b][0:8, :], sl[:, k * 8:k * 8 + 8], hch(k, half * 512, 512),
                            start=(k == 0), stop=(k == 31)), signal=(k == 31))
                    ti_ = tfr.next()
                    e1 = tfr.tiles[ti_][0:8, :]
                    P.wait('scalar', pe_tok, tfr.free[ti_], tnb)
                    t1 = P.op('scalar', lambda e, b=b, e1=e1: e.activation(out=e1, in_=banks.banks[b][0:8, :], func=AF.Exp, scale=-1.0, bias=nbf[:, 0:1]))
                    banks.release(b, t1)
                    P.wait('scalar', t1)
                    t2 = P.op('scalar', lambda e, e1=e1: e.activation(out=e1, in_=e1, func=AF.Ln, scale=1.0, bias=1.0))
                    slot2 = s32.acquire('vector')
                    o = s32.tiles[slot2][0:8, :]
                    P.wait('vector', t2)
                    t3 = P.op('vector', lambda e, o=o, e1=e1: e.tensor_scalar(out=o, in0=e1, scalar1=-1.0, scalar2=None, op0=ALU.mult))
                    tfr.free[ti_] = t3
                    s32.store(slot2, lf[:, half * 512:half * 512 + 512], t3, src=o)
            else:
                ntile = (cw + 127) // 128
                for tl in range(ntile):
                    m = min(128, cw - tl * 128)
                    for half in range(2):
                        b = banks.acquire()
                        for k in range(32):
                            pe_tok = P.op('tensor', lambda e, b=b, k=k, tl=tl, half=half, sl=sl, m=m, cw=cw: e.matmul(
                                banks.banks[b][0:m, :], sl[:, k * cw + tl * 128:k * cw + tl * 128 + m], hch(k, half * 512, 512),
                                start=(k == 0), stop=(k == 31)), signal=(k == 31))
                        hs = slice(half * 512, half * 512 + 512)
                        if kind in ('lat', 'kr'):
                            evac_copy_store(b, m, latd[dsti + tl, 0:m, hs], pe_tok, f32=True)
                        elif kind in ('qB', 'kB'):
                            rope_store(banks.banks[b][:, :], pe_tok, b, 128, RB, cosB, sinB, half,
                                       SC_H if kind == 'qB' else 1.0, qk[dsti + tl, :, hs])
                        else:
                            evac_copy_store(b, 128, qk[dsti + tl, :, hs], pe_tok, scale=(SC_H if kind[0] == 'q' else None))
            slabs.release(slot, pe_tok)
        last_inproj_pe = pe_tok

        P.wait('gpsimd', last_inproj_pe)
        wsem = P.dsem("wsem")
        wq = slab_t[0]
        wkv = slab_t[1]
        wtok = None
        if do_q:
            wtok = P.dma('gpsimd', wq[:, 0:7 * 1536].rearrange("p (k c) -> p k c", c=1536), w_uq.rearrange("(k p) c -> p k c", p=128), wsem)
        if do_kv:
            P.dma('gpsimd', wkv[:, 0:4096].rearrange("p (k c) -> p k c", c=1024), w_uk.rearrange("(k p) c -> p k c", p=128), wsem)
            wtok = P.dma('gpsimd', wkv[:, 4096:8192].rearrange("p (k c) -> p k c", c=1024), w_uv.rearrange("(k p) c -> p k c", p=128), wsem)
        P.wait('sync', s32.final_toks(), last_inproj_pe)
        lat = [hT[:, 2 * j * T:2 * (j + 1) * T].bitcast(F32) for j in range(12)]
        cqn = hT[:, 24 * T:31 * T]
        lsem = P.dsem("lsem")
        ltok = []
        for j in range(12):
            if (j < 7 and not do_q) or (j >= 7 and not do_kv):
                continue
            npart = 64 if j == 11 else 128
            ltok.append(P.dma('sync', lat[j][0:npart, :], latd[j, 0:npart, :], lsem))
        ltok_all = ltok[-1]

        def lat_norm(j0, n, gv, dst_fn, nfeat, war):
            pe = None
            for j in range(n):
                si = sqr.next()
                P.wait('scalar', ltok_all, sqr.free[si])
                ta = P.op('scalar', lambda e, j=j, si=si: e.activation(out=sqr.tiles[si][:, :], in_=lat[j0 + j], func=AF.Square))
                P.wait('tensor', ta)
                if j == 0:
                    P.wait('tensor', war[0])
                for half in range(2):
                    pe = P.op('tensor', lambda e, si=si, half=half, j=j: e.matmul(
                        st[half][:, :], ones[:, :], sqr.tiles[si][:, half * 512:half * 512 + 512],
                        start=(j == 0), stop=(j == n - 1)), signal=(half == 1))
                sqr.free[si] = pe
            sr, tv = rstd_from_stats(nfeat, pe, war[1])
            P.wait('vector', tv)
            tn = None
            for j in range(n):
                tn = P.op('vector', lambda e, j=j: e.scalar_tensor_tensor(
                    out=dst_fn(j), in0=lat[j0 + j], scalar=gv[:, j:j + 1], in1=rstd[:, :], op0=ALU.mult, op1=ALU.mult))
            return sr, tn

        sr, tn_q, tn_kv = st_read_tok, last_norm, None
        if do_q:
            sr, tn_q = lat_norm(0, 7, gq, lambda j: cqn[:, j * T:(j + 1) * T], 896.0, (st_read_tok, last_norm))
        if do_kv:
            sr, tn_kv = lat_norm(7, 4, gkv, lambda j: ckvn[:, j * T:(j + 1) * T], 512.0, (sr, tn_q))
        P.wait('tensor', wtok, tn_q, tn_kv)
        for h in (range(8) if do_q else []):
            for half in range(2):
                hs = slice(half * 512, half * 512 + 512)
                b = banks.acquire()
                for k in range(7):
                    pe_tok = P.op('tensor', lambda e, b=b, k=k, h=h, half=half: e.matmul(
                        banks.banks[b][:, :], wq[:, k * 1536 + 192 * h:k * 1536 + 192 * h + 128],
                        cqn[:, k * T + half * 512:k * T + half * 512 + 512], start=(k == 0), stop=(k == 6)), signal=(k == 6))
                evac_copy_store(b, 128, qk[h, :, hs], pe_tok, scale=SC_A)
                b = banks.acquire()
                for k in range(7):
                    pe_tok = P.op('tensor', lambda e, b=b, k=k, h=h, half=half: e.matmul(
                        banks.banks[b][0:64, :], wq[:, k * 1536 + 192 * h + 128:k * 1536 + 192 * h + 192],
                        cqn[:, k * T + half * 512:k * T + half * 512 + 512], start=(k == 0), stop=(k == 6)), signal=(k == 6))
                rope_store(banks.banks[b][0:64, :], pe_tok, b, 64, RA, cosA, sinA, half, SC_A, r64[h, :, hs])
        for h in (range(8) if do_kv else []):
            for half in range(2):
                hs = slice(half * 512, half * 512 + 512)
                b = banks.acquire()
                for k in range(4):
                    pe_tok = P.op('tensor', lambda e, b=b, k=k, h=h, half=half: e.matmul(
                        banks.banks[b][:, :], wkv[:, k * 1024 + 128 * h:k * 1024 + 128 * h + 128],
                        ckvn[:, k * T + half * 512:k * T + half * 512 + 512], start=(k == 0), stop=(k == 3)), signal=(k == 3))
                evac_copy_store(b, 128, qk[8 + h, :, hs], pe_tok)
        for tt in (range(8) if do_kv else []):
            for vs in range(2):
                b = banks.acquire()
                for k in range(4):
                    pe_tok = P.op('tensor', lambda e, b=b, k=k, tt=tt, vs=vs: e.matmul(
                        banks.banks[b][:, :], ckvn[:, k * T + tt * 128:k * T + tt * 128 + 128],
                        wkv[:, 4096 + k * 1024 + 512 * vs:4096 + k * 1024 + 512 * vs + 512], start=(k == 0), stop=(k == 3)), signal=(k == 3))
                evac_copy_store(b, 128, v[tt * 128:(tt + 1) * 128, 512 * vs:512 * vs + 512], pe_tok)
        for half in (range(2) if do_kv else []):
            hs = slice(half * 512, half * 512 + 512)
            rope_store(lat[11][0:64, hs], ltok_all, None, 64, RA, cosA, sinA, half, 1.0, r64[8, :, hs])
        P.wait('sync', s16.final_toks(), s32.final_toks())
    P.end_phase(cond_core)


S = 8192
NEG = -30000.0

def emit_B(P, h, qkd, r64d, vd, lfd, cbd, sud, trid, od, mixers="ABCD", qchunks=None, cond_core=None):
    P.begin_phase()
    if True:
        bufQ = P.sb("bufQ", [128, S], BF16)
        bufK = P.sb("bufK", [128, S], BF16)
        bufV = P.sb("bufV", [128, 64 * 128], BF16)
        bufX1 = P.sb("bufX1", [128, S], BF16)
        bufX2 = P.sb("bufX2", [128, S], BF16)
        Oacc = P.sb("Oacc", [128, S], F32)
        Lacc = P.sb("Lacc", [128, S], F32)
        p_t = [P.sb(f"pt{i}", [128, 512], BF16) for i in range(4)]
        p2_t = [P.sb(f"p2t{i}", [128, 512], BF16) for i in range(3)]
        f_t = [P.sb(f"ft{i}", [128, 512], F32) for i in range(8)]
        cb = P.sb("cb_sb", [128, 7 * 128], BF16)
        su = P.sb("su_sb", [64, 65], BF16); tri = P.sb("tri_sb", [128, 128], BF16)
        lfr_s = P.sb("lfr_s", [64, 128], F32); lft_s = P.sb("lft_s", [128, 64], F32)
        nbt = P.sb("nbt", [128, 16 * 64], F32)
        ccol = P.sb("ccol", [128, 64], F32); offs = P.sb("offs", [128, 65], F32); p1s = P.sb("p1s", [128, 65], F32)
        spl = [P.sb(f"spl{i}", [128, 128], BF16) for i in range(9)]
        rsd = [P.sb(f"rsd{i}", [128, 128], F32) for i in range(2)]
        dm_t = [P.sb(f"dm{i}", [128, 512], BF16) for i in range(2)]
        bank = P.bank_t
        ident = cb[:, 0:128]; onesb = cb[:, 128:256]; nmU = cb[:, 256:384]; nmL = cb[:, 384:512]
        nmS = cb[:, 512:640]; nTinc = cb[:, 640:768]; zerosb = cb[:, 768:896]
        csem = P.dsem("csem")
        P.dma('sync', cb[:, :], cbd, csem)
        P.dma('gpsimd', su[:, :], sud, csem)
        P.dma('gpsimd', tri[:, :], trid, csem)
        for c_ in range(8):
            P.dma('sync', lfr_s[8 * c_:8 * c_ + 8, :], lfd[c_, h, :].rearrange("(b i) -> b i", i=128), csem)
            ctok = P.dma('sync', lft_s[:, 8 * c_:8 * c_ + 8], lfd[c_, h, :].rearrange("(b i) -> i b", i=128), csem, allow_slow_non_contiguous=True)
        for e in ['scalar', 'vector', 'tensor']:
            P.wait(e, ctok)
        lsem = P.dsem("lsem")
        ost = StoreRing(P, f_t[0:2], "ost")

        def barrier():
            for e in ['scalar', 'vector', 'tensor', 'sync', 'gpsimd']:
                for o in ['scalar', 'vector', 'tensor', 'gpsimd']:
                    if o != e and P.pcnt[o] > 0:
                        P.wait(e, (P.psem[o], P.pcnt[o]))
            for e in ['scalar', 'vector', 'tensor']:
                P.wait(e, ost.final_toks())

        def load(pairs):
            tok = None
            for dst, src in pairs:
                tok = P.dma('sync', dst, src, lsem)
            for e in ['scalar', 'vector', 'tensor']:
                P.wait(e, tok)

        def vpairs(buf, mi):
            c0 = mi * 1024 + h * 128
            prs = []
            for c in range(8):
                prs.append((buf[:, c * 1024:(c + 1) * 1024].rearrange("p (b d) -> p b d", d=128),
                            vd[c, :, c0:c0 + 128].rearrange("(b p) d -> p b d", p=128)))
            return prs

        def vdil(buf, dl):
            c0 = 1024 + h * 128
            nbk = S // dl // 128
            src = vd.rearrange("c t f -> (c t) f")[:, c0:c0 + 128].rearrange("(n i r) d -> r i n d", r=dl, i=128)
            prs = []
            for r in range(dl):
                prs.append((buf[:, r * nbk * 128:(r + 1) * nbk * 128].rearrange("p (n d) -> p n d", d=128), src[r]))
            return prs

        def cols(qc, kb):
            j = kb - 4 * qc
            c0 = 0 if j < 0 else 128 * j
            return j, c0

        def run_AD(mi, is_A, nb_fn=None, dm_fn=None):
            sb = Banks(P, bank[0:3]); ob = Banks(P, bank[3:5]); lb = Banks(P, bank[5:7])
            pr = TileRing(p_t)
            steps = [(qc, kb) for qc in (qchunks if qchunks is not None else range(16)) for kb in range(4 * qc + 4)]
            LA = 2
            st_ = {}
            cur = {}

            def qk(i):
                qc, kb = steps[i]
                j, c0 = cols(qc, kb)
                b = sb.acquire()
                qs = slice(qc * 512 + c0, qc * 512 + 512)
                ks = slice(kb * 128, kb * 128 + 128)
                extra = []
                if is_A:
                    extra.append((bufX2[0:64, ks], bufX1[0:64, qs], slice(c0, 512)))
                else:
                    dmi = dm_fn(qc)
                    extra.append((onesb, dm_t[dmi][:, c0:512], slice(c0, 512)))
                if j >= 0:
                    extra.append((ident, nmU, slice(c0, c0 + 128)))
                P.op('tensor', lambda e: e.matmul(bank[b][:, c0:512], bufK[:, ks], bufQ[:, qs], start=True, stop=False), signal=False)
                t = None
                for n_, (l_, r_, cs_) in enumerate(extra):
                    last = n_ == len(extra) - 1
                    t = P.op('tensor', lambda e, l_=l_, r_=r_, cs_=cs_, last=last: e.matmul(bank[b][:, cs_], l_, r_, start=False, stop=last), signal=last)
                pi = pr.next()
                P.wait('scalar', t, pr.free[pi])
                if is_A:
                    te = P.op('scalar', lambda e: e.activation(out=p_t[pi][:, c0:512], in_=bank[b][:, c0:512], func=AF.Exp))
                else:
                    bias = nb_fn(qc, kb)
                    te = P.op('scalar', lambda e: e.activation(out=p_t[pi][:, c0:512], in_=bank[b][:, c0:512], func=AF.Exp, bias=bias, scale=1.0))
                sb.release(b, te)
                st_[i] = (pi, te)

            def pv(i):
                qc, kb = steps[i]
                j, c0 = cols(qc, kb)
                pi, te = st_.pop(i)
                first = kb == 0
                last = kb == 4 * qc + 3
                if first:
                    cur['o'] = ob.acquire(); cur['l'] = lb.acquire()
                o_, l_ = cur['o'], cur['l']
                P.wait('tensor', te)
                P.op('tensor', lambda e: e.matmul(bank[3 + o_][:, c0:512], bufV[:, kb * 128:kb * 128 + 128], p_t[pi][:, c0:512], start=first, stop=last), signal=False)
                t = P.op('tensor', lambda e: e.matmul(bank[5 + l_][:, c0:512], onesb, p_t[pi][:, c0:512], start=first, stop=last))
                pr.free[pi] = t
                if last:
                    P.wait('vector', t, cur.get('rlw'))
                    t1 = P.op('vector', lambda e: e.reciprocal(out=f_t[2][:, :], in_=bank[5 + l_][:, :]))
                    lb.release(l_, t1)
                    slot = ost.acquire('vector')
                    P.wait('vector', t1)
                    t2 = P.op('vector', lambda e: e.tensor_tensor(out=ost.tiles[slot], in0=bank[3 + o_][:, :], in1=f_t[2][:, :], op=ALU.mult))
                    ob.release(o_, t2)
                    cur['rlw'] = t2
                    ost.store(slot, od[mi, h, :, qc * 512:qc * 512 + 512], t2)

            for i in range(len(steps) + LA):
                if i < len(steps):
                    qk(i)
                if i >= LA:
                    pv(i - LA)

        if 'A' in mixers:
            load([(bufQ[:, :].rearrange("p (c t) -> p c t", c=8), qkd[:, h].rearrange("c p t -> p c t")), (bufK[:, :].rearrange("p (c t) -> p c t", c=8), qkd[:, 8 + h].rearrange("c p t -> p c t")),
                  (bufX1[0:64, :].rearrange("p (c t) -> p c t", c=8), r64d[:, h].rearrange("c p t -> p c t")), (bufX2[0:64, :].rearrange("p (c t) -> p c t", c=8), r64d[:, 8].rearrange("c p t -> p c t"))]
                 + vpairs(bufV, 0))
            run_AD(0, True)
            barrier()

        if 'D' in mixers:
            load([(bufQ[:, :].rearrange("p (c t) -> p c t", c=8), qkd[:, 48 + h].rearrange("c p t -> p c t")), (bufK[:, :].rearrange("p (c t) -> p c t", c=8), qkd[:, 56 + h].rearrange("c p t -> p c t"))] + vpairs(bufV, 3))
            def split3(src, npart, ncol, base):
                hi = spl[base][0:npart, 0:ncol]; mid = spl[base + 1][0:npart, 0:ncol]; lo = spl[base + 2][0:npart, 0:ncol]
                r1 = rsd[0][0:npart, 0:ncol]; r2 = rsd[1][0:npart, 0:ncol]
                t = P.op('vector', lambda e: e.tensor_copy(out=hi, in_=src))
                P.wait('vector', t)
                t = P.op('vector', lambda e: e.tensor_tensor(out=r1, in0=src, in1=hi, op=ALU.subtract))
                P.wait('vector', t)
                t = P.op('vector', lambda e: e.tensor_copy(out=mid, in_=r1))
                P.wait('vector', t)
                t = P.op('vector', lambda e: e.tensor_tensor(out=r2, in0=r1, in1=mid, op=ALU.subtract))
                P.wait('vector', t)
                t = P.op('vector', lambda e: e.tensor_copy(out=lo, in_=r2))
                return [hi, mid, lo], t
            lfr3, t_a = split3(lfr_s[:, :], 64, 128, 0)
            P.wait('vector', t_a)
            lft3, t_b = split3(lft_s[:, :], 128, 64, 3)
            P.wait('tensor', t_a, t_b)
            for n_, a_ in enumerate(lfr3):
                t = P.op('tensor', lambda e, a_=a_, n_=n_: e.matmul(bank[0][:, 0:65], a_, su[:, :], start=(n_ == 0), stop=(n_ == 2)), signal=(n_ == 2))
            P.wait('vector', t, t_b)
            t = P.op('vector', lambda e: e.tensor_copy(out=p1s[:, :], in_=bank[0][:, 0:65]))
            P.wait('vector', t)
            p13, t_c = split3(p1s[:, :], 128, 65, 6)
            P.wait('tensor', t_c)
            for n_, a_ in enumerate(p13):
                t = P.op('tensor', lambda e, a_=a_, n_=n_: e.matmul(bank[1][:, 0:65], onesb, a_, start=(n_ == 0), stop=(n_ == 2)), signal=(n_ == 2))
            for n_, a_ in enumerate(p13):
                P.op('tensor', lambda e, a_=a_, n_=n_: e.matmul(bank[2][:, 0:64], onesb, a_[:, 0:64], start=(n_ == 0), stop=False), signal=False)
            for n_, a_ in enumerate(lft3):
                t = P.op('tensor', lambda e, a_=a_, n_=n_: e.matmul(bank[2][:, 0:64], tri[:, :], a_, start=False, stop=(n_ == 2)), signal=(n_ == 2))
            P.wait('vector', t)
            P.op('vector', lambda e: e.tensor_copy(out=offs[:, :], in_=bank[1][:, 0:65]))
            t = P.op('vector', lambda e: e.tensor_copy(out=ccol[:, :], in_=bank[2][:, 0:64]))
            P.wait('vector', t)
            for qc in range(16):
                t = P.op('vector', lambda e, qc=qc: e.tensor_scalar(out=nbt[:, qc * 64:(qc + 1) * 64], in0=ccol[:, :], scalar1=-1.0,
                                                                    scalar2=offs[:, 4 * qc + 4:4 * qc + 5], op0=ALU.mult, op1=ALU.add))
            P.wait('scalar', t)
            P.wait('tensor', t)
            dmr = TileRing(dm_t)
            dm_state = {}

            def dm_fn(qc):
                if qc not in dm_state:
                    di = dmr.next()
                    P.wait('vector', dmr.free[di], t)
                    tt = None
                    for j in range(4):
                        idx = qc * 64 + 4 * qc + j
                        tt = P.op('vector', lambda e, j=j, idx=idx, di=di: e.tensor_scalar(
                            out=dm_t[di][:, 128 * j:128 * j + 128], in0=ident, scalar1=nbt[:, idx:idx + 1], scalar2=-1.0, op0=ALU.mult, op1=ALU.mult))
                    P.wait('tensor', tt)
                    dm_state[qc] = di
                    if qc >= 1:
                        pass
                return dm_state[qc]
            orig_dm_fn = dm_fn
            snap = {}

            def dm_fn2(qc):
                if qc not in dm_state and dm_state:
                    prev = dm_state[max(dm_state)]
                    dmr.free[prev] = (P.psem['tensor'], P.pcnt['tensor'])
                return orig_dm_fn(qc)

            run_AD(3, False, nb_fn=lambda qc, kb: nbt[:, qc * 64 + kb:qc * 64 + kb + 1], dm_fn=dm_fn2)
            barrier()

        if 'C' in mixers:
            load([(bufQ[:, :].rearrange("p (c t) -> p c t", c=8), qkd[:, 32 + h].rearrange("c p t -> p c t")), (bufK[:, :].rearrange("p (c t) -> p c t", c=8), qkd[:, 40 + h].rearrange("c p t -> p c t"))] + vpairs(bufV, 2))
            zb = Banks(P, bank[0:4]); tb = Banks(P, bank[4:6]); ob = Banks(P, bank[6:8])
            et = TileRing(f_t[3:5]); at = TileRing(f_t[5:7]); Rb = f_t[7]
            spr = TileRing(p2_t); pr = TileRing(p_t)
            steps = [(qc, kb) for qc in (qchunks if qchunks is not None else range(16)) for kb in range(4 * qc + 3, -1, -1)]
            sa = {}; sbb = {}; sc = {}
            cur = {}
            rb_tok = [None]

            def stA(i):
                qc, kb = steps[i]
                j, c0 = cols(qc, kb)
                b = zb.acquire()
                qs = slice(qc * 512 + c0, qc * 512 + 512)
                ks = slice(kb * 128, kb * 128 + 128)
                if j >= 0:
                    P.op('tensor', lambda e: e.matmul(bank[b][:, c0:512], bufK[:, ks], bufQ[:, qs], start=True, stop=False), signal=False)
                    t = P.op('tensor', lambda e: e.matmul(bank[b][:, c0:c0 + 128], ident, nmS, start=False, stop=True))
                else:
                    t = P.op('tensor', lambda e: e.matmul(bank[b][:, c0:512], bufK[:, ks], bufQ[:, qs], start=True, stop=True))
                sa[i] = (b, t)

            def stB(i):
                qc, kb = steps[i]
                j, c0 = cols(qc, kb)
                b, t = sa.pop(i)
                ei = et.next()
                P.wait('scalar', t, et.free[ei])
                t1 = P.op('scalar', lambda e: e.activation(out=et.tiles[ei][:, c0:512], in_=bank[b][:, c0:512], func=AF.Exp))
                si = spr.next()
                P.wait('scalar', t1, spr.free[si])
                t2 = P.op('scalar', lambda e: e.activation(out=p2_t[si][:, c0:512], in_=et.tiles[ei][:, c0:512], func=AF.Ln, bias=1.0, scale=1.0))
                et.free[ei] = t2
                P.wait('tensor', t2)
                t3 = P.op('tensor', lambda e: e.matmul(bank[b][:, c0:512], nTinc, p2_t[si][:, c0:512], start=False, stop=True))
                tbk = tb.acquire()
                t4 = P.op('tensor', lambda e: e.matmul(bank[4 + tbk][:, c0:512], onesb, p2_t[si][:, c0:512], start=True, stop=True))
                spr.free[si] = t4
                sbb[i] = (b, tbk, t3, t4)

            def stC(i):
                qc, kb = steps[i]
                j, c0 = cols(qc, kb)
                b, tbk, t3, t4 = sbb.pop(i)
                first = kb == 4 * qc + 3
                pi = pr.next()
                if first:
                    P.wait('scalar', t3, pr.free[pi])
                    t6 = P.op('scalar', lambda e: e.activation(out=p_t[pi][:, c0:512], in_=bank[b][:, c0:512], func=AF.Exp))
                    zb.release(b, t6)
                    P.wait('vector', t4, rb_tok[0])
                    tm = P.op('vector', lambda e: e.memset(Rb[:, :], 0.0))
                    P.wait('vector', tm)
                    t7 = P.op('vector', lambda e: e.tensor_copy(out=Rb[:, c0:512], in_=bank[4 + tbk][:, c0:512]))
                    rb_tok[0] = t7
                    tb.release(tbk, t7)
                else:
                    ai = at.next()
                    P.wait('vector', t3, at.free[ai], rb_tok[0])
                    t5 = P.op('vector', lambda e: e.tensor_tensor(out=at.tiles[ai][:, c0:512], in0=bank[b][:, c0:512], in1=Rb[:, c0:512], op=ALU.subtract))
                    zb.release(b, t5)
                    P.wait('vector', t4, t5)
                    t7 = P.op('vector', lambda e: e.tensor_tensor(out=Rb[:, c0:512], in0=Rb[:, c0:512], in1=bank[4 + tbk][:, c0:512], op=ALU.add))
                    rb_tok[0] = t7
                    tb.release(tbk, t7)
                    P.wait('scalar', t5, pr.free[pi])
                    t6 = P.op('scalar', lambda e: e.activation(out=p_t[pi][:, c0:512], in_=at.tiles[ai][:, c0:512], func=AF.Exp))
                    at.free[ai] = t6
                sc[i] = (pi, t6)

            def stD(i):
                qc, kb = steps[i]
                j, c0 = cols(qc, kb)
                pi, t6 = sc.pop(i)
                first = kb == 4 * qc + 3
                last = kb == 0
                if first:
                    cur['o'] = ob.acquire()
                    o_ = cur['o']
                    P.op('tensor', lambda e: e.matmul(bank[6 + o_][:, :], zerosb, bufK[:, 0:512], start=True, stop=False), signal=False)
                o_ = cur['o']
                P.wait('tensor', t6)
                t = P.op('tensor', lambda e: e.matmul(bank[6 + o_][:, c0:512], bufV[:, kb * 128:kb * 128 + 128], p_t[pi][:, c0:512], start=False, stop=last))
                pr.free[pi] = t
                if last:
                    slot = ost.acquire('vector')
                    P.wait('vector', t)
                    t2 = P.op('vector', lambda e: e.tensor_copy(out=ost.tiles[slot], in_=bank[6 + o_][:, :]))
                    ob.release(o_, t2)
                    ost.store(slot, od[2, h, :, qc * 512:qc * 512 + 512], t2)

            ns = len(steps)
            for i in range(ns + 3):
                if i < ns:
                    stA(i)
                if 0 <= i - 1 < ns:
                    stB(i - 1)
                if 0 <= i - 2 < ns:
                    stC(i - 2)
                if 0 <= i - 3 < ns:
                    stD(i - 3)
            barrier()

        if 'B' in mixers:
            load([(bufQ[:, :].rearrange("p (c t) -> p c t", c=8), qkd[:, 16 + h].rearrange("c p t -> p c t")), (bufK[:, :].rearrange("p (c t) -> p c t", c=8), qkd[:, 24 + h].rearrange("c p t -> p c t"))] + vpairs(bufV, 1) + vdil(bufX1, 4) + vdil(bufX2, 16))
            ss = Banks(P, bank[0:2]); spv = Banks(P, bank[2:4]); ob = Banks(P, bank[4:6]); lb = Banks(P, bank[6:8])
            prs = TileRing(p_t[0:2]); prp = TileRing(p_t[2:4])
            groups = []
            for dl, vb, nbk in [(1, bufV, 64), (4, bufX1, 16), (16, bufX2, 4)]:
                for r in range(dl):
                    for n0 in range(0, nbk, 4):
                        groups.append((dl, vb, nbk, r, n0))
            st_ = {}
            acc_tok = [None]

            def sub(buf, dl, r, n):
                s0 = r + dl * 128 * n
                return buf[:, s0:s0 + dl * 127 + 1:dl]

            def g1(i):
                dl, vb, nbk, r, n0 = groups[i]
                bs = ss.acquire(); bp = spv.acquire()
                t = None
                for j in range(4):
                    n = n0 + j
                    P.op('tensor', lambda e, j=j, n=n: e.matmul(bank[bs][:, 128 * j:128 * j + 128], sub(bufK, dl, r, n), sub(bufQ, dl, r, n), start=True, stop=False), signal=False)
                    t = P.op('tensor', lambda e, j=j: e.matmul(bank[bs][:, 128 * j:128 * j + 128], ident, nmU, start=False, stop=True), signal=(j == 3))
                tp_ = None
                for j in range(4):
                    n = n0 + j
                    if n == 0:
                        continue
                    P.op('tensor', lambda e, j=j, n=n: e.matmul(bank[2 + bp][:, 128 * j:128 * j + 128], sub(bufK, dl, r, n - 1), sub(bufQ, dl, r, n), start=True, stop=False), signal=False)
                    tp_ = P.op('tensor', lambda e, j=j: e.matmul(bank[2 + bp][:, 128 * j:128 * j + 128], ident, nmL, start=False, stop=True), signal=(j == 3))
                pc0 = 128 if n0 == 0 else 0
                ps_ = prs.next(); pp_ = prp.next()
                P.wait('scalar', t, prs.free[ps_])
                te1 = P.op('scalar', lambda e: e.activation(out=p_t[ps_][:, :], in_=bank[bs][:, :], func=AF.Exp))
                ss.release(bs, te1)
                P.wait('scalar', tp_, prp.free[pp_])
                te2 = P.op('scalar', lambda e: e.activation(out=p_t[2 + pp_][:, pc0:512], in_=bank[2 + bp][:, pc0:512], func=AF.Exp))
                spv.release(bp, te2)
                st_[i] = (ps_, pp_, te1, te2)

            def g2(i):
                dl, vb, nbk, r, n0 = groups[i]
                ps_, pp_, te1, te2 = st_.pop(i)
                o_ = ob.acquire(); l_ = lb.acquire()
                P.wait('tensor', te1, te2)
                t = None
                for j in range(4):
                    n = n0 + j
                    cs = slice(128 * j, 128 * j + 128)
                    vs = lambda nn: vb[:, (r * nbk + nn) * 128:(r * nbk + nn) * 128 + 128]
                    if n > 0:
                        P.op('tensor', lambda e, cs=cs, n=n, vs=vs: e.matmul(bank[4 + o_][:, cs], vs(n - 1), p_t[2 + pp_][:, cs], start=True, stop=False), signal=False)
                        P.op('tensor', lambda e, cs=cs, n=n, vs=vs: e.matmul(bank[4 + o_][:, cs], vs(n), p_t[ps_][:, cs], start=False, stop=True), signal=False)
                        P.op('tensor', lambda e, cs=cs: e.matmul(bank[6 + l_][:, cs], onesb, p_t[2 + pp_][:, cs], start=True, stop=False), signal=False)
                        t = P.op('tensor', lambda e, cs=cs: e.matmul(bank[6 + l_][:, cs], onesb, p_t[ps_][:, cs], start=False, stop=True))
                    else:
                        P.op('tensor', lambda e, cs=cs, n=n, vs=vs: e.matmul(bank[4 + o_][:, cs], vs(n), p_t[ps_][:, cs], start=True, stop=True), signal=False)
                        t = P.op('tensor', lambda e, cs=cs: e.matmul(bank[6 + l_][:, cs], onesb, p_t[ps_][:, cs], start=True, stop=True))
                prs.free[ps_] = t
                prp.free[pp_] = t
                s0 = r + dl * 128 * n0
                asl = slice(s0, s0 + dl * 511 + 1, dl)
                P.wait('vector', t, acc_tok[0])
                if dl == 1:
                    t1 = P.op('vector', lambda e: e.tensor_copy(out=Oacc[:, asl], in_=bank[4 + o_][:, :]))
                    t2 = P.op('vector', lambda e: e.tensor_copy(out=Lacc[:, asl], in_=bank[6 + l_][:, :]))
                else:
                    t1 = P.op('vector', lambda e: e.tensor_tensor(out=Oacc[:, asl], in0=Oacc[:, asl], in1=bank[4 + o_][:, :], op=ALU.add))
                    t2 = P.op('vector', lambda e: e.tensor_tensor(out=Lacc[:, asl], in0=Lacc[:, asl], in1=bank[6 + l_][:, :], op=ALU.add))
                ob.release(o_, t1); lb.release(l_, t2)
                acc_tok[0] = t2

            LA = 1
            for i in range(len(groups) + LA):
                if i < len(groups):
                    g1(i)
                if i >= LA:
                    g2(i - LA)
            P.wait('vector', acc_tok[0])
            for c in range(16):
                csl = slice(c * 512, c * 512 + 512)
                t1 = P.op('vector', lambda e, csl=csl: e.reciprocal(out=Lacc[:, csl], in_=Lacc[:, csl]))
                slot = ost.acquire('vector')
                P.wait('vector', t1)
                t2 = P.op('vector', lambda e, csl=csl, slot=slot: e.tensor_tensor(out=ost.tiles[slot], in0=Oacc[:, csl], in1=Lacc[:, csl], op=ALU.mult))
                ost.store(slot, od[1, h, :, csl], t2)
            barrier()
        P.wait('sync', ost.final_toks())
    P.end_phase(cond_core)


THETA = 500000.0
def rope_tables(pos):
    pos = pos.astype(np.float32)
    invA = (THETA ** (-np.arange(32, dtype=np.float32) * (2.0 / 64))).astype(np.float32)
    angA = pos[None, :] * invA[:, None]
    cosA = np.concatenate([np.cos(angA), np.cos(angA)], 0).astype(np.float32)
    sinA = np.concatenate([np.sin(angA), np.sin(angA)], 0).astype(np.float32)
    invB = (THETA ** (-np.arange(16, dtype=np.float32) * (2.0 / 32))).astype(np.float32)
    angB = pos[None, :] * invB[:, None]
    T = pos.shape[0]
    cosB = np.ones((128, T), np.float32); sinB = np.zeros((128, T), np.float32)
    cosB[0:16] = np.cos(angB); cosB[16:32] = np.cos(angB)
    sinB[0:16] = np.sin(angB); sinB[16:32] = np.sin(angB)
    return cosA, sinA, cosB, sinB
def rot_mats():
    RA = np.zeros((64, 64), np.float32)
    for i in range(32):
        RA[i + 32, i] = -1.0
        RA[i, i + 32] = 1.0
    RB = np.zeros((128, 128), np.float32)
    for i in range(16):
        RB[i + 16, i] = -1.0
        RB[i, i + 16] = 1.0
    return RA, RB
def lay(g, nk):
    return np.ascontiguousarray(g.reshape(nk, 128).T)


NSH = 8
DEPTH = 2


def build_fused():
    nc = bass.Bass("TRN2", target_bir_lowering=False)
    din = lambda n, s, dt=F32: nc.dram_tensor(n, s, dt, kind="ExternalInput").ap()
    xin = din("xin", [NSH, D, T])
    w_in = din("w_in", [DEPTH, D, INC]); gatt = din("gatt", [DEPTH, 128, 32]); gq = din("gq", [DEPTH, 128, 7]); gkv = din("gkv", [DEPTH, 128, 4])
    w_uq = din("w_uq", [DEPTH, 896, 1536]); w_uk = din("w_uk", [DEPTH, 512, 1024]); w_uv = din("w_uv", [DEPTH, 512, 1024])
    bfd = din("bf", [DEPTH, 8, 1])
    gout = din("gout", [DEPTH, 128, 32]); w_o = din("w_o", [DEPTH, D, D]); gmlp = din("gmlp", [DEPTH, 128, 32])
    w_up = din("w_up", [DEPTH, D, FF]); w_down = din("w_down", [DEPTH, FF, D]); gfin = din("gfin", [128, 32])
    cosA = din("cosA", [NSH, 64, T]); sinA = din("sinA", [NSH, 64, T]); cosB = din("cosB", [NSH, 128, T]); sinB = din("sinB", [NSH, 128, T])
    RA = din("RA", [64, 64]); RB = din("RB", [128, 128]); ones = din("ones", [128, 128])
    cbd = din("cb", [128, 7 * 128], BF16); sud = din("su", [64, 65], BF16); trid = din("tri", [128, 128], BF16)
    yT = nc.dram_tensor("yT", [D, T], F32, kind="ExternalOutput").ap()
    X1 = nc.dram_tensor("X1", [NSH, D, T], F32).ap()
    qkd = nc.dram_tensor("qkd", [NSH, 64, 128, T], BF16).ap()
    r64d = nc.dram_tensor("r64d", [NSH, 9, 64, T], BF16).ap()
    vd = nc.dram_tensor("vd", [NSH, T, 4096], BF16).ap()
    lfd = nc.dram_tensor("lfd", [NSH, 8, T], F32).ap()
    od = nc.dram_tensor("od", [4, 8, 128, S], F32).ap()
    latd = nc.dram_tensor("latd", [12, 128, T], F32).ap()
    x1d = nc.dram_tensor("x1d", [D, T], F32).ap()
    own_o = nc.dram_tensor("own_o", [4, 8, 128, T], F32).ap()
    own_x = nc.dram_tensor("own_x", [D, T], F32).ap()
    accd = nc.dram_tensor("accd", [D, T], F32).ap()
    with ExitStack() as es:
        P = Prog(nc, es)
        for l in range(DEPTH):
            Xi = xin if l == 0 else X1
            Xo = X1
            for c in range(NSH):
                a_args = (Xi[c], w_in[l], gatt[l], gq[l], gkv[l], w_uq[l], w_uk[l], w_uv[l], bfd[l],
                          cosA[c], sinA[c], cosB[c], sinB[c], RA, RB, ones, qkd[c], r64d[c], vd[c], lfd[c], latd)
                if l < DEPTH - 1:
                    emit_A(P, *a_args)
                else:
                    emit_A(P, *a_args, which='kv')
                    emit_A(P, *a_args, which='q', cond_core=c)
            for h in range(8):
                if l < DEPTH - 1:
                    emit_B(P, h, qkd, r64d, vd, lfd, cbd, sud, trid, od)
                else:
                    emit_B(P, h, qkd, r64d, vd, lfd, cbd, sud, trid, od, mixers="B")
                    for c in range(NSH):
                        emit_B(P, h, qkd, r64d, vd, lfd, cbd, sud, trid, od, mixers="ADC", qchunks=[2 * c, 2 * c + 1], cond_core=c)
            if l < DEPTH - 1:
                for c in range(NSH):
                    emit_C(P, False, (lambda k, c=c: od[k // 8, k % 8, :, c * T:(c + 1) * T]), Xi[c],
                           gout[l], gmlp[l], gfin, w_o[l], w_up[l], w_down[l], ones, Xo[c], x1d, accd)
            else:
                P.begin_phase()
                gsem = P.dsem("gsem")
                gsem.count += 16 * 8

                def gather(e):
                    pid = P.pid_of(e, 'sync')
                    for m_ in range(4):
                        e.dma_start(out=own_o[m_].rearrange("h p t -> (h p) t"),
                                    in_=od[m_, :, :, bass.ts(pid, T)].rearrange("h p t -> (h p) t")).then_inc(gsem.h, 16)
                    xsrc = Xi[bass.ts(pid, 1)].rearrange("o d t -> d (o t)")
                    for r_ in range(4):
                        e.dma_start(out=own_x[r_ * 1024:(r_ + 1) * 1024, :], in_=xsrc[r_ * 1024:(r_ + 1) * 1024, :]).then_inc(gsem.h, 16)
                P.ops['sync'].append(gather)
                P.wait('sync', (gsem.h, gsem.count))
                P.end_phase()
                emit_C(P, True, (lambda k: own_o[k // 8, k % 8, :, :]), own_x, gout[l], gmlp[l], gfin, w_o[l], w_up[l], w_down[l], ones, yT, x1d, accd)
        n_instr = P.n_instr
    return nc, n_instr


import ml_dtypes
_BF = ml_dtypes.bfloat16
_FUSED = {}


def _b_consts():
    p = np.arange(128)[:, None]
    f = np.arange(128)[None, :]
    ident = (p == f).astype(np.float32)
    ones = np.ones((128, 128), np.float32)
    nmU = np.where(p > f, NEG, 0.0)
    nmL = np.where(p < f, NEG, 0.0)
    nmS = np.where(p >= f, NEG, 0.0)
    nTinc = np.where(p >= f, -1.0, 0.0)
    zeros = np.zeros((128, 128))
    cb = np.concatenate([ident, ones, nmU, nmL, nmS, nTinc, zeros], 1).astype(np.float32)
    su = (np.arange(64)[:, None] < np.arange(65)[None, :]).astype(np.float32)
    tri = (p <= f).astype(np.float32)
    return cb.astype(_BF), su.astype(_BF), tri.astype(_BF)


def kernel(x, g_attn, w_in, g_q, g_kv, w_uq, w_uk, w_uv, b_f, g_out, w_o, g_mlp, w_up, w_down, g_final):
    f32 = lambda a: np.ascontiguousarray(np.asarray(a, dtype=np.float32))
    if 'nc' not in _FUSED:
        _FUSED['nc'], _ = build_fused()
    nc = _FUSED['nc']
    x = f32(x)
    RA, RB = rot_mats()
    cb, su, tri = _b_consts()
    tabs = [rope_tables(c * T + np.arange(T)) for c in range(NSH)]
    layn = lambda g, nk: np.stack([lay(f32(g[l]), nk) for l in range(DEPTH)])
    ins = dict(
        xin=np.ascontiguousarray(x[0].reshape(NSH, T, D).transpose(0, 2, 1)),
        w_in=f32(w_in), gatt=layn(g_attn, 32), gq=layn(g_q, 7), gkv=layn(g_kv, 4),
        w_uq=f32(w_uq), w_uk=f32(w_uk), w_uv=f32(w_uv), bf=f32(b_f).reshape(DEPTH, 8, 1),
        gout=layn(g_out, 32), w_o=f32(w_o), gmlp=layn(g_mlp, 32), w_up=f32(w_up), w_down=f32(w_down), gfin=lay(f32(g_final), 32),
        cosA=np.stack([t[0] for t in tabs]), sinA=np.stack([t[1] for t in tabs]),
        cosB=np.stack([t[2] for t in tabs]), sinB=np.stack([t[3] for t in tabs]),
        RA=RA, RB=RB, ones=np.ones((128, 128), np.float32), cb=cb, su=su, tri=tri)
    res = run_bass_kernel_spmd(nc, [ins for _ in range(NSH)], core_ids=list(range(NSH)))
    out = np.concatenate([res.results[c]["yT"].T for c in range(NSH)], axis=0)
    return np.ascontiguousarray(out).reshape(1, S, D).astype(np.float32)
```

```python
import numpy as np
from contextlib import ExitStack
import concourse.bass as bass
import concourse.mybir as mybir
from concourse.bass_utils import run_bass_kernel_spmd

F32 = mybir.dt.float32
BF16 = mybir.dt.bfloat16
AF = mybir.ActivationFunctionType
ALU = mybir.AluOpType
ENGS = ['sync', 'scalar', 'vector', 'gpsimd', 'tensor']
EPS = 1e-6


class DSem:
    def __init__(self, h):
        self.h = h
        self.count = 0


class Prog:
    def __init__(self, nc, es):
        self.nc = nc
        self.es = es
        self.ops = {e: [] for e in ENGS}
        self.psem = {}
        self.pcnt = {e: 0 for e in ENGS}
        self.waited = {}
        self.nsem = 0
        for e in ['scalar', 'vector', 'gpsimd', 'tensor']:
            self.psem[e] = es.enter_context(nc.semaphore(f"p_{e}"))
        self.bank_t = [es.enter_context(nc.psum_tensor(f"bank{i}", [128, 512], F32)) for i in range(8)]
        self.sem_pool = []
        self.phase_sems = []
        self.phase_es = None
        self.phase_id = 0
        self.n_instr = 0

    def dsem(self, name):
        if self.sem_pool:
            d = self.sem_pool.pop()
        else:
            d = DSem(self.es.enter_context(self.nc.semaphore(f"ds{self.nsem}")))
            self.nsem += 1
        self.phase_sems.append(d)
        return d

    def sb(self, name, shape, dt):
        return self.phase_es.enter_context(self.nc.sbuf_tensor(f"{name}_p{self.phase_id}", shape, dt))

    def pid_of(self, e, name):
        if not hasattr(self, '_pid'):
            self._pid = {}
        if name not in self._pid:
            self._pid[name] = e.partition_id()
        return self._pid[name]

    def begin_phase(self):
        self.phase_es = ExitStack()
        self.phase_id += 1
        self.snap_p = dict(self.pcnt)
        for d in self.sem_pool:
            d.byq = {}

    def barrier_all(self):
        for e in ENGS:
            for o in ['scalar', 'vector', 'gpsimd', 'tensor']:
                if o != e and self.pcnt[o] > 0:
                    self.wait(e, (self.psem[o], self.pcnt[o]))
            for d in self.phase_sems:
                if d.count > 0:
                    self.wait(e, (d.h, d.count))

    def end_phase(self, cond_core=None):
        self.barrier_all()
        self.build(cond_core)
        for k in self.ops:
            self.n_instr += len(self.ops[k])
            self.ops[k] = []
        self.phase_es.close()
        self.phase_es = None
        self.sem_pool.extend(self.phase_sems)
        self.phase_sems = []

    def ps(self, name, shape, dt=F32):
        return self.es.enter_context(self.nc.psum_tensor(name, shape, dt))

    def op(self, eng, fn, signal=True):
        if signal:
            self.pcnt[eng] += 1
            s = self.psem[eng]
            self.ops[eng].append(lambda e, fn=fn, s=s: fn(e).then_inc(s, 1))
            return (s, self.pcnt[eng])
        self.ops[eng].append(fn)
        return None

    def wait(self, eng, *toks):
        for tok in toks:
            if tok is None:
                continue
            if isinstance(tok, list):
                self.wait(eng, *tok)
                continue
            s, v = tok
            if eng in self.psem and s is self.psem[eng]:
                pass
            key = (eng, id(s))
            if self.waited.get(key, 0) >= v:
                continue
            self.waited[key] = v
            self.ops[eng].append(lambda e, s=s, v=v: e.wait_ge(s, v))

    def dma(self, queue, out, in_, ds, **kw):
        ds.count += 16
        ds.byq = getattr(ds, 'byq', {})
        ds.byq[queue] = ds.byq.get(queue, 0) + 16
        self.ops[queue].append(lambda e, out=out, in_=in_, h=ds.h, kw=kw: e.dma_start(
            out=(out(e) if callable(out) else out), in_=(in_(e) if callable(in_) else in_), **kw).then_inc(h, 16))
        return (ds.h, ds.count)

    def build(self, cond_core=None):
        with self.nc.Block() as block:
            for name in ENGS:
                ops = self.ops[name]
                if not ops:
                    continue
                comp = []
                if cond_core is not None:
                    if name in self.psem and self.pcnt[name] > self.snap_p[name]:
                        comp.append((self.psem[name], self.pcnt[name] - self.snap_p[name]))
                    for d in self.phase_sems:
                        n = getattr(d, 'byq', {}).get(name, 0)
                        if n:
                            comp.append((d.h, n))

                def body(e, ops=ops, comp=comp):
                    if cond_core is None:
                        for f in ops:
                            f(e)
                    else:
                        pid = self.pid_of(e, name)
                        with e.If(pid == cond_core):
                            for f in ops:
                                f(e)
                        with e.Else():
                            for s, n in comp:
                                e.sem_inc(s, n)
                getattr(block, name)(body)


class Banks:
    def __init__(self, P, banks):
        self.P = P
        self.banks = banks
        self.free = [None] * len(banks)
        self.n = 0

    def acquire(self):
        i = self.n % len(self.banks)
        self.n += 1
        self.P.wait('tensor', self.free[i])
        return i

    def release(self, i, *toks):
        self.free[i] = list(toks)


class TileRing:
    def __init__(self, tiles):
        self.tiles = tiles
        self.free = [None] * len(tiles)
        self.n = 0

    def next(self):
        i = self.n % len(self.tiles)
        self.n += 1
        return i


class SlabRing:
    def __init__(self, P, tiles, queue='gpsimd'):
        self.P = P
        self.tiles = tiles
        self.sems = [P.dsem(f"slab_sem{i}") for i in range(len(tiles))]
        self.free = [None] * len(tiles)
        self.n = 0
        self.queue = queue

    def load(self, W, r0, kc, c0, cw, kgroup=8):
        P = self.P
        slot = self.n % len(self.tiles)
        self.n += 1
        P.wait(self.queue, self.free[slot])
        t = self.tiles[slot]
        src = W[r0:r0 + kc * 128, c0:c0 + cw].rearrange("(k p) c -> p k c", p=128)
        tok = None
        for k0 in range(0, kc, kgroup):
            k1 = min(kc, k0 + kgroup)
            dst = t[:, k0 * cw:k1 * cw].rearrange("p (k c) -> p k c", c=cw)
            tok = P.dma(self.queue, dst, src[:, k0:k1, :], self.sems[slot])
        return slot, tok

    def release(self, slot, tok):
        self.free[slot] = tok


T = 1024
D = 4096
FF = 16384


class LoadRing:
    def __init__(self, P, tiles, name, queue='sync'):
        self.P = P
        self.tiles = [t if type(t).__name__ == 'AP' else t[:, :] for t in tiles]
        self.sems = [P.dsem(f"{name}_s{i}") for i in range(len(tiles))]
        self.free = [None] * len(tiles)
        self.n = 0
        self.queue = queue

    def load(self, src):
        P = self.P
        slot = self.n % len(self.tiles)
        self.n += 1
        P.wait(self.queue, self.free[slot])
        tok = P.dma(self.queue, self.tiles[slot], src, self.sems[slot])
        return slot, tok

    def release(self, slot, *toks):
        self.free[slot] = list(toks)


class StoreRing:
    def __init__(self, P, tiles, name, queue='sync'):
        self.P = P
        self.tiles = [t if type(t).__name__ == 'AP' else t[:, :] for t in tiles]
        self.sems = [P.dsem(f"{name}_s{i}") for i in range(len(tiles))]
        self.free = [None] * len(tiles)
        self.n = 0
        self.queue = queue

    def acquire(self, eng):
        slot = self.n % len(self.tiles)
        self.n += 1
        self.P.wait(eng, self.free[slot])
        return slot

    def store(self, slot, dst, after, extra=(), src=None):
        P = self.P
        P.wait(self.queue, after)
        tok = P.dma(self.queue, dst, self.tiles[slot] if src is None else src, self.sems[slot])
        self.free[slot] = [tok] + list(extra)
        return tok

    def final_toks(self):
        return [(s.h, s.count) for s in self.sems if s.count > 0]


def xrows(xT, k):
    if callable(xT):
        return xT(k)
    return xT[k * 128:(k + 1) * 128, :]

def emit_C(P, final, oT_tile, xT, goutd, gmlpd, gfind, w_o, w_up, w_down, onesd, yT, x1d, accd):
    P.begin_phase()
    acc_t = accd if final else yT
    acc_tok = {}
    if True:
        actT = P.sb("actT", [128, 32 * T], BF16)
        ubuf = P.sb("ubuf", [128, 16 * T], BF16)
        slab_t = [P.sb(f"slab{i}", [128, 16384], BF16) for i in range(2)]
        tp = [P.sb(f"tp{i}", [128, T], F32) for i in range(6)]
        stg_t = [P.sb(f"stg{i}", [128, 512], F32) for i in range(4)]
        rl_t = [P.sb(f"rl{i}", [128, 512], F32) for i in range(2)]
        rstd = P.sb("rstd", [128, T], F32)
        ones = P.sb("ones_sb", [128, 128], F32)
        gout = P.sb("gout_sb", [128, 32], F32)
        gmlp = P.sb("gmlp_sb", [128, 32], F32)
        gfin = P.sb("gfin_sb", [128, 32], F32)
        bank_t = P.bank_t
        st = bank_t[0:2]
        banks = Banks(P, bank_t[2:8])
        slabs = SlabRing(P, slab_t)
        csem = P.dsem("csem")
        P.dma('sync', ones[:, :], onesd, csem)
        P.dma('sync', gout[:, :], goutd, csem)
        P.dma('sync', gmlp[:, :], gmlpd, csem)
        ctok = P.dma('sync', gfin[:, :], gfind, csem)
        for e in ['scalar', 'vector', 'tensor']:
            P.wait(e, ctok)

        def act(k, half=None):
            if half is None:
                return actT[:, k * T:(k + 1) * T]
            return actT[:, k * T + half * 512:k * T + half * 512 + 512]

        ofp = [ubuf[:, 2 * j * T:2 * (j + 1) * T].bitcast(F32) for j in range(8)]
        oring = LoadRing(P, ofp, "oring")
        xring = LoadRing(P, tp[0:3], "xring")
        sqr = TileRing(tp[3:5])
        stg = StoreRing(P, stg_t, "stg")

        slab_tasks = []
        for s in range(8):
            slab_tasks.append(('o', w_o, 0, 32, 512 * s, 512))
        for fb in range(8):
            for s in range(4):
                slab_tasks.append(('u', w_up, 0, 32, fb * 2048 + 512 * s, 512))
            for s in range(4):
                slab_tasks.append(('d', w_down, fb * 2048, 16, 1024 * s, 1024))
        slab_loaded = {}

        def prefetch(i):
            if i < len(slab_tasks) and i not in slab_loaded:
                _, W, r0, kc, c0, cw = slab_tasks[i]
                slab_loaded[i] = slabs.load(W, r0, kc, c0, cw)

        prefetch(0)

        def rstd_from_stats(n_feat, after_tok, war_toks):
            P.wait('scalar', after_tok, war_toks)
            ta = None
            for half in range(2):
                ta = P.op('scalar', lambda e, half=half: e.activation(
                    out=rstd[:, half * 512:half * 512 + 512], in_=st[half][:, :], func=AF.Sqrt,
                    scale=1.0 / n_feat, bias=EPS))
            P.wait('vector', ta)
            tv = P.op('vector', lambda e: e.reciprocal(out=rstd[:, :], in_=rstd[:, :]))
            return ta, tv

        act_tok = {}
        last_norm_tok = None
        st_read_tok = None
        for gi in range(4):
            ltoks = []
            for j in range(8):
                k = 8 * gi + j
                slot, tok = oring.load(oT_tile(k))
                ltoks.append(tok)
            pe_tok = None
            for j in range(8):
                P.wait('scalar', ltoks[j])
                si = sqr.next()
                P.wait('scalar', sqr.free[si])
                ta = P.op('scalar', lambda e, j=j, si=si: e.activation(out=sqr.tiles[si][:, :], in_=ofp[j], func=AF.Square))
                P.wait('tensor', ta)
                if j == 0:
                    P.wait('tensor', st_read_tok)
                for half in range(2):
                    pe_tok = P.op('tensor', lambda e, j=j, si=si, half=half: e.matmul(
                        st[half][:, :], ones[:, :], sqr.tiles[si][:, half * 512:half * 512 + 512],
                        start=(j == 0), stop=(j == 7)), signal=(half == 1))
                sqr.free[si] = pe_tok
            st_read_tok, tv = rstd_from_stats(1024.0, pe_tok, last_norm_tok)
            P.wait('vector', tv)
            for j in range(8):
                k = 8 * gi + j
                P.wait('vector', ltoks[j])
                last_norm_tok = P.op('vector', lambda e, j=j, k=k: e.scalar_tensor_tensor(
                    out=act(k), in0=ofp[j], scalar=gout[:, k:k + 1], in1=rstd[:, :], op0=ALU.mult, op1=ALU.mult))
                oring.release(j, last_norm_tok)
                act_tok[k] = last_norm_tok

        pending = None
        x1tok = {}
        xload = {}

        def xprefetch(oc):
            if oc < 32 and oc not in xload:
                xload[oc] = xring.load(xrows(xT, oc))

        xprefetch(0)
        xprefetch(1)
        ti = 0
        stat_tok = None
        for s in range(8):
            prefetch(ti + 1)
            slot, stok = slab_loaded[ti]
            P.wait('tensor', stok)
            sl = slab_t[slot]
            pe_tok = None
            for ocl in range(4):
                oc = 4 * s + ocl
                xprefetch(oc + 2)
                xslot, xtok = xload[oc]
                dtoks = []
                for half in range(2):
                    b = banks.acquire()
                    for k in range(32):
                        P.wait('tensor', act_tok[k])
                        pe_tok = P.op('tensor', lambda e, b=b, k=k, ocl=ocl, half=half, sl=sl: e.matmul(
                            banks.banks[b][:, :], sl[:, k * 512 + ocl * 128:k * 512 + ocl * 128 + 128], act(k, half),
                            start=(k == 0), stop=(k == 31)), signal=(k == 31))
                    if pending is not None:
                        pending()
                        pending = None
                    sslot = stg.acquire('vector')
                    P.wait('vector', pe_tok, xtok)
                    tv = P.op('vector', lambda e, b=b, sslot=sslot, xslot=xslot, half=half: e.tensor_tensor(
                        out=stg_t[sslot][:, :], in0=banks.banks[b][:, :],
                        in1=xring.tiles[xslot][:, half * 512:half * 512 + 512], op=ALU.add))
                    banks.release(b, tv)
                    dtoks.append(tv)
                    si = sqr.next()
                    P.wait('scalar', tv, sqr.free[si])
                    ta = P.op('scalar', lambda e, si=si, sslot=sslot: e.activation(
                        out=sqr.tiles[si][:, 0:512], in_=stg_t[sslot][:, :], func=AF.Square))
                    x1tok[(oc, half)] = stg.store(sslot, x1d[oc * 128:(oc + 1) * 128, half * 512:half * 512 + 512], tv, extra=[ta])
                    acc_tok[(oc, half)] = P.dma('sync', acc_t[oc * 128:(oc + 1) * 128, half * 512:half * 512 + 512], stg.tiles[sslot], stg.sems[sslot])
                    stg.free[sslot].append(acc_tok[(oc, half)])

                    def mk(si=si, half=half, oc=oc, ta=ta):
                        def f():
                            nonlocal stat_tok
                            P.wait('tensor', ta)
                            stat_tok = P.op('tensor', lambda e: e.matmul(
                                st[half][:, :], ones[:, :], sqr.tiles[si][:, 0:512], start=(oc == 0), stop=(oc == 31)))
                            sqr.free[si] = stat_tok
                        return f
                    if oc == 0:
                        P.wait('tensor', st_read_tok)
                    pending = mk()
                xring.release(xslot, *dtoks)
            slabs.release(slot, pe_tok)
            ti += 1
        pending()
        pending = None
        st_read_tok, tv = rstd_from_stats(4096.0, stat_tok, last_norm_tok)

        P.wait('vector', tv, stat_tok)
        xload2 = {}

        def x1prefetch(k):
            if k < 32 and k not in xload2:
                P.wait('sync', x1tok[(k, 0)], x1tok[(k, 1)])
                xload2[k] = xring.load(x1d[k * 128:(k + 1) * 128, :])
        x1prefetch(0)
        x1prefetch(1)
        for k in range(32):
            x1prefetch(k + 2)
            xslot, xtok = xload2[k]
            P.wait('vector', xtok)
            last_norm_tok = P.op('vector', lambda e, k=k, xslot=xslot: e.scalar_tensor_tensor(
                out=act(k), in0=xring.tiles[xslot][:, :], scalar=gmlp[:, k:k + 1], in1=rstd[:, :], op0=ALU.mult, op1=ALU.mult))
            xring.release(xslot, last_norm_tok)
            act_tok[k] = last_norm_tok

        rlr = TileRing(rl_t)
        down_last_pe = None
        nev = 0
        for fb in range(8):
            u_last = None
            for s in range(4):
                prefetch(ti + 1)
                slot, stok = slab_loaded[ti]
                P.wait('tensor', stok)
                sl = slab_t[slot]
                pe_tok = None
                for fl in range(4):
                    ffc = 4 * s + fl
                    for half in range(2):
                        b = banks.acquire()
                        for k in range(32):
                            P.wait('tensor', act_tok[k])
                            pe_tok = P.op('tensor', lambda e, b=b, k=k, fl=fl, half=half, sl=sl: e.matmul(
                                banks.banks[b][:, :], sl[:, k * 512 + fl * 128:k * 512 + fl * 128 + 128], act(k, half),
                                start=(k == 0), stop=(k == 31)), signal=(k == 31))
                        ri = rlr.next()
                        P.wait('scalar', pe_tok, rlr.free[ri])
                        ta = P.op('scalar', lambda e, b=b, ri=ri: e.activation(out=rl_t[ri][:, :], in_=banks.banks[b][:, :], func=AF.Relu))
                        banks.release(b, ta)
                        P.wait('vector', ta, down_last_pe)
                        u_last = P.op('vector', lambda e, ri=ri, ffc=ffc, half=half: e.tensor_tensor(
                            out=ubuf[:, ffc * T + half * 512:ffc * T + half * 512 + 512], in0=rl_t[ri][:, :], in1=rl_t[ri][:, :], op=ALU.mult))
                        rlr.free[ri] = u_last
                slabs.release(slot, pe_tok)
                ti += 1
            P.wait('tensor', u_last)
            for s in range(4):
                prefetch(ti + 1)
                slot, stok = slab_loaded[ti]
                P.wait('tensor', stok)
                sl = slab_t[slot]
                pe_tok = None
                for ol in range(8):
                    oc = 8 * s + ol
                    for half in range(2):
                        b = banks.acquire()
                        for k in range(16):
                            pe_tok = P.op('tensor', lambda e, b=b, k=k, ol=ol, half=half, sl=sl: e.matmul(
                                banks.banks[b][:, :], sl[:, k * 1024 + ol * 128:k * 1024 + ol * 128 + 128],
                                ubuf[:, k * T + half * 512:k * T + half * 512 + 512],
                                start=(k == 0), stop=(k == 15)), signal=(k == 15))
                        eng = 'scalar' if nev % 2 == 0 else 'vector'
                        nev += 1
                        sslot = stg.acquire(eng)
                        P.wait(eng, pe_tok)
                        if eng == 'scalar':
                            te = P.op('scalar', lambda e, b=b, sslot=sslot: e.activation(out=stg_t[sslot][:, :], in_=banks.banks[b][:, :], func=AF.Copy))
                        else:
                            te = P.op('vector', lambda e, b=b, sslot=sslot: e.tensor_copy(out=stg_t[sslot][:, :], in_=banks.banks[b][:, :]))
                        banks.release(b, te)
                        P.wait('gpsimd', te, acc_tok[(oc, half)])
                        acc_tok[(oc, half)] = P.dma('gpsimd', acc_t[oc * 128:(oc + 1) * 128, half * 512:half * 512 + 512],
                                                    stg.tiles[sslot], stg.sems[sslot], accum_op=ALU.add)
                        stg.free[sslot] = [acc_tok[(oc, half)]]
                down_last_pe = pe_tok
                slabs.release(slot, pe_tok)
                ti += 1

        P.wait('sync', stg.final_toks())
        osem = P.dsem("osem")
        if final:
            accr = LoadRing(P, tp[3:6], "accr")
            fin_stat = None
            for k in range(32):
                aslot, atok = accr.load(acc_t[k * 128:(k + 1) * 128, :])
                sslot = k % 3
                P.wait('scalar', atok, xring.free[sslot])
                ta = P.op('scalar', lambda e, aslot=aslot, sslot=sslot: e.activation(
                    out=xring.tiles[sslot][:, :], in_=accr.tiles[aslot][:, :], func=AF.Square))
                accr.release(aslot, ta)
                P.wait('tensor', ta)
                if k == 0:
                    P.wait('tensor', st_read_tok)
                for half in range(2):
                    fin_stat = P.op('tensor', lambda e, sslot=sslot, half=half, k=k: e.matmul(
                        st[half][:, :], ones[:, :], xring.tiles[sslot][:, half * 512:half * 512 + 512],
                        start=(k == 0), stop=(k == 31)), signal=(half == 1))
                xring.free[sslot] = [fin_stat]
            st_read_tok, tv = rstd_from_stats(4096.0, fin_stat, last_norm_tok)
            P.wait('vector', tv)
            for k in range(32):
                aslot, atok = accr.load(acc_t[k * 128:(k + 1) * 128, :])
                P.wait('vector', atok)
                tv2 = P.op('vector', lambda e, aslot=aslot, k=k: e.scalar_tensor_tensor(
                    out=accr.tiles[aslot][:, :], in0=accr.tiles[aslot][:, :], scalar=gfin[:, k:k + 1], in1=rstd[:, :],
                    op0=ALU.mult, op1=ALU.mult))
                P.wait('sync', tv2)
                tok = P.dma('sync', yT[k * 128:(k + 1) * 128, :], accr.tiles[aslot][:, :], osem)
                accr.release(aslot, tok)
        P.wait('sync', (osem.h, osem.count))
    P.end_phase()


INC = 10696
SC_A = 192.0 ** -0.5
SC_H = 128.0 ** -0.5

def emit_A(P, xT, w_in, gattd, gqd, gkvd, w_uq, w_uk, w_uv, bfd, cosAd, sinAd, cosBd, sinBd, RAd, RBd, onesd, qk, r64, v, lf, latd, which='all', cond_core=None):
    P.begin_phase()
    if True:
        hT = P.sb("hT", [128, 32 * T], BF16)
        slab_t = [P.sb(f"slab{i}", [128, 16384], BF16) for i in range(2)]
        ckvn = P.sb("ckvn", [128, 4 * T], BF16)
        tp = [P.sb(f"tp{i}", [128, T], F32) for i in range(4)]
        rstd = P.sb("rstd", [128, T], F32)
        cosA = P.sb("cosA_sb", [64, T], F32); sinA = P.sb("sinA_sb", [64, T], F32)
        cosB = P.sb("cosB_sb", [128, T], F32); sinB = P.sb("sinB_sb", [128, T], F32)
        RA = P.sb("RA_sb", [64, 64], F32); RB = P.sb("RB_sb", [128, 128], F32)
        ones = P.sb("ones_sb", [128, 128], F32)
        gatt = P.sb("gatt_sb", [128, 32], F32); gq = P.sb("gq_sb", [128, 7], F32); gkv = P.sb("gkv_sb", [128, 4], F32)
        bfs = P.sb("bf_sb", [8, 1], F32); nbf = P.sb("nbf_sb", [8, 1], F32)
        sb16_t = [P.sb(f"sb16_{i}", [128, 512], BF16) for i in range(4)]
        sf32_t = [P.sb(f"sf32_{i}", [128, 512], F32) for i in range(6)]
        bank_t = P.bank_t
        st = bank_t[0:2]
        banks = Banks(P, bank_t[2:8])
        slabs = SlabRing(P, slab_t)
        csem = P.dsem("csem")
        for dst, src in [(ones, onesd), (gatt, gattd), (gq, gqd), (gkv, gkvd), (bfs, bfd), (cosA, cosAd), (sinA, sinAd),
                         (cosB, cosBd), (sinB, sinBd), (RA, RAd), (RB, RBd)]:
            ctok = P.dma('sync', dst[:, :], src, csem)
        for e in ['scalar', 'vector', 'tensor']:
            P.wait(e, ctok)
        tnb = P.op('vector', lambda e: e.tensor_scalar(out=nbf[:, :], in0=bfs[:, :], scalar1=-1.0, scalar2=None, op0=ALU.mult))

        def hch(k, a=0, n=T):
            return hT[:, k * T + a:k * T + a + n]

        xring = LoadRing(P, tp[0:2], "xring")
        sqr = TileRing(tp[2:4])
        s16 = StoreRing(P, sb16_t, "s16")
        s32 = StoreRing(P, sf32_t[0:2], "s32")
        tfr = TileRing(sf32_t[2:4])
        abr = TileRing(sf32_t[4:6])

        tasks = [('lat', 0, 512, 0), ('lat', 512, 384, 4), ('lat', 896, 512, 7), ('kr', 1408, 64, 11)]
        for mi, mname in enumerate('BCD'):
            base = 1472 + 3072 * mi
            for part in range(2):
                tasks.append(('q' + mname, base + 512 * part, 512, 16 + 16 * mi + 4 * part))
            for part in range(2):
                tasks.append(('k' + mname, base + 1024 + 512 * part, 512, 24 + 16 * mi + 4 * part))
            for part in range(2):
                tasks.append(('v' + mname, base + 2048 + 512 * part, 512, 1024 * (mi + 1) + 512 * part))
        tasks.append(('pf', 10688, 8, 0))
        if which == 'kv':
            tasks = [t_ for t_ in tasks if not (t_[0] in ('qC', 'qD') or (t_[0] == 'lat' and t_[1] < 896))]
        elif which == 'q':
            tasks = [t_ for t_ in tasks if (t_[0] in ('qC', 'qD') or (t_[0] == 'lat' and t_[1] < 896))]
        do_q = which in ('all', 'q')
        do_kv = which in ('all', 'kv')
        loaded = {}

        def prefetch(i):
            if i < len(tasks) and i not in loaded:
                _, c0, cw, _ = tasks[i]
                loaded[i] = slabs.load(w_in, 0, 32, c0, cw)
        prefetch(0)

        def rstd_from_stats(n_feat, after_tok, war_toks):
            P.wait('scalar', after_tok, war_toks)
            ta = None
            for half in range(2):
                ta = P.op('scalar', lambda e, half=half: e.activation(
                    out=rstd[:, half * 512:half * 512 + 512], in_=st[half][:, :], func=AF.Sqrt,
                    scale=1.0 / n_feat, bias=EPS))
            P.wait('vector', ta)
            tv = P.op('vector', lambda e: e.reciprocal(out=rstd[:, :], in_=rstd[:, :]))
            return ta, tv

        pe_tok = None
        for k in range(32):
            xs, xtok = xring.load(xT[k * 128:(k + 1) * 128, :])
            si = sqr.next()
            P.wait('scalar', xtok, sqr.free[si])
            ta = P.op('scalar', lambda e, xs=xs, si=si: e.activation(out=sqr.tiles[si][:, :], in_=xring.tiles[xs], func=AF.Square))
            xring.release(xs, ta)
            P.wait('tensor', ta)
            for half in range(2):
                pe_tok = P.op('tensor', lambda e, si=si, half=half, k=k: e.matmul(
                    st[half][:, :], ones[:, :], sqr.tiles[si][:, half * 512:half * 512 + 512],
                    start=(k == 0), stop=(k == 31)), signal=(half == 1))
            sqr.free[si] = pe_tok
        st_read_tok, tv = rstd_from_stats(4096.0, pe_tok, None)
        P.wait('vector', tv)
        last_norm = None
        norm_toks = []
        for k in range(32):
            xs, xtok = xring.load(xT[k * 128:(k + 1) * 128, :])
            P.wait('vector', xtok)
            last_norm = P.op('vector', lambda e, xs=xs, k=k: e.scalar_tensor_tensor(
                out=hch(k), in0=xring.tiles[xs], scalar=gatt[:, k:k + 1], in1=rstd[:, :], op0=ALU.mult, op1=ALU.mult))
            xring.release(xs, last_norm)
            norm_toks.append(last_norm)

        nev = [0]

        def evac_copy_store(b, npart, dst, pe_tok, scale=None, f32=False):
            ring = s32 if f32 else s16
            eng = 'scalar' if nev[0] % 2 == 0 else 'vector'
            nev[0] += 1
            slot = ring.acquire(eng)
            P.wait(eng, pe_tok)
            o = ring.tiles[slot][0:npart, :]
            i = banks.banks[b][0:npart, :]
            if eng == 'scalar':
                te = P.op('scalar', lambda e: e.activation(out=o, in_=i, func=AF.Copy, scale=(1.0 if scale is None else scale)))
            else:
                if scale is None:
                    te = P.op('vector', lambda e: e.tensor_copy(out=o, in_=i))
                else:
                    te = P.op('vector', lambda e: e.tensor_scalar(out=o, in0=i, scalar1=scale, scalar2=None, op0=ALU.mult))
            banks.release(b, te)
            ring.store(slot, dst, te, src=o)
            return te

        def rope_store(src_ap, src_tok, src_is_psum_bank, npart, Rm, cosT, sinT, half, scale, dst):
            ti_ = tfr.next()
            tft = tfr.tiles[ti_][0:npart, :]
            if src_is_psum_bank is not None:
                P.wait('scalar', src_tok, tfr.free[ti_])
                tcp = P.op('scalar', lambda e: e.activation(out=tft, in_=src_ap, func=AF.Copy, scale=scale))
                banks.release(src_is_psum_bank, tcp)
            else:
                P.wait('scalar', src_tok, tfr.free[ti_])
                tcp = P.op('scalar', lambda e: e.activation(out=tft, in_=src_ap, func=AF.Copy, scale=scale))
            b2 = banks.acquire()
            P.wait('tensor', tcp)
            trot = P.op('tensor', lambda e: e.matmul(banks.banks[b2][0:npart, :], Rm[:, :], tft, start=True, stop=True))
            ai = abr.next()
            bi = abr.next()
            at = abr.tiles[ai][0:npart, :]
            bt = abr.tiles[bi][0:npart, :]
            cs = cosT[0:npart, half * 512:half * 512 + 512]
            sn = sinT[0:npart, half * 512:half * 512 + 512]
            P.wait('vector', tcp, abr.free[ai])
            ta_ = P.op('vector', lambda e: e.tensor_tensor(out=at, in0=tft, in1=cs, op=ALU.mult))
            P.wait('vector', trot, abr.free[bi])
            tb_ = P.op('vector', lambda e: e.tensor_tensor(out=bt, in0=banks.banks[b2][0:npart, :], in1=sn, op=ALU.mult))
            banks.release(b2, tb_)
            slot = s16.acquire('vector')
            o = s16.tiles[slot][0:npart, :]
            P.wait('vector', ta_, tb_)
            to = P.op('vector', lambda e: e.tensor_tensor(out=o, in0=at, in1=bt, op=ALU.add))
            tfr.free[ti_] = [trot, ta_]
            abr.free[ai] = to
            abr.free[bi] = to
            s16.store(slot, dst, to, src=o)

        lat_tok = {}
        for ti, (kind, c0, cw, dsti) in enumerate(tasks):
            prefetch(ti + 1)
            slot, stok = loaded[ti]
            P.wait('tensor', stok)
            sl = slab_t[slot]
            pe_tok = None
            if kind[0] == 'v':
                for tt in range(8):
                    b = banks.acquire()
                    for k in range(32):
                        P.wait('tensor', norm_toks[k])
                        pe_tok = P.op('tensor', lambda e, b=b, k=k, tt=tt, sl=sl: e.matmul(
                            banks.banks[b][:, :], hch(k, tt * 128, 128), sl[:, k * 512:k * 512 + 512],
                            start=(k == 0), stop=(k == 31)), signal=(k == 31))
                    evac_copy_store(b, 128, v[tt * 128:(tt + 1) * 128, dsti:dsti + 512], pe_tok)
            elif kind == 'pf':
                for half in range(2):
                    b = banks.acquire()
                    for k in range(32):
                        P.wait('tensor', norm_toks[k])
                        pe_tok = P.op('tensor', lambda e, b=b, k=k, half=half, sl=sl: e.matmul(
                            banks.banks[b][0:8, :], sl[:, k * 8:k * 8 + 8], hch(k, half * 512, 512),
                            start=(k == 0), stop=(k == 31)), signal=(k == 31))
                    ti_ = tfr.next()
                    e1 = tfr.tiles[ti_][0:8, :]
                    P.wait('scalar', pe_tok, tfr.free[ti_], tnb)
                    t1 = P.op('scalar', lambda e, b=b, e1=e1: e.activation(out=e1, in_=banks.banks[b][0:8, :], func=AF.Exp, scale=-1.0, bias=nbf[:, 0:1]))
                    banks.release(b, t1)
                    P.wait('scalar', t1)
                    t2 = P.op('scalar', lambda e, e1=e1: e.activation(out=e1, in_=e1, func=AF.Ln, scale=1.0, bias=1.0))
                    slot2 = s32.acquire('vector')
                    o = s32.tiles[slot2][0:8, :]
                    P.wait('vector', t2)
                    t3 = P.op('vector', lambda e, o=o, e1=e1: e.tensor_scalar(out=o, in0=e1, scalar1=-1.0, scalar2=None, op0=ALU.mult))
                    tfr.free[ti_] = t3
                    s32.store(slot2, lf[:, half * 512:half * 512 + 512], t3, src=o)
            else:
                ntile = (cw + 127) // 128
                for tl in range(ntile):
                    m = min(128, cw - tl * 128)
                    for half in range(2):
                        b = banks.acquire()
                        for k in range(32):
                            P.wait('tensor', norm_toks[k])
                            pe_tok = P.op('tensor', lambda e, b=b, k=k, tl=tl, half=half, sl=sl, m=m, cw=cw: e.matmul(
                                banks.banks[b][0:m, :], sl[:, k * cw + tl * 128:k * cw + tl * 128 + m], hch(k, half * 512, 512),
                                start=(k == 0), stop=(k == 31)), signal=(k == 31))
                        hs = slice(half * 512, half * 512 + 512)
                        if kind in ('lat', 'kr'):
                            evac_copy_store(b, m, latd[dsti + tl, 0:m, hs], pe_tok, f32=True)
                        elif kind in ('qB', 'kB'):
                            rope_store(banks.banks[b][:, :], pe_tok, b, 128, RB, cosB, sinB, half,
                                       SC_H if kind == 'qB' else 1.0, qk[dsti + tl, :, hs])
                        else:
                            evac_copy_store(b, 128, qk[dsti + tl, :, hs], pe_tok, scale=(SC_H if kind[0] == 'q' else None))
            slabs.release(slot, pe_tok)
        last_inproj_pe = pe_tok

        P.wait('gpsimd', last_inproj_pe)
        wsem = P.dsem("wsem")
        wq = slab_t[0]
        wkv = slab_t[1]
        wtok = None
        if do_q:
            wtok = P.dma('gpsimd', wq[:, 0:7 * 1536].rearrange("p (k c) -> p k c", c=1536), w_uq.rearrange("(k p) c -> p k c", p=128), wsem)
        if do_kv:
            P.dma('gpsimd', wkv[:, 0:4096].rearrange("p (k c) -> p k c", c=1024), w_uk.rearrange("(k p) c -> p k c", p=128), wsem)
            wtok = P.dma('gpsimd', wkv[:, 4096:8192].rearrange("p (k c) -> p k c", c=1024), w_uv.rearrange("(k p) c -> p k c", p=128), wsem)
        P.wait('sync', s32.final_toks(), last_inproj_pe)
        lat = [hT[:, 2 * j * T:2 * (j + 1) * T].bitcast(F32) for j in range(12)]
        cqn = hT[:, 24 * T:31 * T]
        lsem = P.dsem("lsem")
        ltok = []
        for j in range(12):
            if (j < 7 and not do_q) or (j >= 7 and not do_kv):
                continue
            npart = 64 if j == 11 else 128
            ltok.append(P.dma('sync', lat[j][0:npart, :], latd[j, 0:npart, :], lsem))
        ltok_all = ltok[-1]

        def lat_norm(j0, n, gv, dst_fn, nfeat, war):
            pe = None
            for j in range(n):
                si = sqr.next()
                P.wait('scalar', ltok_all, sqr.free[si])
                ta = P.op('scalar', lambda e, j=j, si=si: e.activation(out=sqr.tiles[si][:, :], in_=lat[j0 + j], func=AF.Square))
                P.wait('tensor', ta)
                if j == 0:
                    P.wait('tensor', war[0])
                for half in range(2):
                    pe = P.op('tensor', lambda e, si=si, half=half, j=j: e.matmul(
                        st[half][:, :], ones[:, :], sqr.tiles[si][:, half * 512:half * 512 + 512],
                        start=(j == 0), stop=(j == n - 1)), signal=(half == 1))
                sqr.free[si] = pe
            sr, tv = rstd_from_stats(nfeat, pe, war[1])
            P.wait('vector', tv)
            tn = None
            for j in range(n):
                tn = P.op('vector', lambda e, j=j: e.scalar_tensor_tensor(
                    out=dst_fn(j), in0=lat[j0 + j], scalar=gv[:, j:j + 1], in1=rstd[:, :], op0=ALU.mult, op1=ALU.mult))
            return sr, tn

        sr, tn_q, tn_kv = st_read_tok, last_norm, None
        if do_q:
            sr, tn_q = lat_norm(0, 7, gq, lambda j: cqn[:, j * T:(j + 1) * T], 896.0, (st_read_tok, last_norm))
        if do_kv:
            sr, tn_kv = lat_norm(7, 4, gkv, lambda j: ckvn[:, j * T:(j + 1) * T], 512.0, (sr, tn_q))
        P.wait('tensor', wtok, tn_q, tn_kv)
        for h in (range(8) if do_q else []):
            for half in range(2):
                hs = slice(half * 512, half * 512 + 512)
                b = banks.acquire()
                for k in range(7):
                    pe_tok = P.op('tensor', lambda e, b=b, k=k, h=h, half=half: e.matmul(
                        banks.banks[b][:, :], wq[:, k * 1536 + 192 * h:k * 1536 + 192 * h + 128],
                        cqn[:, k * T + half * 512:k * T + half * 512 + 512], start=(k == 0), stop=(k == 6)), signal=(k == 6))
                evac_copy_store(b, 128, qk[h, :, hs], pe_tok, scale=SC_A)
                b = banks.acquire()
                for k in range(7):
                    pe_tok = P.op('tensor', lambda e, b=b, k=k, h=h, half=half: e.matmul(
                        banks.banks[b][0:64, :], wq[:, k * 1536 + 192 * h + 128:k * 1536 + 192 * h + 192],
                        cqn[:, k * T + half * 512:k * T + half * 512 + 512], start=(k == 0), stop=(k == 6)), signal=(k == 6))
                rope_store(banks.banks[b][0:64, :], pe_tok, b, 64, RA, cosA, sinA, half, SC_A, r64[h, :, hs])
        for h in (range(8) if do_kv else []):
            for half in range(2):
                hs = slice(half * 512, half * 512 + 512)
                b = banks.acquire()
                for k in range(4):
                    pe_tok = P.op('tensor', lambda e, b=b, k=k, h=h, half=half: e.matmul(
                        banks.banks[b][:, :], wkv[:, k * 1024 + 128 * h:k * 1024 + 128 * h + 128],
                        ckvn[:, k * T + half * 512:k * T + half * 512 + 512], start=(k == 0), stop=(k == 3)), signal=(k == 3))
                evac_copy_store(b, 128, qk[8 + h, :, hs], pe_tok)
        for tt in (range(8) if do_kv else []):
            for vs in range(2):
                b = banks.acquire()
                for k in range(4):
                    pe_tok = P.op('tensor', lambda e, b=b, k=k, tt=tt, vs=vs: e.matmul(
                        banks.banks[b][:, :], ckvn[:, k * T + tt * 128:k * T + tt * 128 + 128],
                        wkv[:, 4096 + k * 1024 + 512 * vs:4096 + k * 1024 + 512 * vs + 512], start=(k == 0), stop=(k == 3)), signal=(k == 3))
                evac_copy_store(b, 128, v[tt * 128:(tt + 1) * 128, 512 * vs:512 * vs + 512], pe_tok)
        for half in (range(2) if do_kv else []):
            hs = slice(half * 512, half * 512 + 512)
            rope_store(lat[11][0:64, hs], ltok_all, None, 64, RA, cosA, sinA, half, 1.0, r64[8, :, hs])
        P.wait('sync', s16.final_toks(), s32.final_toks())
    P.end_phase(cond_core)


S = 8192
NEG = -30000.0

def emit_B(P, h, qkd, r64d, vd, lfd, cbd, sud, trid, od, mixers="ABCD", qchunks=None, cond_core=None):
    P.begin_phase()
    if True:
        bufQ = P.sb("bufQ", [128, S], BF16)
        bufK = P.sb("bufK", [128, S], BF16)
        bufV = P.sb("bufV", [128, 64 * 128], BF16)
        bufX1 = P.sb("bufX1", [128, S], BF16)
        bufX2 = P.sb("bufX2", [128, S], BF16)
        Oacc = P.sb("Oacc", [128, S], F32)
        Lacc = P.sb("Lacc", [128, S], F32)
        p_t = [P.sb(f"pt{i}", [128, 512], BF16) for i in range(4)]
        p2_t = [P.sb(f"p2t{i}", [128, 512], BF16) for i in range(3)]
        f_t = [P.sb(f"ft{i}", [128, 512], F32) for i in range(8)]
        cb = P.sb("cb_sb", [128, 7 * 128], BF16)
        su = P.sb("su_sb", [64, 65], BF16); tri = P.sb("tri_sb", [128, 128], BF16)
        lfr_s = P.sb("lfr_s", [64, 128], F32); lft_s = P.sb("lft_s", [128, 64], F32)
        nbt = P.sb("nbt", [128, 16 * 64], F32)
        ccol = P.sb("ccol", [128, 64], F32); offs = P.sb("offs", [128, 65], F32); p1s = P.sb("p1s", [128, 65], F32)
        spl = [P.sb(f"spl{i}", [128, 128], BF16) for i in range(9)]
        rsd = [P.sb(f"rsd{i}", [128, 128], F32) for i in range(2)]
        dm_t = [P.sb(f"dm{i}", [128, 512], BF16) for i in range(2)]
        bank = P.bank_t
        ident = cb[:, 0:128]; onesb = cb[:, 128:256]; nmU = cb[:, 256:384]; nmL = cb[:, 384:512]
        nmS = cb[:, 512:640]; nTinc = cb[:, 640:768]; zerosb = cb[:, 768:896]
        csem = P.dsem("csem")
        P.dma('sync', cb[:, :], cbd, csem)
        P.dma('gpsimd', su[:, :], sud, csem)
        P.dma('gpsimd', tri[:, :], trid, csem)
        for c_ in range(8):
            P.dma('sync', lfr_s[8 * c_:8 * c_ + 8, :], lfd[c_, h, :].rearrange("(b i) -> b i", i=128), csem)
            ctok = P.dma('sync', lft_s[:, 8 * c_:8 * c_ + 8], lfd[c_, h, :].rearrange("(b i) -> i b", i=128), csem, allow_slow_non_contiguous=True)
        for e in ['scalar', 'vector', 'tensor']:
            P.wait(e, ctok)
        lsem = P.dsem("lsem")
        ost = StoreRing(P, f_t[0:2], "ost")

        def barrier():
            for e in ['scalar', 'vector', 'tensor', 'sync', 'gpsimd']:
                for o in ['scalar', 'vector', 'tensor', 'gpsimd']:
                    if o != e and P.pcnt[o] > 0:
                        P.wait(e, (P.psem[o], P.pcnt[o]))
            for e in ['scalar', 'vector', 'tensor']:
                P.wait(e, ost.final_toks())

        def load(pairs):
            tok = None
            for dst, src in pairs:
                tok = P.dma('sync', dst, src, lsem)
            for e in ['scalar', 'vector', 'tensor']:
                P.wait(e, tok)

        def vpairs(buf, mi):
            c0 = mi * 1024 + h * 128
            prs = []
            for c in range(8):
                prs.append((buf[:, c * 1024:(c + 1) * 1024].rearrange("p (b d) -> p b d", d=128),
                            vd[c, :, c0:c0 + 128].rearrange("(b p) d -> p b d", p=128)))
            return prs

        def vdil(buf, dl):
            c0 = 1024 + h * 128
            nbk = S // dl // 128
            src = vd.rearrange("c t f -> (c t) f")[:, c0:c0 + 128].rearrange("(n i r) d -> r i n d", r=dl, i=128)
            prs = []
            for r in range(dl):
                prs.append((buf[:, r * nbk * 128:(r + 1) * nbk * 128].rearrange("p (n d) -> p n d", d=128), src[r]))
            return prs

        def cols(qc, kb):
            j = kb - 4 * qc
            c0 = 0 if j < 0 else 128 * j
            return j, c0

        def run_AD(mi, is_A, nb_fn=None, dm_fn=None):
            sb = Banks(P, bank[0:3]); ob = Banks(P, bank[3:5]); lb = Banks(P, bank[5:7])
            pr = TileRing(p_t)
            steps = [(qc, kb) for qc in (qchunks if qchunks is not None else range(16)) for kb in range(4 * qc + 4)]
            LA = 2
            st_ = {}
            cur = {}

            def qk(i):
                qc, kb = steps[i]
                j, c0 = cols(qc, kb)
                b = sb.acquire()
                qs = slice(qc * 512 + c0, qc * 512 + 512)
                ks = slice(kb * 128, kb * 128 + 128)
                extra = []
                if is_A:
                    extra.append((bufX2[0:64, ks], bufX1[0:64, qs], slice(c0, 512)))
                else:
                    dmi = dm_fn(qc)
                    extra.append((onesb, dm_t[dmi][:, c0:512], slice(c0, 512)))
                if j >= 0:
                    extra.append((ident, nmU, slice(c0, c0 + 128)))
                P.op('tensor', lambda e: e.matmul(bank[b][:, c0:512], bufK[:, ks], bufQ[:, qs], start=True, stop=False), signal=False)
                t = None
                for n_, (l_, r_, cs_) in enumerate(extra):
                    last = n_ == len(extra) - 1
                    t = P.op('tensor', lambda e, l_=l_, r_=r_, cs_=cs_, last=last: e.matmul(bank[b][:, cs_], l_, r_, start=False, stop=last), signal=last)
                pi = pr.next()
                P.wait('scalar', t, pr.free[pi])
                if is_A:
                    te = P.op('scalar', lambda e: e.activation(out=p_t[pi][:, c0:512], in_=bank[b][:, c0:512], func=AF.Exp))
                else:
                    bias = nb_fn(qc, kb)
                    te = P.op('scalar', lambda e: e.activation(out=p_t[pi][:, c0:512], in_=bank[b][:, c0:512], func=AF.Exp, bias=bias, scale=1.0))
                sb.release(b, te)
                st_[i] = (pi, te)

            def pv(i):
                qc, kb = steps[i]
                j, c0 = cols(qc, kb)
                pi, te = st_.pop(i)
                first = kb == 0
                last = kb == 4 * qc + 3
                if first:
                    cur['o'] = ob.acquire(); cur['l'] = lb.acquire()
                o_, l_ = cur['o'], cur['l']
                P.wait('tensor', te)
                P.op('tensor', lambda e: e.matmul(bank[3 + o_][:, c0:512], bufV[:, kb * 128:kb * 128 + 128], p_t[pi][:, c0:512], start=first, stop=last), signal=False)
                t = P.op('tensor', lambda e: e.matmul(bank[5 + l_][:, c0:512], onesb, p_t[pi][:, c0:512], start=first, stop=last))
                pr.free[pi] = t
                if last:
                    P.wait('vector', t, cur.get('rlw'))
                    t1 = P.op('vector', lambda e: e.reciprocal(out=f_t[2][:, :], in_=bank[5 + l_][:, :]))
                    lb.release(l_, t1)
                    slot = ost.acquire('vector')
                    P.wait('vector', t1)
                    t2 = P.op('vector', lambda e: e.tensor_tensor(out=ost.tiles[slot], in0=bank[3 + o_][:, :], in1=f_t[2][:, :], op=ALU.mult))
                    ob.release(o_, t2)
                    cur['rlw'] = t2
                    ost.store(slot, od[mi, h, :, qc * 512:qc * 512 + 512], t2)

            for i in range(len(steps) + LA):
                if i < len(steps):
                    qk(i)
                if i >= LA:
                    pv(i - LA)

        if 'A' in mixers:
            load([(bufQ[:, :].rearrange("p (c t) -> p c t", c=8), qkd[:, h].rearrange("c p t -> p c t")), (bufK[:, :].rearrange("p (c t) -> p c t", c=8), qkd[:, 8 + h].rearrange("c p t -> p c t")),
                  (bufX1[0:64, :].rearrange("p (c t) -> p c t", c=8), r64d[:, h].rearrange("c p t -> p c t")), (bufX2[0:64, :].rearrange("p (c t) -> p c t", c=8), r64d[:, 8].rearrange("c p t -> p c t"))]
                 + vpairs(bufV, 0))
            run_AD(0, True)
            barrier()

        if 'D' in mixers:
            load([(bufQ[:, :].rearrange("p (c t) -> p c t", c=8), qkd[:, 48 + h].rearrange("c p t -> p c t")), (bufK[:, :].rearrange("p (c t) -> p c t", c=8), qkd[:, 56 + h].rearrange("c p t -> p c t"))] + vpairs(bufV, 3))
            def split3(src, npart, ncol, base):
                hi = spl[base][0:npart, 0:ncol]; mid = spl[base + 1][0:npart, 0:ncol]; lo = spl[base + 2][0:npart, 0:ncol]
                r1 = rsd[0][0:npart, 0:ncol]; r2 = rsd[1][0:npart, 0:ncol]
                t = P.op('vector', lambda e: e.tensor_copy(out=hi, in_=src))
                P.wait('vector', t)
                t = P.op('vector', lambda e: e.tensor_tensor(out=r1, in0=src, in1=hi, op=ALU.subtract))
                P.wait('vector', t)
                t = P.op('vector', lambda e: e.tensor_copy(out=mid, in_=r1))
                P.wait('vector', t)
                t = P.op('vector', lambda e: e.tensor_tensor(out=r2, in0=r1, in1=mid, op=ALU.subtract))
                P.wait('vector', t)
                t = P.op('vector', lambda e: e.tensor_copy(out=lo, in_=r2))
                return [hi, mid, lo], t
            lfr3, t_a = split3(lfr_s[:, :], 64, 128, 0)
            P.wait('vector', t_a)
            lft3, t_b = split3(lft_s[:, :], 128, 64, 3)
            P.wait('tensor', t_a, t_b)
            for n_, a_ in enumerate(lfr3):
                t = P.op('tensor', lambda e, a_=a_, n_=n_: e.matmul(bank[0][:, 0:65], a_, su[:, :], start=(n_ == 0), stop=(n_ == 2)), signal=(n_ == 2))
            P.wait('vector', t, t_b)
            t = P.op('vector', lambda e: e.tensor_copy(out=p1s[:, :], in_=bank[0][:, 0:65]))
            P.wait('vector', t)
            p13, t_c = split3(p1s[:, :], 128, 65, 6)
            P.wait('tensor', t_c)
            for n_, a_ in enumerate(p13):
                t = P.op('tensor', lambda e, a_=a_, n_=n_: e.matmul(bank[1][:, 0:65], onesb, a_, start=(n_ == 0), stop=(n_ == 2)), signal=(n_ == 2))
            for n_, a_ in enumerate(p13):
                P.op('tensor', lambda e, a_=a_, n_=n_: e.matmul(bank[2][:, 0:64], onesb, a_[:, 0:64], start=(n_ == 0), stop=False), signal=False)
            for n_, a_ in enumerate(lft3):
                t = P.op('tensor', lambda e, a_=a_, n_=n_: e.matmul(bank[2][:, 0:64], tri[:, :], a_, start=False, stop=(n_ == 2)), signal=(n_ == 2))
            P.wait('vector', t)
            P.op('vector', lambda e: e.tensor_copy(out=offs[:, :], in_=bank[1][:, 0:65]))
            t = P.op('vector', lambda e: e.tensor_copy(out=ccol[:, :], in_=bank[2][:, 0:64]))
            P.wait('vector', t)
            for qc in range(16):
                t = P.op('vector', lambda e, qc=qc: e.tensor_scalar(out=nbt[:, qc * 64:(qc + 1) * 64], in0=ccol[:, :], scalar1=-1.0,
                                                                    scalar2=offs[:, 4 * qc + 4:4 * qc + 5], op0=ALU.mult, op1=ALU.add))
            P.wait('scalar', t)
            P.wait('tensor', t)
            dmr = TileRing(dm_t)
            dm_state = {}

            def dm_fn(qc):
                if qc not in dm_state:
                    di = dmr.next()
                    P.wait('vector', dmr.free[di], t)
                    tt = None
                    for j in range(4):
                        idx = qc * 64 + 4 * qc + j
                        tt = P.op('vector', lambda e, j=j, idx=idx, di=di: e.tensor_scalar(
                            out=dm_t[di][:, 128 * j:128 * j + 128], in0=ident, scalar1=nbt[:, idx:idx + 1], scalar2=-1.0, op0=ALU.mult, op1=ALU.mult))
                    P.wait('tensor', tt)
                    dm_state[qc] = di
                    if qc >= 1:
                        pass
                return dm_state[qc]
            orig_dm_fn = dm_fn
            snap = {}

            def dm_fn2(qc):
                if qc not in dm_state and dm_state:
                    prev = dm_state[max(dm_state)]
                    dmr.free[prev] = (P.psem['tensor'], P.pcnt['tensor'])
                return orig_dm_fn(qc)

            run_AD(3, False, nb_fn=lambda qc, kb: nbt[:, qc * 64 + kb:qc * 64 + kb + 1], dm_fn=dm_fn2)
            barrier()

        if 'C' in mixers:
            load([(bufQ[:, :].rearrange("p (c t) -> p c t", c=8), qkd[:, 32 + h].rearrange("c p t -> p c t")), (bufK[:, :].rearrange("p (c t) -> p c t", c=8), qkd[:, 40 + h].rearrange("c p t -> p c t"))] + vpairs(bufV, 2))
            zb = Banks(P, bank[0:4]); tb = Banks(P, bank[4:6]); ob = Banks(P, bank[6:8])
            et = TileRing(f_t[3:5]); at = TileRing(f_t[5:7]); Rb = f_t[7]
            spr = TileRing(p2_t); pr = TileRing(p_t)
            steps = [(qc, kb) for qc in (qchunks if qchunks is not None else range(16)) for kb in range(4 * qc + 3, -1, -1)]
            sa = {}; sbb = {}; sc = {}
            cur = {}
            rb_tok = [None]

            def stA(i):
                qc, kb = steps[i]
                j, c0 = cols(qc, kb)
                b = zb.acquire()
                qs = slice(qc * 512 + c0, qc * 512 + 512)
                ks = slice(kb * 128, kb * 128 + 128)
                if j >= 0:
                    P.op('tensor', lambda e: e.matmul(bank[b][:, c0:512], bufK[:, ks], bufQ[:, qs], start=True, stop=False), signal=False)
                    t = P.op('tensor', lambda e: e.matmul(bank[b][:, c0:c0 + 128], ident, nmS, start=False, stop=True))
                else:
                    t = P.op('tensor', lambda e: e.matmul(bank[b][:, c0:512], bufK[:, ks], bufQ[:, qs], start=True, stop=True))
                sa[i] = (b, t)

            def stB(i):
                qc, kb = steps[i]
                j, c0 = cols(qc, kb)
                b, t = sa.pop(i)
                ei = et.next()
                P.wait('scalar', t, et.free[ei])
                t1 = P.op('scalar', lambda e: e.activation(out=et.tiles[ei][:, c0:512], in_=bank[b][:, c0:512], func=AF.Exp))
                si = spr.next()
                P.wait('scalar', t1, spr.free[si])
                t2 = P.op('scalar', lambda e: e.activation(out=p2_t[si][:, c0:512], in_=et.tiles[ei][:, c0:512], func=AF.Ln, bias=1.0, scale=1.0))
                et.free[ei] = t2
                P.wait('tensor', t2)
                t3 = P.op('tensor', lambda e: e.matmul(bank[b][:, c0:512], nTinc, p2_t[si][:, c0:512], start=False, stop=True))
                tbk = tb.acquire()
                t4 = P.op('tensor', lambda e: e.matmul(bank[4 + tbk][:, c0:512], onesb, p2_t[si][:, c0:512], start=True, stop=True))
                spr.free[si] = t4
                sbb[i] = (b, tbk, t3, t4)

            def stC(i):
                qc, kb = steps[i]
                j, c0 = cols(qc, kb)
                b, tbk, t3, t4 = sbb.pop(i)
                first = kb == 4 * qc + 3
                pi = pr.next()
                if first:
                    P.wait('scalar', t3, pr.free[pi])
                    t6 = P.op('scalar', lambda e: e.activation(out=p_t[pi][:, c0:512], in_=bank[b][:, c0:512], func=AF.Exp))
                    zb.release(b, t6)
                    P.wait('vector', t4, rb_tok[0])
                    tm = P.op('vector', lambda e: e.memset(Rb[:, :], 0.0))
                    P.wait('vector', tm)
                    t7 = P.op('vector', lambda e: e.tensor_copy(out=Rb[:, c0:512], in_=bank[4 + tbk][:, c0:512]))
                    rb_tok[0] = t7
                    tb.release(tbk, t7)
                else:
                    ai = at.next()
                    P.wait('vector', t3, at.free[ai], rb_tok[0])
                    t5 = P.op('vector', lambda e: e.tensor_tensor(out=at.tiles[ai][:, c0:512], in0=bank[b][:, c0:512], in1=Rb[:, c0:512], op=ALU.subtract))
                    zb.release(b, t5)
                    P.wait('vector', t4, t5)
                    t7 = P.op('vector', lambda e: e.tensor_tensor(out=Rb[:, c0:512], in0=Rb[:, c0:512], in1=bank[4 + tbk][:, c0:512], op=ALU.add))
                    rb_tok[0] = t7
                    tb.release(tbk, t7)
                    P.wait('scalar', t5, pr.free[pi])
                    t6 = P.op('scalar', lambda e: e.activation(out=p_t[pi][:, c0:512], in_=at.tiles[ai][:, c0:512], func=AF.Exp))
                    at.free[ai] = t6
                sc[i] = (pi, t6)

            def stD(i):
                qc, kb = steps[i]
                j, c0 = cols(qc, kb)
                pi, t6 = sc.pop(i)
                first = kb == 4 * qc + 3
                last = kb == 0
                if first:
                    cur['o'] = ob.acquire()
                    o_ = cur['o']
                    P.op('tensor', lambda e: e.matmul(bank[6 + o_][:, :], zerosb, bufK[:, 0:512], start=True, stop=False), signal=False)
                o_ = cur['o']
                P.wait('tensor', t6)
                t = P.op('tensor', lambda e: e.matmul(bank[6 + o_][:, c0:512], bufV[:, kb * 128:kb * 128 + 128], p_t[pi][:, c0:512], start=False, stop=last))
                pr.free[pi] = t
                if last:
                    slot = ost.acquire('vector')
                    P.wait('vector', t)
                    t2 = P.op('vector', lambda e: e.tensor_copy(out=ost.tiles[slot], in_=bank[6 + o_][:, :]))
                    ob.release(o_, t2)
                    ost.store(slot, od[2, h, :, qc * 512:qc * 512 + 512], t2)

            ns = len(steps)
            for i in range(ns + 3):
                if i < ns:
                    stA(i)
                if 0 <= i - 1 < ns:
                    stB(i - 1)
                if 0 <= i - 2 < ns:
                    stC(i - 2)
                if 0 <= i - 3 < ns:
                    stD(i - 3)
            barrier()

        if 'B' in mixers:
            load([(bufQ[:, :].rearrange("p (c t) -> p c t", c=8), qkd[:, 16 + h].rearrange("c p t -> p c t")), (bufK[:, :].rearrange("p (c t) -> p c t", c=8), qkd[:, 24 + h].rearrange("c p t -> p c t"))] + vpairs(bufV, 1) + vdil(bufX1, 4) + vdil(bufX2, 16))
            ss = Banks(P, bank[0:2]); spv = Banks(P, bank[2:4]); ob = Banks(P, bank[4:6]); lb = Banks(P, bank[6:8])
            prs = TileRing(p_t[0:2]); prp = TileRing(p_t[2:4])
            groups = []
            for dl, vb, nbk in [(1, bufV, 64), (4, bufX1, 16), (16, bufX2, 4)]:
                for r in range(dl):
                    for n0 in range(0, nbk, 4):
                        groups.append((dl, vb, nbk, r, n0))
            st_ = {}
            acc_tok = [None]

            def sub(buf, dl, r, n):
                s0 = r + dl * 128 * n
                return buf[:, s0:s0 + dl * 127 + 1:dl]

            def g1(i):
                dl, vb, nbk, r, n0 = groups[i]
                bs = ss.acquire(); bp = spv.acquire()
                t = None
                for j in range(4):
                    n = n0 + j
                    P.op('tensor', lambda e, j=j, n=n: e.matmul(bank[bs][:, 128 * j:128 * j + 128], sub(bufK, dl, r, n), sub(bufQ, dl, r, n), start=True, stop=False), signal=False)
                    t = P.op('tensor', lambda e, j=j: e.matmul(bank[bs][:, 128 * j:128 * j + 128], ident, nmU, start=False, stop=True), signal=(j == 3))
                tp_ = None
                for j in range(4):
                    n = n0 + j
                    if n == 0:
                        continue
                    P.op('tensor', lambda e, j=j, n=n: e.matmul(bank[2 + bp][:, 128 * j:128 * j + 128], sub(bufK, dl, r, n - 1), sub(bufQ, dl, r, n), start=True, stop=False), signal=False)
                    tp_ = P.op('tensor', lambda e, j=j: e.matmul(bank[2 + bp][:, 128 * j:128 * j + 128], ident, nmL, start=False, stop=True), signal=(j == 3))
                pc0 = 128 if n0 == 0 else 0
                ps_ = prs.next(); pp_ = prp.next()
                P.wait('scalar', t, prs.free[ps_])
                te1 = P.op('scalar', lambda e: e.activation(out=p_t[ps_][:, :], in_=bank[bs][:, :], func=AF.Exp))
                ss.release(bs, te1)
                P.wait('scalar', tp_, prp.free[pp_])
                te2 = P.op('scalar', lambda e: e.activation(out=p_t[2 + pp_][:, pc0:512], in_=bank[2 + bp][:, pc0:512], func=AF.Exp))
                spv.release(bp, te2)
                st_[i] = (ps_, pp_, te1, te2)

            def g2(i):
                dl, vb, nbk, r, n0 = groups[i]
                ps_, pp_, te1, te2 = st_.pop(i)
                o_ = ob.acquire(); l_ = lb.acquire()
                P.wait('tensor', te1, te2)
                t = None
                for j in range(4):
                    n = n0 + j
                    cs = slice(128 * j, 128 * j + 128)
                    vs = lambda nn: vb[:, (r * nbk + nn) * 128:(r * nbk + nn) * 128 + 128]
                    if n > 0:
                        P.op('tensor', lambda e, cs=cs, n=n, vs=vs: e.matmul(bank[4 + o_][:, cs], vs(n - 1), p_t[2 + pp_][:, cs], start=True, stop=False), signal=False)
                        P.op('tensor', lambda e, cs=cs, n=n, vs=vs: e.matmul(bank[4 + o_][:, cs], vs(n), p_t[ps_][:, cs], start=False, stop=True), signal=False)
                        P.op('tensor', lambda e, cs=cs: e.matmul(bank[6 + l_][:, cs], onesb, p_t[2 + pp_][:, cs], start=True, stop=False), signal=False)
                        t = P.op('tensor', lambda e, cs=cs: e.matmul(bank[6 + l_][:, cs], onesb, p_t[ps_][:, cs], start=False, stop=True))
                    else:
                        P.op('tensor', lambda e, cs=cs, n=n, vs=vs: e.matmul(bank[4 + o_][:, cs], vs(n), p_t[ps_][:, cs], start=True, stop=True), signal=False)
                        t = P.op('tensor', lambda e, cs=cs: e.matmul(bank[6 + l_][:, cs], onesb, p_t[ps_][:, cs], start=True, stop=True))
                prs.free[ps_] = t
                prp.free[pp_] = t
                s0 = r + dl * 128 * n0
                asl = slice(s0, s0 + dl * 511 + 1, dl)
                P.wait('vector', t, acc_tok[0])
                if dl == 1:
                    t1 = P.op('vector', lambda e: e.tensor_copy(out=Oacc[:, asl], in_=bank[4 + o_][:, :]))
                    t2 = P.op('vector', lambda e: e.tensor_copy(out=Lacc[:, asl], in_=bank[6 + l_][:, :]))
                else:
                    t1 = P.op('vector', lambda e: e.tensor_tensor(out=Oacc[:, asl], in0=Oacc[:, asl], in1=bank[4 + o_][:, :], op=ALU.add))
                    t2 = P.op('vector', lambda e: e.tensor_tensor(out=Lacc[:, asl], in0=Lacc[:, asl], in1=bank[6 + l_][:, :], op=ALU.add))
                ob.release(o_, t1); lb.release(l_, t2)
                acc_tok[0] = t2

            LA = 1
            for i in range(len(groups) + LA):
                if i < len(groups):
                    g1(i)
                if i >= LA:
                    g2(i - LA)
            P.wait('vector', acc_tok[0])
            for c in range(16):
                csl = slice(c * 512, c * 512 + 512)
                t1 = P.op('vector', lambda e, csl=csl: e.reciprocal(out=Lacc[:, csl], in_=Lacc[:, csl]))
                slot = ost.acquire('vector')
                P.wait('vector', t1)
                t2 = P.op('vector', lambda e, csl=csl, slot=slot: e.tensor_tensor(out=ost.tiles[slot], in0=Oacc[:, csl], in1=Lacc[:, csl], op=ALU.mult))
                ost.store(slot, od[1, h, :, csl], t2)
            barrier()
        P.wait('sync', ost.final_toks())
    P.end_phase(cond_core)


THETA = 500000.0
def rope_tables(pos):
    pos = pos.astype(np.float32)
    invA = (THETA ** (-np.arange(32, dtype=np.float32) * (2.0 / 64))).astype(np.float32)
    angA = pos[None, :] * invA[:, None]
    cosA = np.concatenate([np.cos(angA), np.cos(angA)], 0).astype(np.float32)
    sinA = np.concatenate([np.sin(angA), np.sin(angA)], 0).astype(np.float32)
    invB = (THETA ** (-np.arange(16, dtype=np.float32) * (2.0 / 32))).astype(np.float32)
    angB = pos[None, :] * invB[:, None]
    T = pos.shape[0]
    cosB = np.ones((128, T), np.float32); sinB = np.zeros((128, T), np.float32)
    cosB[0:16] = np.cos(angB); cosB[16:32] = np.cos(angB)
    sinB[0:16] = np.sin(angB); sinB[16:32] = np.sin(angB)
    return cosA, sinA, cosB, sinB
def rot_mats():
    RA = np.zeros((64, 64), np.float32)
    for i in range(32):
        RA[i + 32, i] = -1.0
        RA[i, i + 32] = 1.0
    RB = np.zeros((128, 128), np.float32)
    for i in range(16):
        RB[i + 16, i] = -1.0
        RB[i, i + 16] = 1.0
    return RA, RB
def lay(g, nk):
    return np.ascontiguousarray(g.reshape(nk, 128).T)


NSH = 8
DEPTH = 2


def build_fused():
    nc = bass.Bass("TRN2", target_bir_lowering=False)
    din = lambda n, s, dt=F32: nc.dram_tensor(n, s, dt, kind="ExternalInput").ap()
    xin = din("xin", [NSH, D, T])
    w_in = din("w_in", [DEPTH, D, INC]); gatt = din("gatt", [DEPTH, 128, 32]); gq = din("gq", [DEPTH, 128, 7]); gkv = din("gkv", [DEPTH, 128, 4])
    w_uq = din("w_uq", [DEPTH, 896, 1536]); w_uk = din("w_uk", [DEPTH, 512, 1024]); w_uv = din("w_uv", [DEPTH, 512, 1024])
    bfd = din("bf", [DEPTH, 8, 1])
    gout = din("gout", [DEPTH, 128, 32]); w_o = din("w_o", [DEPTH, D, D]); gmlp = din("gmlp", [DEPTH, 128, 32])
    w_up = din("w_up", [DEPTH, D, FF]); w_down = din("w_down", [DEPTH, FF, D]); gfin = din("gfin", [128, 32])
    cosA = din("cosA", [NSH, 64, T]); sinA = din("sinA", [NSH, 64, T]); cosB = din("cosB", [NSH, 128, T]); sinB = din("sinB", [NSH, 128, T])
    RA = din("RA", [64, 64]); RB = din("RB", [128, 128]); ones = din("ones", [128, 128])
    cbd = din("cb", [128, 7 * 128], BF16); sud = din("su", [64, 65], BF16); trid = din("tri", [128, 128], BF16)
    yT = nc.dram_tensor("yT", [D, T], F32, kind="ExternalOutput").ap()
    X1 = nc.dram_tensor("X1", [NSH, D, T], F32).ap()
    qkd = nc.dram_tensor("qkd", [NSH, 64, 128, T], BF16).ap()
    r64d = nc.dram_tensor("r64d", [NSH, 9, 64, T], BF16).ap()
    vd = nc.dram_tensor("vd", [NSH, T, 4096], BF16).ap()
    lfd = nc.dram_tensor("lfd", [NSH, 8, T], F32).ap()
    od = nc.dram_tensor("od", [4, 8, 128, S], F32).ap()
    latd = nc.dram_tensor("latd", [12, 128, T], F32).ap()
    x1d = nc.dram_tensor("x1d", [D, T], F32).ap()
    own_o = nc.dram_tensor("own_o", [4, 8, 128, T], F32).ap()
    own_x = nc.dram_tensor("own_x", [D, T], F32).ap()
    accd = nc.dram_tensor("accd", [D, T], F32).ap()
    with ExitStack() as es:
        P = Prog(nc, es)
        for l in range(DEPTH):
            Xi = xin if l == 0 else X1
            Xo = X1
            for c in range(NSH):
                a_args = (Xi[c], w_in[l], gatt[l], gq[l], gkv[l], w_uq[l], w_uk[l], w_uv[l], bfd[l],
                          cosA[c], sinA[c], cosB[c], sinB[c], RA, RB, ones, qkd[c], r64d[c], vd[c], lfd[c], latd)
                if l < DEPTH - 1:
                    emit_A(P, *a_args)
                else:
                    emit_A(P, *a_args, which='kv')
                    emit_A(P, *a_args, which='q', cond_core=c)
            for h in range(8):
                if l < DEPTH - 1:
                    emit_B(P, h, qkd, r64d, vd, lfd, cbd, sud, trid, od)
                else:
                    emit_B(P, h, qkd, r64d, vd, lfd, cbd, sud, trid, od, mixers="B")
                    for c in range(NSH):
                        emit_B(P, h, qkd, r64d, vd, lfd, cbd, sud, trid, od, mixers="ADC", qchunks=[2 * c, 2 * c + 1], cond_core=c)
            if l < DEPTH - 1:
                for c in range(NSH):
                    emit_C(P, False, (lambda k, c=c: od[k // 8, k % 8, :, c * T:(c + 1) * T]), Xi[c],
                           gout[l], gmlp[l], gfin, w_o[l], w_up[l], w_down[l], ones, Xo[c], x1d, accd)
            else:
                P.begin_phase()
                gsem = P.dsem("gsem")
                gsem.count += 16 * 8

                def gather(e):
                    pid = P.pid_of(e, 'sync')
                    for m_ in range(4):
                        e.dma_start(out=own_o[m_].rearrange("h p t -> (h p) t"),
                                    in_=od[m_, :, :, bass.ts(pid, T)].rearrange("h p t -> (h p) t")).then_inc(gsem.h, 16)
                    xsrc = Xi[bass.ts(pid, 1)].rearrange("o d t -> d (o t)")
                    for r_ in range(4):
                        e.dma_start(out=own_x[r_ * 1024:(r_ + 1) * 1024, :], in_=xsrc[r_ * 1024:(r_ + 1) * 1024, :]).then_inc(gsem.h, 16)
                P.ops['sync'].append(gather)
                P.wait('sync', (gsem.h, gsem.count))
                P.end_phase()
                emit_C(P, True, (lambda k: own_o[k // 8, k % 8, :, :]), own_x, gout[l], gmlp[l], gfin, w_o[l], w_up[l], w_down[l], ones, yT, x1d, accd)
        n_instr = P.n_instr
    return nc, n_instr


import ml_dtypes
_BF = ml_dtypes.bfloat16
_FUSED = {}


def _b_consts():
    p = np.arange(128)[:, None]
    f = np.arange(128)[None, :]
    ident = (p == f).astype(np.float32)
    ones = np.ones((128, 128), np.float32)
    nmU = np.where(p > f, NEG, 0.0)
    nmL = np.where(p < f, NEG, 0.0)
    nmS = np.where(p >= f, NEG, 0.0)
    nTinc = np.where(p >= f, -1.0, 0.0)
    zeros = np.zeros((128, 128))
    cb = np.concatenate([ident, ones, nmU, nmL, nmS, nTinc, zeros], 1).astype(np.float32)
    su = (np.arange(64)[:, None] < np.arange(65)[None, :]).astype(np.float32)
    tri = (p <= f).astype(np.float32)
    return cb.astype(_BF), su.astype(_BF), tri.astype(_BF)


def kernel(x, g_attn, w_in, g_q, g_kv, w_uq, w_uk, w_uv, b_f, g_out, w_o, g_mlp, w_up, w_down, g_final):
    f32 = lambda a: np.ascontiguousarray(np.asarray(a, dtype=np.float32))
    if 'nc' not in _FUSED:
        _FUSED['nc'], _ = build_fused()
    nc = _FUSED['nc']
    x = f32(x)
    RA, RB = rot_mats()
    cb, su, tri = _b_consts()
    tabs = [rope_tables(c * T + np.arange(T)) for c in range(NSH)]
    layn = lambda g, nk: np.stack([lay(f32(g[l]), nk) for l in range(DEPTH)])
    ins = dict(
        xin=np.ascontiguousarray(x[0].reshape(NSH, T, D).transpose(0, 2, 1)),
        w_in=f32(w_in), gatt=layn(g_attn, 32), gq=layn(g_q, 7), gkv=layn(g_kv, 4),
        w_uq=f32(w_uq), w_uk=f32(w_uk), w_uv=f32(w_uv), bf=f32(b_f).reshape(DEPTH, 8, 1),
        gout=layn(g_out, 32), w_o=f32(w_o), gmlp=layn(g_mlp, 32), w_up=f32(w_up), w_down=f32(w_down), gfin=lay(f32(g_final), 32),
        cosA=np.stack([t[0] for t in tabs]), sinA=np.stack([t[1] for t in tabs]),
        cosB=np.stack([t[2] for t in tabs]), sinB=np.stack([t[3] for t in tabs]),
        RA=RA, RB=RB, ones=np.ones((128, 128), np.float32), cb=cb, su=su, tri=tri)
    res = run_bass_kernel_spmd(nc, [ins for _ in range(NSH)], core_ids=list(range(NSH)))
    out = np.concatenate([res.results[c]["yT"].T for c in range(NSH)], axis=0)
    return np.ascontiguousarray(out).reshape(1, S, D).astype(np.float32)
```
